# Optimizing a Trainium2 kernel written in Bass

```python
import jax
import jax.numpy as jnp
from jax import lax
import numpy as np

D_MODEL = 2048
BATCH = 8
SEQ = 4096
DEPTH = 4

GRID_W = 64
CTX_LEN = 256
N_MIXERS = 4
EPS = 1e-6
ROPE_THETA = 10000.0
F32 = jnp.float32

ML_HEADS = 8
ML_DQK = 128
ML_DV = 256
ML_QK = ML_HEADS * ML_DQK
ML_V = ML_HEADS * ML_DV
ML_CHUNK = 64

NA_HEADS = 16
NA_DH = D_MODEL // NA_HEADS
WIN_ROWS = 8
WIN_COLS = 16

CV_WIDTH = 31

LRU_WIDTH = D_MODEL
LRU_BLOCKS = 8
LRU_BW = LRU_WIDTH // LRU_BLOCKS
LRU_CONV = 4
LRU_C = 8.0

D_FF = 5632
FFN_CONV = 3

kernel_name = 'hybrid_mlstm_natten_conformer_rglru_dit'


def rms_norm(x, g):
    xf = x.astype(F32)
    y = xf * lax.rsqrt(jnp.mean(xf * xf, axis=-1, keepdims=True) + EPS)
    return (y * g.astype(F32)).astype(x.dtype)


def layer_norm(x, g, b):
    xf = x.astype(F32)
    mu = jnp.mean(xf, axis=-1, keepdims=True)
    var = jnp.mean(jnp.square(xf - mu), axis=-1, keepdims=True)
    return ((xf - mu) * lax.rsqrt(var + EPS) * g.astype(F32) + b.astype(F32)).astype(x.dtype)


def modulate(x, shift, scale):
    return x * (1 + scale) + shift


def dwconv(x, w, pad_left, pad_right):
    return lax.conv_general_dilated(
        x, w[:, None, :].astype(x.dtype), window_strides=(1,),
        padding=[(pad_left, pad_right)], dimension_numbers=('NWC', 'WIO', 'NWC'),
        feature_group_count=x.shape[-1])


def rope_1d(x, pos):
    d = x.shape[-1]
    freqs = ROPE_THETA ** (-jnp.arange(0, d, 2, dtype=F32) / d)
    ang = pos.astype(F32)[:, None] * freqs[None, :]
    cos, sin = jnp.cos(ang).astype(x.dtype), jnp.sin(ang).astype(x.dtype)
    x1, x2 = jnp.split(x, 2, axis=-1)
    return jnp.concatenate([x1 * cos - x2 * sin, x1 * sin + x2 * cos], axis=-1)


def axial_rope(x):
    S, d = x.shape[-2], x.shape[-1]
    t = jnp.arange(S)
    return jnp.concatenate([rope_1d(x[..., : d // 2], t // GRID_W),
                            rope_1d(x[..., d // 2:], t % GRID_W)], axis=-1)


def mlstm_chunkwise(q, k, v, logi, logf, state):
    B, H, S, _ = q.shape
    L = ML_CHUNK
    nc = S // L

    def chunks(a):
        return jnp.moveaxis(a.reshape(a.shape[:2] + (nc, L) + a.shape[3:]), 2, 0)

    tril = jnp.tril(jnp.ones((L, L), dtype=bool))

    def step(carry, inp):
        C, n, m = carry
        qc, kc, vc, li, lf = inp
        qc, kc, vc = qc.astype(F32), kc.astype(F32), vc.astype(F32)
        b = jnp.cumsum(lf, axis=-1)
        dmat = jnp.where(tril, b[..., :, None] - b[..., None, :] + li[..., None, :], -jnp.inf)
        inter = b + m[..., None]
        m_t = jnp.maximum(inter, jnp.max(dmat, axis=-1))
        w_inter = jnp.exp(inter - m_t)
        s_qk = jnp.einsum('bhtd,bhsd->bhts', qc, kc) * jnp.exp(dmat - m_t[..., None])
        num = (w_inter[..., None] * jnp.einsum('bhtd,bhde->bhte', qc, C)
               + jnp.einsum('bhts,bhse->bhte', s_qk, vc))
        den = w_inter * jnp.einsum('bhtd,bhd->bht', qc, n) + jnp.sum(s_qk, axis=-1)
        h = num / jnp.maximum(jnp.abs(den), jnp.exp(-m_t))[..., None]
        b_last = b[..., -1]
        w_s = b_last[..., None] - b + li
        m_new = jnp.maximum(b_last + m, jnp.max(w_s, axis=-1))
        decay = jnp.exp(b_last + m - m_new)
        e_s = jnp.exp(w_s - m_new[..., None])
        C_new = decay[..., None, None] * C + jnp.einsum('bhs,bhsd,bhse->bhde', e_s, kc, vc)
        n_new = decay[..., None] * n + jnp.einsum('bhs,bhsd->bhd', e_s, kc)
        return (C_new, n_new, m_new), h

    state, hs = lax.scan(step, state, (chunks(q), chunks(k), chunks(v), chunks(logi), chunks(logf)))
    return jnp.moveaxis(hs, 0, 2).reshape(B, H, S, -1), state


def mlstm_mixer(xl, xc, w_in, w_gate, b_gate, head_g, w_out, need_ctx):
    def project(x, rotary):
        B, S, _ = x.shape
        q, k, v, o = jnp.split(x @ w_in, [ML_QK, 2 * ML_QK, 2 * ML_QK + ML_V], axis=-1)
        q = q.reshape(B, S, ML_HEADS, ML_DQK).transpose(0, 2, 1, 3) * (ML_DQK ** -0.5)
        k = k.reshape(B, S, ML_HEADS, ML_DQK).transpose(0, 2, 1, 3)
        v = v.reshape(B, S, ML_HEADS, ML_DV).transpose(0, 2, 1, 3)
        if rotary:
            q, k = axial_rope(q), axial_rope(k)
        g = (x @ w_gate).astype(F32) + b_gate.astype(F32)
        g = g.reshape(B, S, 4, ML_HEADS).transpose(2, 0, 3, 1)
        fwd = (g[0], jax.nn.log_sigmoid(g[1]))
        bwd = (g[2], jax.nn.log_sigmoid(g[3]))
        return q, k, v, o, fwd, bwd

    def flip(a):
        return jnp.flip(a, axis=2)

    def readout(h, o):
        B, H, S, dv = h.shape
        h = rms_norm(h.transpose(0, 2, 1, 3), head_g.reshape(H, dv)).reshape(B, S, H * dv)
        return ((h.astype(o.dtype) * jax.nn.sigmoid(o)) @ w_out).astype(o.dtype)

    B = xl.shape[0]
    zero = (jnp.zeros((B, ML_HEADS, ML_DQK, ML_DV), F32), jnp.zeros((B, ML_HEADS, ML_DQK), F32),
            jnp.zeros((B, ML_HEADS), F32))
    qc, kc, vc, oc, gfc, gbc = project(xc, False)
    hc_f, st_f = mlstm_chunkwise(qc, kc, vc, gfc[0], gfc[1], zero)
    hc_b, st_b = mlstm_chunkwise(flip(qc), flip(kc), flip(vc), flip(gbc[0]), flip(gbc[1]), zero)
    ql, kl, vl, ol, gfl, gbl = project(xl, True)
    hl_f, _ = mlstm_chunkwise(ql, kl, vl, gfl[0], gfl[1], st_f)
    hl_b, _ = mlstm_chunkwise(flip(ql), flip(kl), flip(vl), flip(gbl[0]), flip(gbl[1]), st_b)
    yl = readout(hl_f + flip(hl_b), ol)
    yc = readout(hc_f + flip(hc_b), oc) if need_ctx else None
    return yl, yc


def na_mixer(xl, xc, w_qkv, q_g, k_g, rpb, w_o, need_ctx):
    def qkv(x):
        B, S, _ = x.shape
        z = (x @ w_qkv).reshape(B, S, 3, NA_HEADS, NA_DH)
        q = rms_norm(z[:, :, 0], q_g) * (NA_DH ** -0.5)
        k = rms_norm(z[:, :, 1], k_g)
        v = z[:, :, 2]
        return q.transpose(0, 2, 1, 3), k.transpose(0, 2, 1, 3), v.transpose(0, 2, 1, 3)

    ql, kl, vl = qkv(xl)
    qc, kc, vc = qkv(xc)
    B, H, S, dh = ql.shape
    rows = S // GRID_W
    win_r = min(WIN_ROWS, rows)
    n_loc = win_r * GRID_W
    kg = kl.reshape(B, H, rows, GRID_W, dh)
    vg = vl.reshape(B, H, rows, GRID_W, dh)
    qrows = jnp.moveaxis(ql.reshape(B, H, rows, GRID_W, dh), 2, 0)
    col = jnp.arange(GRID_W)
    cstart = jnp.clip(col - WIN_COLS // 2, 0, GRID_W - WIN_COLS)
    colmask = (col[None, :] >= cstart[:, None]) & (col[None, :] < cstart[:, None] + WIN_COLS)
    dc_idx = jnp.clip(col[None, :] - col[:, None] + WIN_COLS - 1, 0, 2 * WIN_COLS - 2)

    def row_block(args):
        r, q_r = args
        rs = jnp.clip(r - WIN_ROWS // 2, 0, rows - win_r)
        k_s = lax.dynamic_slice_in_dim(kg, rs, win_r, axis=2)
        v_s = lax.dynamic_slice_in_dim(vg, rs, win_r, axis=2)
        dr_idx = rs + jnp.arange(win_r) - r + WIN_ROWS - 1
        bias = rpb[:, dr_idx[None, :, None], dc_idx[:, None, :]].astype(F32)
        s_loc = jnp.einsum('bhqd,bhrkd->bhqrk', q_r, k_s).astype(F32) + bias
        s_loc = jnp.where(colmask[:, None, :], s_loc, -jnp.inf)
        s_ctx = jnp.einsum('bhqd,bhcd->bhqc', q_r, kc).astype(F32)
        p = jax.nn.softmax(jnp.concatenate([s_loc.reshape(B, H, GRID_W, n_loc), s_ctx], axis=-1),
                           axis=-1).astype(v_s.dtype)
        p_loc = p[..., :n_loc].reshape(B, H, GRID_W, win_r, GRID_W)
        return (jnp.einsum('bhqrk,bhrkd->bhqd', p_loc, v_s)
                + jnp.einsum('bhqc,bhcd->bhqd', p[..., n_loc:], vc))

    o = lax.map(row_block, (jnp.arange(rows), qrows))
    yl = o.transpose(1, 0, 3, 2, 4).reshape(B, S, H * dh) @ w_o
    yc = None
    if need_ctx:
        pc = jax.nn.softmax(jnp.einsum('bhqd,bhkd->bhqk', qc, kc).astype(F32), axis=-1).astype(vc.dtype)
        oc = jnp.einsum('bhqk,bhkd->bhqd', pc, vc)
        yc = oc.transpose(0, 2, 1, 3).reshape(B, qc.shape[2], H * dh) @ w_o
    return yl, yc


def conformer_conv(x, w_pw1, dw, dw_b, ln_g, ln_b, w_pw2):
    a, g = jnp.split(x @ w_pw1, 2, axis=-1)
    h = a * jax.nn.sigmoid(g)
    h = dwconv(h, dw, CV_WIDTH // 2, CV_WIDTH // 2) + dw_b
    h = layer_norm(h, ln_g, ln_b)
    return jax.nn.silu(h) @ w_pw2


def linear_recurrence(a, b, h0):
    b = b.at[:, 0].add(a[:, 0] * h0)

    def combine(left, right):
        a_l, b_l = left
        a_r, b_r = right
        return a_l * a_r, a_r * b_l + b_r

    _, h = lax.associative_scan(combine, (a, b), axis=1)
    return h


def lru_direction(u, w_r, b_r, w_i, b_i, lam, h0):
    B, S, R = u.shape
    ub = u.reshape(B, S, LRU_BLOCKS, LRU_BW)
    r = jax.nn.sigmoid((jnp.einsum('bsnk,nkj->bsnj', ub, w_r).reshape(B, S, R) + b_r).astype(F32))
    i = jax.nn.sigmoid((jnp.einsum('bsnk,nkj->bsnj', ub, w_i).reshape(B, S, R) + b_i).astype(F32))
    log_a = -LRU_C * jax.nn.softplus(-lam.astype(F32)) * r
    a = jnp.exp(log_a)
    b = jnp.sqrt(-jnp.expm1(2.0 * log_a)) * (i * u.astype(F32))
    return linear_recurrence(a, b, h0)


def rglru_mixer(xl, xc, w_in, conv_w, conv_b, w_gate, b_gate, lam, w_out, need_ctx):
    def branches(x):
        gate, u = jnp.split(x @ w_in, 2, axis=-1)
        u = dwconv(u, conv_w, LRU_CONV // 2, LRU_CONV - 1 - LRU_CONV // 2) + conv_b
        return gate, u

    def flip(a):
        return jnp.flip(a, axis=1)

    def out(h, gate):
        return (h.astype(gate.dtype) * jax.nn.gelu(gate)) @ w_out

    pf = (w_gate[0], b_gate[0], w_gate[1], b_gate[1], lam[0])
    pb = (w_gate[2], b_gate[2], w_gate[3], b_gate[3], lam[1])
    gc, uc = branches(xc)
    gl, ul = branches(xl)
    h0 = jnp.zeros((xl.shape[0], LRU_WIDTH), F32)
    hc_f = lru_direction(uc, *pf, h0)
    hc_b = flip(lru_direction(flip(uc), *pb, h0))
    hl_f = lru_direction(ul, *pf, hc_f[:, -1])
    hl_b = flip(lru_direction(flip(ul), *pb, hc_b[:, 0]))
    yl = out(hl_f + hl_b, gl)
    yc = out(hc_f + hc_b, gc) if need_ctx else None
    return yl, yc


def conv_ffn(x, w_gu, conv_w, w_down):
    g, u = jnp.split(x @ w_gu, 2, axis=-1)
    g = dwconv(g, conv_w, FFN_CONV // 2, FFN_CONV // 2)
    return (jax.nn.silu(g) * u) @ w_down


def setup_inputs(seed: int = 0) -> dict:
    key = jax.random.key(seed)
    ks = iter(jax.random.split(key, 48))

    def nrm(shape, scale):
        return jax.random.normal(next(ks), shape, F32) * scale

    D = D_MODEL
    nA, nB, nC, nD = [len(range(kind, DEPTH, N_MIXERS)) for kind in range(N_MIXERS)]
    f_bias = jnp.linspace(3.0, 6.0, ML_HEADS, dtype=F32)
    a0 = jax.random.uniform(next(ks), (nD, 2, LRU_WIDTH), F32, 0.9, 0.999)
    p0 = a0 ** (1.0 / LRU_C)
    return {
        'x': nrm((BATCH, SEQ, D), 1.0),
        'c': nrm((BATCH, D), 1.0),
        'ctx': nrm((BATCH, CTX_LEN, D), 1.0),
        'c_ctx': nrm((D,), 1.0),
        'norm_mix': 1.0 + nrm((DEPTH, D), 0.1),
        'norm_ffn': 1.0 + nrm((DEPTH, D), 0.1),
        'ada_w': nrm((DEPTH, D, 6 * D), 0.5 * D ** -0.5),
        'ada_b': nrm((DEPTH, 6 * D), 0.02),
        'ml_w_in': nrm((nA, D, 2 * ML_QK + 2 * ML_V), D ** -0.5),
        'ml_w_gate': nrm((nA, D, 4 * ML_HEADS), 0.3 * D ** -0.5),
        'ml_b_gate': jnp.concatenate([nrm((nA, ML_HEADS), 0.1), f_bias + nrm((nA, ML_HEADS), 0.1),
                                      nrm((nA, ML_HEADS), 0.1), f_bias + nrm((nA, ML_HEADS), 0.1)], axis=-1),
        'ml_head_g': 1.0 + nrm((nA, ML_V), 0.1),
        'ml_w_out': nrm((nA, ML_V, D), ML_V ** -0.5),
        'na_w_qkv': nrm((nB, D, 3 * D), D ** -0.5),
        'na_q_g': 1.0 + nrm((nB, NA_DH), 0.1),
        'na_k_g': 1.0 + nrm((nB, NA_DH), 0.1),
        'na_rpb': nrm((nB, NA_HEADS, 2 * WIN_ROWS - 1, 2 * WIN_COLS - 1), 0.2),
        'na_w_o': nrm((nB, D, D), D ** -0.5),
        'cv_w_pw1': nrm((nC, D, 2 * D), D ** -0.5),
        'cv_dw': nrm((nC, CV_WIDTH, D), CV_WIDTH ** -0.5),
        'cv_dw_b': nrm((nC, D), 0.02),
        'cv_ln_g': 1.0 + nrm((nC, D), 0.1),
        'cv_ln_b': nrm((nC, D), 0.02),
        'cv_w_pw2': nrm((nC, D, D), D ** -0.5),
        'lr_w_in': nrm((nD, D, 2 * LRU_WIDTH), D ** -0.5),
        'lr_conv': nrm((nD, LRU_CONV, LRU_WIDTH), LRU_CONV ** -0.5),
        'lr_conv_b': nrm((nD, LRU_WIDTH), 0.02),
        'lr_w_gate': nrm((nD, 4, LRU_BLOCKS, LRU_BW, LRU_BW), LRU_BW ** -0.5),
        'lr_b_gate': nrm((nD, 4, LRU_WIDTH), 0.02),
        'lr_lambda': jnp.log(p0) - jnp.log1p(-p0),
        'lr_w_out': nrm((nD, LRU_WIDTH, D), LRU_WIDTH ** -0.5),
        'ffn_w_gu': nrm((DEPTH, D, 2 * D_FF), D ** -0.5),
        'ffn_conv': nrm((DEPTH, FFN_CONV, D_FF), FFN_CONV ** -0.5),
        'ffn_w_down': nrm((DEPTH, D_FF, D), D_FF ** -0.5),
    }


def reference(x, c, ctx, c_ctx, norm_mix, norm_ffn, ada_w, ada_b,
              ml_w_in, ml_w_gate, ml_b_gate, ml_head_g, ml_w_out,
              na_w_qkv, na_q_g, na_k_g, na_rpb, na_w_o,
              cv_w_pw1, cv_dw, cv_dw_b, cv_ln_g, cv_ln_b, cv_w_pw2,
              lr_w_in, lr_conv, lr_conv_b, lr_w_gate, lr_b_gate, lr_lambda, lr_w_out,
              ffn_w_gu, ffn_conv, ffn_w_down):
    xl, xc = x, ctx
    for i in range(DEPTH):
        kind, j = i % N_MIXERS, i // N_MIXERS
        need_ctx = i < DEPTH - 1
        mod_l = jnp.split((jax.nn.silu(c) @ ada_w[i] + ada_b[i])[:, None, :], 6, axis=-1)
        mod_c = jnp.split((jax.nn.silu(c_ctx) @ ada_w[i] + ada_b[i])[None, None, :], 6, axis=-1)
        hl = modulate(rms_norm(xl, norm_mix[i]), mod_l[0], mod_l[1])
        hc = modulate(rms_norm(xc, norm_mix[i]), mod_c[0], mod_c[1])
        if kind == 0:
            yl, yc = mlstm_mixer(hl, hc, ml_w_in[j], ml_w_gate[j], ml_b_gate[j], ml_head_g[j],
                                 ml_w_out[j], need_ctx)
        elif kind == 1:
            yl, yc = na_mixer(hl, hc, na_w_qkv[j], na_q_g[j], na_k_g[j], na_rpb[j], na_w_o[j], need_ctx)
        elif kind == 2:
            cv = (cv_w_pw1[j], cv_dw[j], cv_dw_b[j], cv_ln_g[j], cv_ln_b[j], cv_w_pw2[j])
            yl = conformer_conv(hl, *cv)
            yc = conformer_conv(hc, *cv) if need_ctx else None
        else:
            yl, yc = rglru_mixer(hl, hc, lr_w_in[j], lr_conv[j], lr_conv_b[j], lr_w_gate[j],
                                 lr_b_gate[j], lr_lambda[j], lr_w_out[j], need_ctx)
        xl = xl + mod_l[2] * yl.astype(xl.dtype)
        xl = xl + mod_l[5] * conv_ffn(modulate(rms_norm(xl, norm_ffn[i]), mod_l[3], mod_l[4]),
                                      ffn_w_gu[i], ffn_conv[i], ffn_w_down[i])
        if need_ctx:
            xc = xc + mod_c[2] * yc.astype(xc.dtype)
            xc = xc + mod_c[5] * conv_ffn(modulate(rms_norm(xc, norm_ffn[i]), mod_c[3], mod_c[4]),
                                          ffn_w_gu[i], ffn_conv[i], ffn_w_down[i])
    return xl
```

```python
import contextlib
import numpy as np
import concourse.bass as bass
import concourse.mybir as mybir
from concourse.bass_utils import run_bass_kernel_spmd

F32 = mybir.dt.float32
BF16 = mybir.dt.bfloat16
ALU = mybir.AluOpType
AF = mybir.ActivationFunctionType

D = 2048
KC = 16
DFF = 5632
FC = 44
PAD = 16
NCTX = 256
NLAT = 4096
C0 = PAD
C1 = C0 + NCTX
L0 = C1 + 2 * PAD
L1 = L0 + NLAT
TP = L1 + PAD
EPS = 1e-6
DEPTH = 4


class Res:
    __slots__ = ('name', 'w', 'r', 'multi')

    def __init__(self, name, multi=False):
        self.name = name
        self.w = {}
        self.r = {}
        self.multi = multi


class DSem:
    __slots__ = ('sem', 'val')

    def __init__(self, sem):
        self.sem = sem
        self.val = 0


class Buf:
    def __init__(self, ap, name):
        self.ap = ap
        self.res = Res(name)

    def __getitem__(self, idx):
        return self.ap[idx]


class Prog:
    QS = ('pe', 'dve', 'act', 'pool', 'sp')

    def __init__(self, nc, stack):
        self.nc = nc
        self.stack = stack
        self.q = {k: [] for k in self.QS}
        self.csem = {k: stack.enter_context(nc.semaphore('c_' + k)) for k in self.QS}
        self.cnt = {k: 0 for k in self.QS}
        self.pending = {k: False for k in self.QS}
        self.seen = {k: {} for k in self.QS}
        self.rings = {}
        self.ringpos = {}
        for q, n in (('sp', 40), ('pool', 16), ('act', 8)):
            self.rings[q] = [DSem(stack.enter_context(nc.semaphore('d_%s%d' % (q, i)))) for i in range(n)]
            self.ringpos[q] = 0
        self.ninstr = 0

    def sb(self, name, shape, dtype):
        return Buf(self.stack.enter_context(self.nc.sbuf_tensor(name, shape, dtype))[:], name)

    def ps(self, name, shape, dtype):
        return Buf(self.stack.enter_context(self.nc.psum_tensor(name, shape, dtype))[:], name)

    def op(self, q, fn, reads=(), writes=(), inc=True, dsem=None):
        deps = {}

        def add(d):
            for k, sv in d.items():
                if k not in deps or deps[k][1] < sv[1]:
                    deps[k] = sv

        for r in reads:
            add(r.w)
        for w in writes:
            add(w.w)
            add(w.r)
        if dsem is not None and dsem.val > 0:
            add({id(dsem.sem): (dsem.sem, dsem.val)})
        own = id(self.csem[q])
        if q == 'pe' and own in deps:
            del deps[own]
        seen = self.seen[q]
        waits = []
        for k, (s, v) in deps.items():
            if seen.get(k, 0) >= v:
                continue
            seen[k] = v
            waits.append((s, v))
        if dsem is not None:
            dsem.val += 16
            tick = (dsem.sem, dsem.val)
            incinfo = (dsem.sem, 16)
        else:
            tick = (self.csem[q], self.cnt[q] + 1)
            if inc:
                self.cnt[q] += 1
                incinfo = (self.csem[q], 1)
                self.pending[q] = False
            else:
                incinfo = None
                self.pending[q] = True
        k = id(tick[0])
        for r in reads:
            if k not in r.r or r.r[k][1] < tick[1]:
                r.r[k] = tick
        for w in writes:
            if w.multi:
                if k not in w.w or w.w[k][1] < tick[1]:
                    w.w[k] = tick
            else:
                w.w = {k: tick}
                w.r = {}
        self.q[q].append((waits, fn, incinfo))
        self.ninstr += 1

    def dma(self, q, out, in_, reads=(), writes=(), **kw):
        ring = self.rings[q]
        ds = ring[self.ringpos[q] % len(ring)]
        self.ringpos[q] += 1
        self.op(q, lambda e: e.dma_start(out=out, in_=in_, **kw), reads, writes, dsem=ds)

    def barrier(self):
        for q in self.QS:
            assert not self.pending[q], q
        targets = [(self.csem[k], self.cnt[k]) for k in self.QS if self.cnt[k] > 0]
        for ring in self.rings.values():
            targets += [(d.sem, d.val) for d in ring if d.val > 0]
        for q in self.QS:
            seen = self.seen[q]
            waits = []
            for (s, v) in targets:
                if q == 'pe' and s is self.csem['pe']:
                    continue
                if seen.get(id(s), 0) >= v:
                    continue
                seen[id(s)] = v
                waits.append((s, v))
            if waits:
                self.q[q].append((waits, None, None))

    def emit(self):
        nc = self.nc
        names = {'pe': 'tensor', 'dve': 'vector', 'act': 'scalar', 'pool': 'gpsimd', 'sp': 'sync'}
        with nc.Block() as block:
            for k in self.QS:
                lst = self.q[k]

                def body(eng, lst=lst):
                    for waits, fn, incinfo in lst:
                        for (s, v) in waits:
                            eng.wait_ge(s, v)
                        if fn is None:
                            continue
                        ins = fn(eng)
                        if incinfo is not None:
                            ins.then_inc(incinfo[0], incinfo[1])

                getattr(block, names[k])(body)

    def mm(self, out, lhsT, rhs, start, stop, reads, writes, inc):
        self.op('pe', lambda e: e.matmul(out, lhsT=lhsT, rhs=rhs, start=start, stop=stop), reads, writes, inc=inc)

    def tr(self, out, in_, ident, reads, writes, inc=True):
        self.op('pe', lambda e: e.transpose(out, in_, ident), reads, writes, inc=inc)

    def act(self, out, in_, func, reads, writes, **kw):
        self.op('act', lambda e: e.activation(out=out, in_=in_, func=func, **kw), reads, writes)

    def tt(self, q, out, in0, in1, op, reads, writes):
        self.op(q, lambda e: e.tensor_tensor(out=out, in0=in0, in1=in1, op=op), reads, writes)

    def ts(self, q, out, in0, s1, s2, op0, op1, reads, writes):
        if op1 is None:
            self.op(q, lambda e: e.tensor_scalar(out=out, in0=in0, scalar1=s1, scalar2=None, op0=op0), reads, writes)
        else:
            self.op(q, lambda e: e.tensor_scalar(out=out, in0=in0, scalar1=s1, scalar2=s2, op0=op0, op1=op1), reads, writes)

    def stt(self, q, out, in0, scalar, in1, op0, op1, reads, writes):
        self.op(q, lambda e: e.scalar_tensor_tensor(out=out, in0=in0, scalar=scalar, in1=in1, op0=op0, op1=op1), reads, writes)

    def copy(self, q, out, in_, reads, writes):
        if q == 'act_copy':
            self.op('act', lambda e: e.activation(out=out, in_=in_, func=AF.Copy), reads, writes)
        else:
            self.op(q, lambda e: e.tensor_copy(out=out, in_=in_), reads, writes)

    def memset(self, q, ap, val, writes):
        self.op(q, lambda e: e.memset(ap, val), (), writes)

    def recip(self, out, in_, reads, writes):
        self.op('dve', lambda e: e.reciprocal(out=out, in_=in_), reads, writes)


def colT(v):
    v = np.asarray(v, np.float32)
    F = v.shape[-1]
    r = v.reshape(v.shape[:-1] + (F // 128, 128))
    r = np.moveaxis(r, -1, 0)
    return np.ascontiguousarray(r)


def tiles_plain():
    t = [(C0, NCTX, 1)]
    for i in range(8):
        t.append((L0 + 512 * i, 512, 0))
    return t


def tiles_n(n, lat_only=False):
    t = [] if lat_only else [(C0, NCTX, 1)]
    s = 0
    while s < NLAT:
        w = min(n, NLAT - s)
        t.append((L0 + s, w, 0))
        s += w
    return t


class MK:
    def __init__(self, layers=(0, 1, 2, 3), debug=None, skip_mixer=False, last_layer=3):
        self.layers = layers
        self.debug = debug
        self.skip_mixer = skip_mixer
        self.last_layer = last_layer
        self.nc = bass.Bass("TRN2", target_bir_lowering=False)
        self.dr = {}

    def din(self, name, shape, dtype=F32):
        t = self.nc.dram_tensor(name, list(shape), dtype, kind="ExternalInput")
        self.dr[name] = t.ap()
        return self.dr[name]

    def dscr(self, name, shape, dtype):
        t = self.nc.dram_tensor(name, list(shape), dtype)
        a = t.ap()
        a_res = Res(name, multi=True)
        return a, a_res

    def build(self):
        nc = self.nc
        with contextlib.ExitStack() as stack:
            self.p = p = Prog(nc, stack)
            self.declare_io()
            self.alloc(stack)
            self.prologue()
            for l in self.layers:
                self.layer(l)
            self.epilogue_out()
            p.emit()
        return nc

    def declare_io(self):
        nc = self.nc
        self.xT_in = self.din('xT', [D, TP])
        self.cT = self.din('cT', [128, KC, 2])
        self.ada_w = self.din('ada_w', [DEPTH, D, 6 * D])
        self.abT = self.din('abT', [128, DEPTH, 96])
        self.ngT = self.din('ngT', [128, DEPTH, 2, KC])
        self.ffn_w_gu = self.din('ffn_w_gu', [DEPTH, D, 2 * DFF])
        self.ffn_cwT = self.din('ffn_cwT', [128, DEPTH, 3, FC])
        self.ffn_w_down = self.din('ffn_w_down', [DEPTH, DFF, D])
        self.in_res = Res('inputs', multi=True)
        self.ident_in = self.din('ident', [128, 128])
        self.declare_mixer_io()
        out = nc.dram_tensor('outT', [D, NLAT], F32, kind="ExternalOutput")
        self.outT = out.ap()
        self.out_res = Res('outT', multi=True)
        self.xT, self.xT_res = self.dscr('xT_s', [D, TP], F32)
        self.hT, self.hT_res = self.dscr('hT_s', [D, TP], BF16)
        self.HID, self.HID_res = self.dscr('hid_s', [20, 128, FC, 228], BF16)

    def declare_mixer_io(self):
        pass

    def alloc(self, stack):
        p = self.p
        self.ARENA = 160 * 1024
        self.arena = p.sb('arena', [128, self.ARENA // 4], F32)
        self.stage = [p.sb('stg%d' % i, [128, 512], F32) for i in range(8)]
        self.stage_i = 0
        self.psum = [p.ps('ps%d' % i, [128, 512], F32) for i in range(8)]
        self.psum_i = 0
        self.ident_f = p.sb('ident_f', [128, 128], F32)
        self.ident_b = p.sb('ident_b', [128, 128], BF16)
        self.ones_b = p.sb('ones_b', [128, 128], BF16)
        self.ones_f = p.sb('ones_f', [128, 128], F32)
        self.eps_t = p.sb('eps_t', [128, 1], F32)
        self.modT = p.sb('modT', [128, 96, 2], F32)
        self.lv = p.sb('lv', [128, 6, KC, 2], F32)
        self.ngs = p.sb('ngs', [128, DEPTH, 2, KC], F32)
        self.abs_ = p.sb('abs', [128, DEPTH, 96], F32)
        self.cws = p.sb('cws', [128, DEPTH, 3, FC], F32)
        self.scT = p.sb('scT', [128, KC, 2], F32)
        self.W = [self.aview('W%d' % i, i * 45056, [128, 22528], BF16) for i in range(2)]
        self.X = [self.aview('X%d' % i, 90112 + i * 24576, [128, 12288], BF16) for i in range(2)]
        self.WW = self.aview('WW', 0, [128, 45056], BF16)
        self.w_i = 0
        self.x_i = 0
        self.EXTRA = 90112 + 2 * 24576

    def aview(self, name, off, shape, dtype):
        nbytes = int(np.prod(shape[1:])) * (4 if dtype == F32 else 2)
        assert off % 4 == 0 and nbytes % 4 == 0 and off + nbytes <= self.ARENA, (name, off, nbytes)
        ap = self.arena.ap[:, off // 4:(off + nbytes) // 4]
        if dtype != F32:
            ap = ap.bitcast(dtype)
        if len(shape) == 3:
            ap = ap.rearrange('p (a b) -> p a b', a=shape[1])
        elif len(shape) == 4:
            ap = ap.rearrange('p (a b c) -> p a b c', a=shape[1], b=shape[2])
        return Buf(ap, name)

    def next_stage(self):
        b = self.stage[self.stage_i % len(self.stage)]
        self.stage_i += 1
        return b

    def next_psum(self):
        b = self.psum[self.psum_i % 8]
        self.psum_i += 1
        return b

    def prologue(self):
        p = self.p
        nc = self.nc
        p.memset('dve', self.ones_b[:], 1.0, [self.ones_b.res])
        p.memset('dve', self.ones_f[:], 1.0, [self.ones_f.res])
        p.memset('dve', self.eps_t[:], EPS, [self.eps_t.res])
        p.dma('sp', self.ident_f[:], self.ident_in, [self.in_res], [self.ident_f.res])
        p.copy('dve', self.ident_b[:], self.ident_f[:], [self.ident_f.res], [self.ident_b.res])
        p.dma('sp', self.ngs[:], self.ngT, [self.in_res], [self.ngs.res])
        p.dma('sp', self.abs_[:], self.abT, [self.in_res], [self.abs_.res])
        p.dma('sp', self.cws[:], self.ffn_cwT, [self.in_res], [self.cws.res])
        p.dma('sp', self.scT[:], self.cT, [self.in_res], [self.scT.res])
        p.act(self.scT[:], self.scT[:], AF.Silu, [self.scT.res], [self.scT.res])
        xin = self.xT_in.rearrange('(kc p) t -> p kc t', p=128)
        xs = self.xT.rearrange('(kc p) t -> p kc t', p=128)
        nchunk = 8
        wch = TP // nchunk
        tb = [self.aview('cp%d' % i, i * 36864, [128, KC, wch], F32) for i in range(2)]
        assert KC * wch * 4 <= 36864
        for i in range(nchunk):
            b = tb[i % 2]
            p.dma('sp', b[:], xin[:, :, i * wch:(i + 1) * wch], [self.in_res], [b.res])
            p.dma('sp', xs[:, :, i * wch:(i + 1) * wch], b[:], [b.res], [self.xT_res])
        zt = self.aview('zt', 80000, [128, KC, 2 * PAD], BF16)
        p.memset('dve', zt[:], 0.0, [zt.res])
        hs = self.hT.rearrange('(kc p) t -> p kc t', p=128)
        p.dma('sp', hs[:, :, 0:PAD], zt[:, :, 0:PAD], [zt.res], [self.hT_res])
        p.dma('sp', hs[:, :, C1:L0], zt[:, :, :], [zt.res], [self.hT_res])
        p.dma('sp', hs[:, :, L1:TP], zt[:, :, 0:PAD], [zt.res], [self.hT_res])
        p.barrier()

    def mods(self, l):
        p = self.p
        p.barrier()
        wb = [self.aview('aw%d' % i, i * 32768, [128, KC, 512], F32) for i in range(2)]
        ps = self.next_psum()
        psv = ps.ap[:, 0:192].rearrange('p (j s) -> p j s', s=2)
        for blk in range(24):
            b = wb[blk % 2]
            src = self.ada_w[l, :, blk * 512:(blk + 1) * 512].rearrange('(kc p) n -> p kc n', p=128)
            p.dma('sp', b[:], src, [self.in_res], [b.res])
            for jj in range(4):
                j = blk * 4 + jj
                for kc in range(KC):
                    p.mm(psv[:, j, :], b[:, kc, jj * 128:(jj + 1) * 128], self.scT[:, kc, :], kc == 0, kc == KC - 1,
                         [b.res, self.scT.res], [ps.res], inc=(kc == KC - 1))
        for s in range(2):
            p.tt('dve', self.modT[:, :, s], psv[:, :, s], self.abs_[:, l, :], ALU.add,
                 [ps.res, self.abs_.res], [self.modT.res])
        m = self.modT
        lv = self.lv
        for half in range(2):
            base = half * 48
            for s in range(2):
                p.stt('dve', lv[:, half * 3 + 0, :, s], m[:, base + 16:base + 32, s], 1.0, self.ngs[:, l, half, :],
                      ALU.add, ALU.mult, [m.res, self.ngs.res], [lv.res])
                p.copy('dve', lv[:, half * 3 + 1, :, s], m[:, base:base + 16, s], [m.res], [lv.res])
                p.copy('dve', lv[:, half * 3 + 2, :, s], m[:, base + 32:base + 48, s], [m.res], [lv.res])
        p.barrier()

    def norm(self, l, half, lat_only=False):
        p = self.p
        p.barrier()
        xin = [self.aview('nx%d' % i, i * 32768, [128, KC, 512], F32) for i in range(2)]
        sq = self.aview('nsq', 65536, [128, KC, 512], BF16)
        ob = [self.aview('nob%d' % i, 81920 + i * 16384, [128, KC, 512], BF16) for i in range(2)]
        rs = self.aview('nrs', 114688, [128, 512], F32)
        tmp = [self.aview('ntmp%d' % i, 116736 + i * 2048, [128, 512], F32) for i in range(4)]
        xs = self.xT.rearrange('(kc p) t -> p kc t', p=128)
        hs = self.hT.rearrange('(kc p) t -> p kc t', p=128)
        lv = self.lv
        tl = tiles_plain()
        if lat_only:
            tl = tl[1:]
        for ti, (c0, w, s) in enumerate(tl):
            xb = xin[ti % 2]
            o = ob[ti % 2]
            p.dma('sp', xb[:, :, 0:w], xs[:, :, c0:c0 + w], [self.xT_res], [xb.res])
            p.act(sq[:, :, 0:w], xb[:, :, 0:w], AF.Square, [xb.res], [sq.res])
            ps = self.next_psum()
            for kc in range(KC):
                p.mm(ps[:, 0:w], self.ones_b[:], sq[:, kc, 0:w], kc == 0, kc == KC - 1, [sq.res, self.ones_b.res],
                     [ps.res], inc=(kc == KC - 1))
            p.act(rs[:, 0:w], ps[:, 0:w], AF.Sqrt, [ps.res, self.eps_t.res], [rs.res], bias=self.eps_t[:, 0:1],
                  scale=1.0 / D)
            p.recip(rs[:, 0:w], rs[:, 0:w], [rs.res], [rs.res])
            for kc in range(KC):
                t = tmp[kc % 4]
                p.tt('dve', t[:, 0:w], xb[:, kc, 0:w], rs[:, 0:w], ALU.mult, [xb.res, rs.res], [t.res])
                p.act(o[:, kc, 0:w], t[:, 0:w], AF.Identity, [t.res, lv.res], [o.res],
                      scale=lv[:, half * 3 + 0, kc, s:s + 1], bias=lv[:, half * 3 + 1, kc, s:s + 1])
            p.dma('sp', hs[:, :, c0:c0 + w], o[:, :, 0:w], [o.res], [self.hT_res])
        p.barrier()

    def linear(self, xsrc, xres, Kc, wblocks, tiles, epi, xload=None, defer=False, wide=False):
        p = self.p
        pend = None
        wbufs = {}
        xbufs = {}

        def load_w(bi):
            blk = wblocks[bi]
            ncols = sum(n for _, n in blk['loads'])
            if wide:
                assert Kc * ncols <= 45056, (Kc, ncols)
                wres = [self.W[0].res, self.W[1].res]
                wv = self.WW.ap[:, 0:Kc * ncols].rearrange('p (kc n) -> p kc n', kc=Kc)
            else:
                wb = self.W[self.w_i % 2]
                self.w_i += 1
                assert Kc * ncols <= 22528, (Kc, ncols)
                wres = [wb.res]
                wv = wb.ap[:, 0:Kc * ncols].rearrange('p (kc n) -> p kc n', kc=Kc)
            off = 0
            for (src, n) in blk['loads']:
                p.dma('pool', wv[:, :, off:off + n], src.rearrange('(kc p) n -> p kc n', p=128),
                      [self.in_res], wres)
                off += n
            wbufs[bi] = (wres, wv)

        def load_x(bi, ti):
            tile = tiles[ti]
            c0, w, s, hl, hr = tile
            wt = w + hl + hr
            xb = self.X[self.x_i % 2]
            self.x_i += 1
            assert Kc * wt <= 12288
            xv = xb.ap[:, 0:Kc * wt].rearrange('p (kc t) -> p kc t', kc=Kc)
            if xload is not None:
                xload(ti, tile, xv, xb)
            else:
                p.dma('pool', xv, xsrc[:, c0 - hl:c0 + w + hr].rearrange('(kc p) t -> p kc t', p=128),
                      [xres], [xb.res])
            xbufs[(bi, ti)] = (xb, xv)

        its = [(bi, ti) for bi in range(len(wblocks)) for ti in range(len(tiles))]
        load_w(0)
        load_x(*its[0])
        for n, (bi, ti) in enumerate(its):
            if n + 1 < len(its):
                load_x(*its[n + 1])
            if bi + 1 < len(wblocks) and not wide and ti == 0:
                load_w(bi + 1)
            blk = wblocks[bi]
            wres, wv = wbufs[bi]
            xb, xv = xbufs.pop((bi, ti))
            tile = tiles[ti]
            c0, w, s, hl, hr = tile
            wt = w + hl + hr
            for gi, grp in enumerate(blk['groups']):
                pss = []
                for (co, cw) in grp:
                    ps = self.next_psum()
                    for kc in range(Kc):
                        p.mm(ps[0:cw, 0:wt], wv[:, kc, co:co + cw], xv[:, kc, :], kc == 0, kc == Kc - 1,
                             wres + [xb.res], [ps.res], inc=(kc == Kc - 1))
                    pss.append(ps)
                if defer:
                    if pend is not None:
                        epi(*pend)
                    pend = (bi, gi, ti, tile, pss)
                else:
                    epi(bi, gi, ti, tile, pss)
            if wide and bi + 1 < len(wblocks) and ti == len(tiles) - 1:
                load_w(bi + 1)
        if pend is not None:
            epi(*pend)

    def epi_residual(self, gate_kind, chunk_of, col_map=None):
        p = self.p
        xs = self.xT

        def epi(bi, gi, ti, tile, pss):
            c0, w, s, hl, hr = tile
            dc = chunk_of(bi, gi)
            ps = pss[0]
            st = self.next_stage()
            p.dma('pool', st[:, 0:w], xs[dc * 128:(dc + 1) * 128, c0:c0 + w], [self.xT_res], [st.res])
            p.stt('dve', st[:, 0:w], ps[:, hl:hl + w], self.lv[:, gate_kind, dc, s:s + 1], st[:, 0:w], ALU.mult, ALU.add,
                  [ps.res, st.res, self.lv.res], [st.res])
            p.dma('sp', xs[dc * 128:(dc + 1) * 128, c0:c0 + w], st[:, 0:w], [st.res], [self.xT_res])

        return epi

    def ffn(self, l, lat_only=False):
        p = self.p
        base = tiles_n(456, lat_only)
        tiles = [(c0, w, s, 1, 1) for (c0, w, s) in base]
        toff = 1 if lat_only else 0
        wblocks = []
        f = 0
        while f < FC:
            nf = 1 if f == 0 else min(5, FC - f)
            wblocks.append(dict(
                loads=[(self.ffn_w_gu[l, :, f * 128:(f + nf) * 128], nf * 128),
                       (self.ffn_w_gu[l, :, DFF + f * 128:DFF + (f + nf) * 128], nf * 128)],
                groups=[[(j * 128, 128), (nf * 128 + j * 128, 128)] for j in range(nf)], f0=f))
            f += nf
        cws = self.cws

        def epi_up(bi, gi, ti, tile, pss):
            c0, w, s, hl, hr = tile
            fch = wblocks[bi]['f0'] + gi
            pg, pu = pss
            t1 = self.next_stage()
            sg = self.next_stage()
            hb = self.next_stage()
            hbv = hb.ap.bitcast(BF16)
            p.ts('dve', t1[:, 0:w], pg[:, 1:w + 1], cws[:, l, 1, fch:fch + 1], None, ALU.mult, None,
                 [pg.res, cws.res], [t1.res])
            p.stt('dve', t1[:, 0:w], pg[:, 0:w], cws[:, l, 0, fch:fch + 1], t1[:, 0:w], ALU.mult, ALU.add,
                  [pg.res, cws.res, t1.res], [t1.res])
            p.stt('dve', t1[:, 0:w], pg[:, 2:w + 2], cws[:, l, 2, fch:fch + 1], t1[:, 0:w], ALU.mult, ALU.add,
                  [pg.res, cws.res, t1.res], [t1.res])
            p.act(sg[:, 0:w], t1[:, 0:w], AF.Silu, [t1.res], [sg.res])
            p.tt('dve', hbv[:, 0:w], sg[:, 0:w], pu[:, 1:w + 1], ALU.mult, [sg.res, pu.res], [hb.res])
            wh = w // 2
            i0 = 2 * (ti + toff)
            p.dma('sp', self.HID[i0:i0 + 2, :, fch, 0:wh].rearrange('h p w -> p h w'),
                  hbv[:, 0:w].rearrange('p (h w) -> p h w', h=2), [hb.res], [self.HID_res])

        self.linear(self.hT, self.hT_res, KC, wblocks, tiles, epi_up)
        dtiles = []
        for ti, (c0, w, s) in enumerate(base):
            wh = w // 2
            for h in range(2):
                dtiles.append((c0 + h * wh, wh, s, 0, 0))
        dblocks = []
        for (b0, nb) in ((0, 8), (8, 8)):
            dblocks.append(dict(loads=[(self.ffn_w_down[l, :, b0 * 128:(b0 + nb) * 128], nb * 128)],
                                groups=[[(j * 128, 128)] for j in range(nb)], c0=b0))

        def xload(ti, tile, xv, xb):
            c0, w, s, hl, hr = tile
            p.dma('pool', xv, self.HID[ti + 2 * toff, :, :, 0:w], [self.HID_res], [xb.res])

        self.linear(None, None, FC, dblocks, dtiles, self.epi_residual(5, lambda bi, gi: dblocks[bi]['c0'] + gi), xload=xload, wide=True)

    def layer(self, l):
        last = (l == self.last_layer)
        self.mods(l)
        if not self.skip_mixer:
            self.norm(l, 0)
            self.mixer(l, last)
        self.norm(l, 1, lat_only=last)
        self.ffn(l, lat_only=last)

    def mixer(self, l, last):
        raise NotImplementedError

    def epilogue_out(self):
        p = self.p
        p.barrier()
        xs = self.xT.rearrange('(kc p) t -> p kc t', p=128)
        os_ = self.outT.rearrange('(kc p) t -> p kc t', p=128)
        tb = [self.aview('ocp%d' % i, i * 32768, [128, KC, 512], F32) for i in range(2)]
        for i in range(8):
            b = tb[i % 2]
            p.dma('sp', b[:], xs[:, :, L0 + i * 512:L0 + (i + 1) * 512], [self.xT_res], [b.res])
            p.dma('sp', os_[:, :, i * 512:(i + 1) * 512], b[:], [b.res], [self.out_res])
        if self.debug:
            self.debug_out()
        p.barrier()

    def debug_out(self):
        pass


def prep_common(inp, b):
    xT = np.zeros((D, TP), np.float32)
    xT[:, C0:C1] = inp['ctx'][b].T
    xT[:, L0:L1] = inp['x'][b].T
    cT = np.stack([colT(inp['c'][b]), colT(inp['c_ctx'])], axis=-1)
    m = {
        'xT': xT,
        'cT': np.ascontiguousarray(cT),
        'ada_w': inp['ada_w'],
        'abT': colT(inp['ada_b']),
        'ngT': colT(np.stack([inp['norm_mix'], inp['norm_ffn']], axis=1)),
        'ffn_w_gu': inp['ffn_w_gu'],
        'ffn_cwT': colT(inp['ffn_conv']),
        'ffn_w_down': inp['ffn_w_down'],
        'ident': np.eye(128, dtype=np.float32),
    }
    return m


def kernel(**inputs):
    inp = {k: np.asarray(v) for k, v in inputs.items()}
    mk = MKFull()
    nc = mk.build()
    in_maps = []
    for b in range(8):
        m = prep_common(inp, b)
        m.update(prep_mixers(inp, b))
        in_maps.append(m)
    res = run_bass_kernel_spmd(nc, in_maps, core_ids=list(range(8)))
    out = np.stack([np.ascontiguousarray(r['outT'].T) for r in res.results], axis=0)
    return out.astype(np.float32)


class MKFull(MK):
    def declare_mixer_io(self):
        din = self.din
        L = self.layers
        if 0 in L:
            self.ml_w_in = din('ml_w_in', [D, 6144])
            self.ml_w_qkp = din('ml_w_qkp', [D, 2048])
            self.ml_w_gate = din('ml_w_gate', [D, 32])
            self.ml_bg = din('ml_bg', [32, 1])
            self.ml_hg = din('ml_hg', [1, 2048])
            self.ml_w_out = din('ml_w_out', [D, D])
            self.rope = din('rope', [4, 128, TP])
            self.tri = din('tri', [2, 128, 128])
        if 1 in L:
            self.na_w_qkv = din('na_w_qkv', [D, 6144])
            self.na_g = din('na_g', [128, 2])
            self.na_tab = din('na_tab', [16, 2, 128, 16, 64])
            self.na_w_o = din('na_w_o', [D, D])
        if 2 in L:
            self.cv_w_pw1 = din('cv_w_pw1', [D, 4096])
            self.cv_dwT = din('cv_dwT', [128, 16, 31])
            self.cv_vT = din('cv_vT', [128, 3, 16])
            self.cv_w_pw2 = din('cv_w_pw2', [D, D])
        if 3 in L:
            self.lr_w_in = din('lr_w_in', [D, 4096])
            self.lr_cvT = din('lr_cvT', [128, 5, 16])
            self.lr_w_gate = din('lr_w_gate', [4, 8, 256, 256])
            self.lr_bgT = din('lr_bgT', [128, 4, 16])
            self.lr_lamT = din('lr_lamT', [128, 2, 16])
            self.lr_w_out = din('lr_w_out', [D, D])
        self.S = {}
        self.Sres = {}
        for n in ('S1', 'S2', 'S3', 'S4', 'S5'):
            self.S[n], self.Sres[n] = self.dscr(n, [D, TP], BF16)
        self.F1, self.F1_res = self.dscr('F1', [D, TP], F32)

    def prologue(self):
        MK.prologue(self)
        p = self.p
        zt = self.aview('zt2', 80000, [128, KC, 2 * PAD], BF16)
        p.memset('dve', zt[:], 0.0, [zt.res])
        hs = self.S['S1'].rearrange('(kc p) t -> p kc t', p=128)
        p.dma('sp', hs[:, :, 0:PAD], zt[:, :, 0:PAD], [zt.res], [self.Sres['S1']])
        p.dma('sp', hs[:, :, C1:L0], zt[:, :, :], [zt.res], [self.Sres['S1']])
        p.dma('sp', hs[:, :, L1:TP], zt[:, :, 0:PAD], [zt.res], [self.Sres['S1']])
        p.barrier()

    def mixer(self, l, last):
        [self.mixer_ml, self.mixer_na, self.mixer_cv, self.mixer_lr][l % 4](l, last)

    def bfstage(self):
        st = self.next_stage()
        return st, st.ap.bitcast(BF16)

    def plain(self, lat_only=False):
        t = [(c0, w, s, 0, 0) for (c0, w, s) in tiles_plain()]
        return t[1:] if lat_only else t

    def out_proj(self, w_ap, src, last):
        blocks = []
        for (b0, nb) in ((0, 2), (2, 7), (9, 7)):
            blocks.append(dict(loads=[(w_ap[:, b0 * 128:(b0 + nb) * 128], nb * 128)],
                               groups=[[(j * 128, 128)] for j in range(nb)], c0=b0))
        self.linear(self.S[src], self.Sres[src], KC, blocks, self.plain(last),
                    self.epi_residual(2, lambda bi, gi: blocks[bi]['c0'] + gi))

    def mixer_cv(self, l, last):
        p = self.p
        S1, S1r = self.S['S1'], self.Sres['S1']
        S2, S2r = self.S['S2'], self.Sres['S2']
        blocks = []
        j = 0
        while j < 16:
            nf = min(5, 16 - j)
            blocks.append(dict(loads=[(self.cv_w_pw1[:, j * 128:(j + nf) * 128], nf * 128),
                                      (self.cv_w_pw1[:, D + j * 128:D + (j + nf) * 128], nf * 128)],
                               groups=[[(i * 128, 128), (nf * 128 + i * 128, 128)] for i in range(nf)], f0=j))
            j += nf

        def epiA(bi, gi, ti, tile, pss):
            c0, w, s, hl, hr = tile
            jc = blocks[bi]['f0'] + gi
            pa, pg = pss
            sg = self.next_stage()
            hb, hbv = self.bfstage()
            p.act(sg[:, 0:w], pg[:, 0:w], AF.Sigmoid, [pg.res], [sg.res])
            p.tt('dve', hbv[:, 0:w], pa[:, 0:w], sg[:, 0:w], ALU.mult, [pa.res, sg.res], [hb.res])
            p.dma('sp', S1[jc * 128:(jc + 1) * 128, c0:c0 + w], hbv[:, 0:w], [hb.res], [S1r])

        self.linear(self.hT, self.hT_res, KC, blocks, self.plain(), epiA)
        p.barrier()
        dg = [self.aview('dg%d' % i, i * 8192, [128, 31, 128], BF16) for i in range(2)]
        xt = [self.aview('cxt%d' % i, 16384 + i * 2048, [128, 544], BF16) for i in range(4)]
        dwT = self.aview('dwT', 24576, [128, 16, 31], F32)
        vT = self.aview('vT', 28672, [128, 3, 16], F32)
        p.dma('sp', dwT[:], self.cv_dwT, [self.in_res], [dwT.res])
        p.dma('sp', vT[:], self.cv_vT, [self.in_res], [vT.res])
        n = 0
        for jc in range(16):
            d = dg[jc % 2]
            for k in range(31):
                if k % 2 == 0:
                    p.ts('dve', d[:, k, :], self.ident_b[:], dwT[:, jc, k:k + 1], None, ALU.mult, None,
                         [self.ident_b.res, dwT.res], [d.res])
                else:
                    p.act(d[:, k, :], self.ident_b[:], AF.Copy, [self.ident_b.res, dwT.res], [d.res],
                          scale=dwT[:, jc, k:k + 1])
            for (c0, w, s, _, _) in self.plain():
                x = xt[n % 4]
                n += 1
                p.dma('pool', x[:, 0:w + 30], S1[jc * 128:(jc + 1) * 128, c0 - 15:c0 + w + 15], [S1r], [x.res])
                ps = self.next_psum()
                for k in range(31):
                    p.mm(ps[:, 0:w], d[:, k, :], x[:, k:k + w], k == 0, k == 30, [d.res, x.res], [ps.res], inc=(k == 30))
                st = self.next_stage()
                p.act(st[:, 0:w], ps[:, 0:w], AF.Identity, [ps.res, vT.res], [st.res], bias=vT[:, 0, jc:jc + 1], scale=1.0)
                p.dma('sp', self.F1[jc * 128:(jc + 1) * 128, c0:c0 + w], st[:, 0:w], [st.res], [self.F1_res])
        p.barrier()
        yin = [self.aview('ly%d' % i, i * 32768, [128, KC, 512], F32) for i in range(2)]
        sq = self.aview('lsq', 65536, [128, KC, 512], BF16)
        ob = [self.aview('lob%d' % i, 81920 + i * 16384, [128, KC, 512], BF16) for i in range(2)]
        sm = [self.aview('lsm%d' % i, 114688 + i * 2048, [128, 512], F32) for i in range(7)]
        vT = self.aview('vT2', 131072, [128, 3, 16], F32)
        p.dma('sp', vT[:], self.cv_vT, [self.in_res], [vT.res])
        mean, m2, rstd = sm[0], sm[1], sm[2]
        tmp = sm[3:7]
        ys = self.F1.rearrange('(kc p) t -> p kc t', p=128)
        os_ = S2.rearrange('(kc p) t -> p kc t', p=128)
        for ti, (c0, w, s, _, _) in enumerate(self.plain()):
            y = yin[ti % 2]
            o = ob[ti % 2]
            p.dma('sp', y[:, :, 0:w], ys[:, :, c0:c0 + w], [self.F1_res], [y.res])
            p.act(sq[:, :, 0:w], y[:, :, 0:w], AF.Square, [y.res], [sq.res])
            ps1 = self.next_psum()
            ps2 = self.next_psum()
            for kc in range(KC):
                p.mm(ps1[:, 0:w], self.ones_f[:], y[:, kc, 0:w], kc == 0, kc == KC - 1, [y.res, self.ones_f.res],
                     [ps1.res], inc=(kc == KC - 1))
            for kc in range(KC):
                p.mm(ps2[:, 0:w], self.ones_b[:], sq[:, kc, 0:w], kc == 0, kc == KC - 1, [sq.res, self.ones_b.res],
                     [ps2.res], inc=(kc == KC - 1))
            p.act(mean[:, 0:w], ps1[:, 0:w], AF.Copy, [ps1.res], [mean.res], scale=1.0 / D)
            p.tt('dve', m2[:, 0:w], mean[:, 0:w], mean[:, 0:w], ALU.mult, [mean.res], [m2.res])
            p.stt('dve', m2[:, 0:w], ps2[:, 0:w], 1.0 / D, m2[:, 0:w], ALU.mult, ALU.subtract, [ps2.res, m2.res], [m2.res])
            p.act(rstd[:, 0:w], m2[:, 0:w], AF.Sqrt, [m2.res, self.eps_t.res], [rstd.res], bias=self.eps_t[:, 0:1], scale=1.0)
            p.recip(rstd[:, 0:w], rstd[:, 0:w], [rstd.res], [rstd.res])
            for kc in range(KC):
                t = tmp[kc % 4]
                p.tt('dve', t[:, 0:w], y[:, kc, 0:w], mean[:, 0:w], ALU.subtract, [y.res, mean.res], [t.res])
                p.tt('dve', t[:, 0:w], t[:, 0:w], rstd[:, 0:w], ALU.mult, [t.res, rstd.res], [t.res])
                p.act(o[:, kc, 0:w], t[:, 0:w], AF.Silu, [t.res, vT.res], [o.res], scale=vT[:, 1, kc:kc + 1],
                      bias=vT[:, 2, kc:kc + 1])
            p.dma('sp', os_[:, :, c0:c0 + w], o[:, :, 0:w], [o.res], [S2r])
        p.barrier()
        self.out_proj(self.cv_w_pw2, 'S2', last)

    def mixer_lr(self, l, last):
        p = self.p
        S1, S1r = self.S['S1'], self.Sres['S1']
        S2, S2r = self.S['S2'], self.Sres['S2']
        S3, S3r = self.S['S3'], self.Sres['S3']
        cv = self.aview('lrcv', self.EXTRA, [128, 5, 16], F32)
        p.dma('sp', cv[:], self.lr_cvT, [self.in_res], [cv.res])
        tiles = [(c0, w, s, 2, 1) for (c0, w, s) in tiles_n(456)]
        blocks = [dict(loads=[(self.lr_w_in[:, b * 1024:(b + 1) * 1024], 1024)], groups=[[(j * 128, 128)] for j in range(8)])
                  for b in range(4)]

        def epiA(bi, gi, ti, tile, pss):
            c0, w, s, hl, hr = tile
            ps = pss[0]
            if bi < 2:
                c = bi * 8 + gi
                st, stv = self.bfstage()
                p.act(stv[:, 0:w], ps[:, hl:hl + w], AF.Gelu_apprx_tanh, [ps.res], [st.res])
                p.dma('sp', S2[c * 128:(c + 1) * 128, c0:c0 + w], stv[:, 0:w], [st.res], [S2r])
            else:
                c = (bi - 2) * 8 + gi
                t = self.next_stage()
                tb, tbv = self.bfstage()
                p.ts('dve', t[:, 0:w], ps[:, 0:w], cv[:, 0, c:c + 1], cv[:, 4, c:c + 1], ALU.mult, ALU.add,
                     [ps.res, cv.res], [t.res])
                for k in range(1, 4):
                    p.stt('dve', t[:, 0:w], ps[:, k:k + w], cv[:, k, c:c + 1], t[:, 0:w], ALU.mult, ALU.add,
                          [ps.res, cv.res, t.res], [t.res])
                p.act(tbv[:, 0:w], t[:, 0:w], AF.Copy, [t.res], [tb.res])
                p.dma('sp', self.F1[c * 128:(c + 1) * 128, c0:c0 + w], t[:, 0:w], [t.res], [self.F1_res])
                p.dma('sp', S1[c * 128:(c + 1) * 128, c0:c0 + w], tbv[:, 0:w], [tb.res], [S1r])

        self.linear(self.hT, self.hT_res, KC, blocks, tiles, epiA)
        p.barrier()
        SEQ = TP * 4
        A = [self.aview('lrA%d' % d, (2 * d) * SEQ, [128, TP], F32) for d in range(2)]
        Bv = [self.aview('lrB%d' % d, (2 * d + 1) * SEQ, [128, TP], F32) for d in range(2)]
        ubf = self.aview('lrubf', 4 * SEQ, [128, 2, TP], BF16)
        uf = self.aview('lruf', 5 * SEQ, [128, TP], F32)
        gg = self.aview('lrgg', 6 * SEQ, [128, TP], BF16)
        ob = self.aview('lrob', 6 * SEQ + TP * 2, [128, TP], BF16)
        o2 = 7 * SEQ
        wg = [self.aview('lrwg%d' % i, o2 + i * 2048, [128, 4, 2, 128], BF16) for i in range(2)]
        bgT = self.aview('lrbg', o2 + 4096, [128, 4, 16], F32)
        lam = self.aview('lrlam', o2 + 4096 + 256, [128, 2, 16], F32)
        kap = self.aview('lrkap', o2 + 4096 + 512, [128, 2, 2, 16], F32)
        p.dma('sp', bgT[:], self.lr_bgT, [self.in_res], [bgT.res])
        p.dma('sp', lam[:], self.lr_lamT, [self.in_res], [lam.res])
        p.act(lam[:], lam[:], AF.Exp, [lam.res], [lam.res], scale=-1.0)
        p.act(lam[:], lam[:], AF.Ln, [lam.res, self.ones_f.res], [lam.res], bias=self.ones_f[:, 0:1], scale=1.0)
        p.ts('dve', kap[:, 0, :, :], lam[:], -LRU_C, None, ALU.mult, None, [lam.res], [kap.res])
        p.ts('dve', kap[:, 1, :, :], lam[:], -2.0 * LRU_C, None, ALU.mult, None, [lam.res], [kap.res])
        pl = self.plain()

        def rev(b, a0, a1):
            return b.ap[:, a0:a1][:, ::-1]

        for c in range(16):
            nb, jo = c // 2, c % 2
            w = wg[c % 2]
            for t in range(4):
                p.dma('pool', w[:, t, :, :],
                      self.lr_w_gate[t, nb, :, jo * 128:(jo + 1) * 128].rearrange('(ki p) j -> p ki j', p=128),
                      [self.in_res], [w.res])
            if jo == 0:
                p.dma('sp', ubf[:], S1[nb * 256:(nb + 1) * 256, :].rearrange('(ki p) t -> p ki t', p=128), [S1r], [ubf.res])
            p.dma('sp', uf[:], self.F1[c * 128:(c + 1) * 128, :], [self.F1_res], [uf.res])
            p.dma('sp', gg[:], S2[c * 128:(c + 1) * 128, :], [S2r], [gg.res])
            for (c0, w_, s, _, _) in pl:
                pss = [self.next_psum() for _ in range(4)]
                for t in range(4):
                    for ki in range(2):
                        p.mm(pss[t][:, 0:w_], w[:, t, ki, :], ubf[:, ki, c0:c0 + w_], ki == 0, ki == 1,
                             [w.res, ubf.res], [pss[t].res], inc=(ki == 1))
                rr = [self.next_stage() for _ in range(2)]
                ii = [self.next_stage() for _ in range(2)]
                r2s = [self.next_stage() for _ in range(2)]
                for d in range(2):
                    p.act(rr[d][:, 0:w_], pss[2 * d][:, 0:w_], AF.Sigmoid, [pss[2 * d].res, bgT.res], [rr[d].res],
                          bias=bgT[:, 2 * d, c:c + 1], scale=1.0)
                    p.act(ii[d][:, 0:w_], pss[2 * d + 1][:, 0:w_], AF.Sigmoid, [pss[2 * d + 1].res, bgT.res], [ii[d].res],
                          bias=bgT[:, 2 * d + 1, c:c + 1], scale=1.0)
                for d in range(2):
                    p.act(A[d][:, c0:c0 + w_], rr[d][:, 0:w_], AF.Exp, [rr[d].res, kap.res], [A[d].res],
                          scale=kap[:, 0, d, c:c + 1])
                    p.tt('dve', r2s[d][:, 0:w_], A[d][:, c0:c0 + w_], A[d][:, c0:c0 + w_], ALU.mult, [A[d].res], [r2s[d].res])
                for d in range(2):
                    p.act(r2s[d][:, 0:w_], r2s[d][:, 0:w_], AF.Sqrt, [r2s[d].res, self.ones_f.res], [r2s[d].res], scale=-1.0,
                          bias=self.ones_f[:, 0:1])
                    p.tt('pool', ii[d][:, 0:w_], ii[d][:, 0:w_], r2s[d][:, 0:w_], ALU.mult, [ii[d].res, r2s[d].res], [ii[d].res])
                    p.tt('dve', Bv[d][:, c0:c0 + w_], ii[d][:, 0:w_], uf[:, c0:c0 + w_], ALU.mult, [ii[d].res, uf.res], [Bv[d].res])

            def scan(out, a, b, init, reads, writes):
                p.op('dve', lambda e: e.tensor_tensor_scan(out=out, data0=a, data1=b, initial=init, op0=ALU.mult,
                                                          op1=ALU.add), reads, writes)

            rw = ([A[0].res, Bv[0].res], [Bv[0].res])
            scan(Bv[0][:, C0:C1], A[0][:, C0:C1], Bv[0][:, C0:C1], 0.0, *rw)
            scan(Bv[0][:, L0:L1], A[0][:, L0:L1], Bv[0][:, L0:L1], Bv[0][:, C1 - 1:C1], *rw)
            rw = ([A[1].res, Bv[1].res], [Bv[1].res])
            scan(rev(Bv[1], C0, C1), rev(A[1], C0, C1), rev(Bv[1], C0, C1), 0.0, *rw)
            scan(rev(Bv[1], L0, L1), rev(A[1], L0, L1), rev(Bv[1], L0, L1), Bv[1][:, C0:C0 + 1], *rw)
            segs = [(L0 + i * 1024, L0 + (i + 1) * 1024) for i in range(4)]
            if not last:
                segs = [(C0, C1)] + segs
            for (a0, a1) in segs:
                p.tt('dve', Bv[0][:, a0:a1], Bv[0][:, a0:a1], Bv[1][:, a0:a1], ALU.add, [Bv[0].res, Bv[1].res], [Bv[0].res])
                p.tt('dve', ob[:, a0:a1], Bv[0][:, a0:a1], gg[:, a0:a1], ALU.mult, [Bv[0].res, gg.res], [ob.res])
            if not last:
                p.dma('sp', S3[c * 128:(c + 1) * 128, C0:C1], ob[:, C0:C1], [ob.res], [S3r])
            p.dma('sp', S3[c * 128:(c + 1) * 128, L0:L1], ob[:, L0:L1], [ob.res], [S3r])
        p.barrier()
        self.out_proj(self.lr_w_out, 'S3', last)


    def mixer_ml(self, l, last):
        p = self.p
        S1, S1r = self.S['S1'], self.Sres['S1']
        S2, S2r = self.S['S2'], self.Sres['S2']
        S3, S3r = self.S['S3'], self.Sres['S3']
        S4, S4r = self.S['S4'], self.Sres['S4']
        S5, S5r = self.S['S5'], self.Sres['S5']
        bg = self.aview('mlbg', self.EXTRA, [128, 1], F32)
        p.dma('sp', bg[0:32, :], self.ml_bg, [self.in_res], [bg.res])
        blocks = []
        for which in range(2):
            for (h0, nh) in ((0, 5), (5, 3)):
                cb = which * 1024 + h0 * 128
                blocks.append(dict(kind='qk', which=which, f0=h0,
                                   loads=[(self.ml_w_in[:, cb:cb + nh * 128], nh * 128),
                                          (self.ml_w_qkp[:, cb:cb + nh * 128], nh * 128)],
                                   groups=[[(i * 128, 128), (nh * 128 + i * 128, 128)] for i in range(nh)]))
        for b in range(2):
            blocks.append(dict(kind='v', f0=b * 8, loads=[(self.ml_w_in[:, 2048 + b * 1024:2048 + (b + 1) * 1024], 1024)],
                               groups=[[(i * 128, 128)] for i in range(8)]))
        blocks.append(dict(kind='o', f0=0, loads=[(self.ml_w_in[:, 4096:5120], 1024)],
                           groups=[[(i * 128, 128)] for i in range(8)]))
        blocks.append(dict(kind='o', f0=8, loads=[(self.ml_w_in[:, 5120:6144], 1024), (self.ml_w_gate[:, 0:32], 32)],
                           groups=[[(i * 128, 128)] for i in range(8)] + [[(1024, 32)]]))

        def epiA(bi, gi, ti, tile, pss):
            c0, w, s, hl, hr = tile
            blk = blocks[bi]
            if blk['kind'] == 'qk':
                which = blk['which']
                h = blk['f0'] + gi
                pa, pb = pss
                ct = self.next_stage()
                sn = self.next_stage()
                p.dma('pool', ct[:, 0:w], self.rope[2 * which, :, c0:c0 + w], [self.in_res], [ct.res])
                p.dma('pool', sn[:, 0:w], self.rope[2 * which + 1, :, c0:c0 + w], [self.in_res], [sn.res])
                p.tt('dve', ct[:, 0:w], pa[:, 0:w], ct[:, 0:w], ALU.mult, [pa.res, ct.res], [ct.res])
                p.tt('dve', sn[:, 0:w], pb[:, 0:w], sn[:, 0:w], ALU.mult, [pb.res, sn.res], [sn.res])
                ob, obv = self.bfstage()
                p.tt('dve', obv[:, 0:w], ct[:, 0:w], sn[:, 0:w], ALU.add, [ct.res, sn.res], [ob.res])
                dst, dr = (S1, S1r) if which == 0 else (S2, S2r)
                p.dma('sp', dst[h * 128:(h + 1) * 128, c0:c0 + w], obv[:, 0:w], [ob.res], [dr])
            elif blk['kind'] == 'v':
                c = blk['f0'] + gi
                ob, obv = self.bfstage()
                p.act(obv[:, 0:w], pss[0][:, 0:w], AF.Copy, [pss[0].res], [ob.res])
                p.dma('sp', S3[c * 128:(c + 1) * 128, c0:c0 + w], obv[:, 0:w], [ob.res], [S3r])
            else:
                ps = pss[0]
                if gi < 8:
                    c = blk['f0'] + gi
                    ob, obv = self.bfstage()
                    p.act(obv[:, 0:w], ps[:, 0:w], AF.Sigmoid, [ps.res], [ob.res])
                    p.dma('sp', S4[c * 128:(c + 1) * 128, c0:c0 + w], obv[:, 0:w], [ob.res], [S4r])
                else:
                    st = self.next_stage()
                    p.act(st[0:32, 0:w], ps[0:32, 0:w], AF.Identity, [ps.res, bg.res], [st.res], bias=bg[0:32, 0:1], scale=1.0)
                    p.dma('sp', self.F1[0:32, c0:c0 + w], st[0:32, 0:w], [st.res], [self.F1_res])

        self.linear(self.hT, self.hT_res, KC, blocks, self.plain(), epiA)
        p.barrier()
        gT = self.aview('mlgT', 0, [128, TP], F32)
        qT = self.aview('mlq', 17664, [128, TP], BF16)
        kT = self.aview('mlk', 26496, [128, TP], BF16)
        vT = self.aview('mlv', 35328, [128, 2, TP], BF16)
        so = self.aview('mlso', 52992, [128, 2, TP], BF16)
        Hs = self.aview('mlH', 70656, [128, 34, 256], F32)
        hgrow = self.aview('mlhgr', 70656, [128, 2048], F32)
        Ob = self.aview('mlO', 105472, [128, 2, TP], BF16)
        hgB = self.aview('mlhgB', 123136, [128, 2048], F32)
        Gt = self.aview('mlGt', 131328, [128, 34, 32], F32)
        NL = self.aview('mlNL', 135680, [128, 34, 32], F32)
        LF = [self.aview('mlLF%d' % d, 140032 + d * 1088, [128, 34, 8], F32) for d in range(2)]
        Wd = [self.aview('mlW%d' % d, 142208 + d * 1088, [128, 34, 8], F32) for d in range(2)]
        Ed = [self.aview('mlE%d' % d, 144384 + d * 1088, [128, 34, 8], F32) for d in range(2)]
        ELd = [self.aview('mlEL%d' % d, 146560 + d * 1088, [128, 34, 8], F32) for d in range(2)]
        Cs = [self.aview('mlC%d' % d, 148736 + d * 1032, [128, 258], F32) for d in range(2)]
        Cb = [self.aview('mlCb%d' % d, 150800 + d * 516, [128, 258], BF16) for d in range(2)]
        Vta = [self.aview('mlVta%d' % d, 151832 + d * 516, [128, 258], BF16) for d in range(4)]
        tri = self.aview('mltri', 153896, [128, 2, 128], F32)
        ssb = [self.aview('mlss%d' % i, 154920 + i * 16, [128, 4], F32) for i in range(8)]
        p.dma('sp', tri[:], self.tri.rearrange('a p t -> p a t'), [self.in_res], [tri.res])
        p.dma('sp', gT[0:32, :], self.F1[0:32, :], [self.F1_res], [gT.res])
        p.dma('sp', hgrow[0:1, :], self.ml_hg, [self.in_res], [hgrow.res])
        for i in range(4):
            ps = self.next_psum()
            p.mm(ps[:, :], self.ones_f[0:1, :], hgrow[0:1, i * 512:(i + 1) * 512], True, True, [self.ones_f.res, hgrow.res],
                 [ps.res], inc=True)
            p.copy('dve', hgB[:, i * 512:(i + 1) * 512], ps[:, :], [ps.res], [hgB.res])

        def tcol(ck):
            return C0 + 128 * ck if ck < 2 else L0 + 128 * (ck - 2)

        for g in range(9):
            cks = list(range(g * 4, min(34, g * 4 + 4)))
            ps = self.next_psum()
            for i, ck in enumerate(cks):
                p.tr(ps[:, i * 32:(i + 1) * 32], gT[0:32, tcol(ck):tcol(ck) + 128], self.ident_f[0:32, 0:32],
                     [gT.res, self.ident_f.res], [ps.res], inc=(i == len(cks) - 1))
            n = len(cks)
            p.copy('dve', Gt[:, g * 4:g * 4 + n, :], ps[:, 0:n * 32].rearrange('p (a b) -> p a b', a=n), [ps.res], [Gt.res])
        p.act(NL[:], Gt[:], AF.Exp, [Gt.res], [NL.res], scale=-1.0)
        p.act(NL[:], NL[:], AF.Ln, [NL.res, self.ones_f.res], [NL.res], bias=self.ones_f[:, 0:1], scale=1.0)
        for d in range(2):
            p.ts('dve', LF[d][:], NL[:, :, 16 * d + 8:16 * d + 16], -1.0, None, ALU.mult, None, [NL.res], [LF[d].res])
        for d in range(2):
            lf2 = LF[d].ap.rearrange('p a b -> p (a b)')
            psB = self.next_psum()
            psT = self.next_psum()
            p.mm(psB[:, 0:272], tri[:, d, :], lf2, True, True, [tri.res, LF[d].res], [psB.res], inc=True)
            p.mm(psT[:, 0:272], self.ones_f[:], lf2, True, True, [self.ones_f.res, LF[d].res], [psT.res], inc=True)
            w2 = Wd[d].ap.rearrange('p a b -> p (a b)')
            p.tt('dve', Wd[d][:], Gt[:, :, 16 * d:16 * d + 8], psB[:, 0:272].rearrange('p (a b) -> p a b', a=34), ALU.subtract,
                 [Gt.res, psB.res], [Wd[d].res])
            p.act(w2, w2, AF.Exp, [Wd[d].res], [Wd[d].res])
            p.act(Ed[d].ap.rearrange('p a b -> p (a b)'), psB[:, 0:272], AF.Exp, [psB.res], [Ed[d].res])
            p.act(ELd[d].ap.rearrange('p a b -> p (a b)'), psT[:, 0:272], AF.Exp, [psT.res], [ELd[d].res])
        for v4 in Vta:
            p.memset('dve', v4[:], 1.0, [v4.res])
        p.barrier()
        order = [list(range(34)), [1, 0] + list(range(33, 1, -1))]
        vi = 0
        for h in range(8):
            p.dma('sp', qT[:], S1[h * 128:(h + 1) * 128, :], [S1r], [qT.res])
            p.dma('sp', kT[:], S2[h * 128:(h + 1) * 128, :], [S2r], [kT.res])
            p.dma('sp', vT[:], S3[2 * h * 128:(2 * h + 2) * 128, :].rearrange('(j p) t -> p j t', p=128), [S3r], [vT.res])
            p.dma('sp', so[:], S4[2 * h * 128:(2 * h + 2) * 128, :].rearrange('(j p) t -> p j t', p=128), [S4r], [so.res])
            p.memset('dve', Hs[:], 0.0, [Hs.res])
            for d in range(2):
                p.memset('dve', Cs[d][:], 0.0, [Cs[d].res])
                p.memset('dve', Cb[d][:], 0.0, [Cb[d].res])
            for i in range(34):
                for d in range(2):
                    ck = order[d][i]
                    col = tcol(ck)
                    wcol = Wd[d][:, ck, h:h + 1]
                    ecol = Ed[d][:, ck, h:h + 1]
                    elcol = ELd[d][:, ck, h:h + 1]
                    va = Vta[vi % 4]
                    vi += 1
                    tp = self.next_psum()
                    tpv = tp.ap.bitcast(BF16)
                    p.tr(tpv[:, 0:128], kT[:, col:col + 128], self.ident_b[:], [kT.res, self.ident_b.res], [tp.res], inc=False)
                    p.tr(tpv[:, 128:256], vT[:, 0, col:col + 128], self.ident_b[:], [vT.res], [tp.res], inc=False)
                    p.tr(tpv[:, 256:384], vT[:, 1, col:col + 128], self.ident_b[:], [vT.res], [tp.res], inc=True)
                    ktw, ktwv = self.bfstage()
                    p.act(ktwv[:, 0:128], tpv[:, 0:128], AF.Copy, [tp.res, Wd[d].res], [ktw.res], scale=wcol)
                    p.copy('dve', va[:, 0:256], tpv[:, 128:384], [tp.res], [va.res])
                    ps_s = self.next_psum()
                    p.mm(ps_s[:, 0:128], kT[:, col:col + 128], qT[:, col:col + 128], True, True, [kT.res, qT.res],
                         [ps_s.res], inc=True)
                    pt, ptv = self.bfstage()
                    p.stt('dve', ptv[:, 0:128], ps_s[:, 0:128], wcol, tri[:, d, :], ALU.mult, ALU.mult,
                          [ps_s.res, Wd[d].res, tri.res], [pt.res])
                    ps_n = self.next_psum()
                    p.mm(ps_n[:, 0:257], ptv[:, 0:128], va[:, 0:257], True, False, [pt.res, va.res], [ps_n.res], inc=False)
                    p.mm(ps_n[:, 0:257], qT[:, col:col + 128], Cb[d][:, 0:257], False, True, [qT.res, Cb[d].res], [ps_n.res],
                         inc=True)
                    sm = ssb[vi % 8]
                    p.act(sm[:, 0:1], ps_n[:, 256:257], AF.Abs, [ps_n.res], [sm.res])
                    p.recip(sm[:, 1:2], sm[:, 0:1], [sm.res], [sm.res])
                    p.tt('dve', sm[:, 2:3], sm[:, 1:2], ecol, ALU.min, [sm.res, Ed[d].res], [sm.res])
                    p.stt('dve', Hs[:, ck, :], ps_n[:, 0:256], sm[:, 2:3], Hs[:, ck, :], ALU.mult, ALU.add,
                          [ps_n.res, sm.res, Hs.res], [Hs.res])
                    ps_c = self.next_psum()
                    p.mm(ps_c[:, 0:257], ktwv[:, 0:128], va[:, 0:257], True, True, [ktw.res, va.res], [ps_c.res], inc=True)
                    p.tt('dve', Cs[d][:, 0:257], ps_c[:, 0:257], Cs[d][:, 0:257], ALU.add, [ps_c.res, Cs[d].res], [Cs[d].res])
                    p.act(Cb[d][:, 0:257], Cs[d][:, 0:257], AF.Copy, [Cs[d].res, ELd[d].res], [Cb[d].res], scale=elcol)
                    p.ts('pool', Cs[d][:, 0:257], Cs[d][:, 0:257], elcol, None, ALU.mult, None, [Cs[d].res, ELd[d].res], [Cs[d].res])
            for ck in range(34):
                col = tcol(ck)
                sm = ssb[ck % 8]
                junk = self.next_stage()
                p.op('act', lambda e, o_=junk[:, 0:256], i_=Hs[:, ck, :], a_=sm[:, 0:1]: e.activation(
                    out=o_, in_=i_, func=AF.Square, accum_out=a_), [Hs.res], [junk.res, sm.res])
                p.act(sm[:, 1:2], sm[:, 0:1], AF.Sqrt, [sm.res, self.eps_t.res], [sm.res], bias=self.eps_t[:, 0:1], scale=1.0 / 256)
                p.recip(sm[:, 2:3], sm[:, 1:2], [sm.res], [sm.res])
                hn = self.next_stage()
                p.stt('dve', hn[:, 0:256], Hs[:, ck, :], sm[:, 2:3], hgB[:, h * 256:(h + 1) * 256], ALU.mult, ALU.mult,
                      [Hs.res, sm.res, hgB.res], [hn.res])
                ps_t = self.next_psum()
                p.tr(ps_t[:, 0:128], hn[:, 0:128], self.ident_f[:], [hn.res, self.ident_f.res], [ps_t.res], inc=False)
                p.tr(ps_t[:, 128:256], hn[:, 128:256], self.ident_f[:], [hn.res], [ps_t.res], inc=True)
                p.tt('dve', Ob[:, :, col:col + 128], ps_t[:, 0:256].rearrange('p (j t) -> p j t', j=2), so[:, :, col:col + 128],
                     ALU.mult, [ps_t.res, so.res], [Ob.res])
            dst = S5[2 * h * 128:(2 * h + 2) * 128, :].rearrange('(j p) t -> p j t', p=128)
            p.dma('sp', dst[:, :, C0:C1], Ob[:, :, C0:C1], [Ob.res], [S5r])
            p.dma('sp', dst[:, :, L0:L1], Ob[:, :, L0:L1], [Ob.res], [S5r])
        p.barrier()
        self.out_proj(self.ml_w_out, 'S5', last)


    def mixer_na(self, l, last):
        p = self.p
        S1, S1r = self.S['S1'], self.Sres['S1']
        S2, S2r = self.S['S2'], self.Sres['S2']
        S3, S3r = self.S['S3'], self.Sres['S3']
        S4, S4r = self.S['S4'], self.Sres['S4']
        gq = self.aview('nag', self.EXTRA, [128, 2], F32)
        p.dma('sp', gq[:], self.na_g, [self.in_res], [gq.res])
        p.ts('dve', gq[:, 0:1], gq[:, 0:1], 128.0 ** -0.5, None, ALU.mult, None, [gq.res], [gq.res])
        blocks = []
        c = 0
        while c < 48:
            n = min(10, 48 - c)
            blocks.append(dict(loads=[(self.na_w_qkv[:, c * 128:(c + n) * 128], n * 128)],
                               groups=[[(i * 128, 128)] for i in range(n)], c0=c))
            c += n

        def epiA(bi, gi, ti, tile, pss):
            c0, w, s, hl, hr = tile
            cg = blocks[bi]['c0'] + gi
            ps = pss[0]
            if cg < 32:
                which, h = cg // 16, cg % 16
                sq, sqv = self.bfstage()
                p.act(sqv[:, 0:w], ps[:, 0:w], AF.Square, [ps.res], [sq.res])
                ps2 = self.next_psum()
                p.mm(ps2[:, 0:w], self.ones_b[:], sqv[:, 0:w], True, True, [sq.res, self.ones_b.res], [ps2.res], inc=True)
                rs = self.next_stage()
                p.act(rs[:, 0:w], ps2[:, 0:w], AF.Sqrt, [ps2.res, self.eps_t.res], [rs.res], bias=self.eps_t[:, 0:1],
                      scale=1.0 / 128)
                p.recip(rs[:, 0:w], rs[:, 0:w], [rs.res], [rs.res])
                ob, obv = self.bfstage()
                p.stt('dve', obv[:, 0:w], ps[:, 0:w], gq[:, which:which + 1], rs[:, 0:w], ALU.mult, ALU.mult,
                      [ps.res, gq.res, rs.res], [ob.res])
                dst, dr = (S1, S1r) if which == 0 else (S2, S2r)
                p.dma('sp', dst[h * 128:(h + 1) * 128, c0:c0 + w], obv[:, 0:w], [ob.res], [dr])
            else:
                cc = cg - 32
                ob, obv = self.bfstage()
                p.act(obv[:, 0:w], ps[:, 0:w], AF.Copy, [ps.res], [ob.res])
                p.dma('sp', S3[cc * 128:(cc + 1) * 128, c0:c0 + w], obv[:, 0:w], [ob.res], [S3r])

        self.linear(self.hT, self.hT_res, KC, blocks, self.plain(), epiA, defer=True)
        p.barrier()
        SEQ = TP * 2
        qT = [self.aview('naq%d' % i, i * SEQ, [128, TP], BF16) for i in range(2)]
        kT = [self.aview('nak%d' % i, (2 + i) * SEQ, [128, TP], BF16) for i in range(2)]
        vT = [self.aview('nav%d' % i, (4 + i) * SEQ, [128, TP], BF16) for i in range(2)]
        Ob = [self.aview('nao%d' % i, (6 + i) * SEQ, [128, TP], BF16) for i in range(2)]
        o2 = 8 * SEQ
        Vt = [self.aview('naVt%d' % i, o2 + i * 8704, [128, 34, 128], BF16) for i in range(2)]
        o3 = o2 + 2 * 8704
        tab = [self.aview('natab%d' % i, o3 + i * 8192, [128, 2, 16, 64], F32) for i in range(2)]
        Sb = self.psum[0:4]
        Ob_ps = self.psum[4:6]
        Db_ps = self.psum[6:8]

        def tcol(tk):
            return C0 + 128 * tk if tk < 2 else L0 + 128 * (tk - 2)

        def load_head(h):
            p.dma('pool', qT[h % 2][:], S1[h * 128:(h + 1) * 128, :], [S1r], [qT[h % 2].res])
            p.dma('pool', kT[h % 2][:], S2[h * 128:(h + 1) * 128, :], [S2r], [kT[h % 2].res])
            p.dma('pool', vT[h % 2][:], S3[h * 128:(h + 1) * 128, :], [S3r], [vT[h % 2].res])
            p.dma('pool', tab[h % 2][:], self.na_tab[h].rearrange('a p u c -> p a u c'), [self.in_res], [tab[h % 2].res])

        load_head(0)
        for h in range(16):
            q, k, v, O, V, tb = qT[h % 2], kT[h % 2], vT[h % 2], Ob[h % 2], Vt[h % 2], tab[h % 2]
            if h + 1 < 16:
                load_head(h + 1)
            for g in range(9):
                tks = list(range(g * 4, min(34, g * 4 + 4)))
                ps = Sb[g % 4]
                psv = ps.ap.bitcast(BF16)
                for i, tk in enumerate(tks):
                    p.tr(psv[:, i * 128:(i + 1) * 128], v[:, tcol(tk):tcol(tk) + 128], self.ident_b[:],
                         [v.res, self.ident_b.res], [ps.res], inc=(i == len(tks) - 1))
                n = len(tks)
                p.copy('dve' if g % 2 else 'act_copy', V[:, g * 4:g * 4 + n, :],
                       psv[:, 0:n * 128].rearrange('p (a b) -> p a b', a=n), [ps.res], [V.res])
            units = []
            qts = ([] if last else [(C0, None)]) + [(L0 + 256 * j, j) for j in range(16)]
            for qi, (qc0, j) in enumerate(qts):
                keys = []
                if j is not None:
                    if j == 0:
                        kts, ti_ = range(0, 4), 1
                    elif j == 15:
                        kts, ti_ = range(28, 32), 1
                    else:
                        kts, ti_ = range(2 * j - 2, 2 * j + 4), 0
                    for kt in kts:
                        keys.append((2 + kt, (ti_, 7 - 2 * kt + 4 * j)))
                keys += [(0, None), (1, None)]
                for ki, (tk, bias) in enumerate(keys):
                    units.append(dict(qi=qi, qc0=qc0, tk=tk, bias=bias, first=(ki == 0), last=(ki == len(keys) - 1)))

            def emit_S(i, u):
                ps = Sb[i % 4]
                kc_ = tcol(u['tk'])
                p.mm(ps[:, 0:256], k[:, kc_:kc_ + 128], q[:, u['qc0']:u['qc0'] + 256], True, True, [k.res, q.res], [ps.res],
                     inc=True)
                pt, ptv = self.bfstage()
                if u['bias'] is not None:
                    ti_, u0 = u['bias']
                    tmp = self.next_stage()
                    p.tt('dve', tmp.ap[:, 0:256].rearrange('p (a b) -> p a b', a=4),
                         ps.ap[:, 0:256].rearrange('p (a b) -> p a b', a=4), tb[:, ti_, u0:u0 + 4, :], ALU.add,
                         [ps.res, tb.res], [tmp.res])
                    p.act(ptv[:, 0:256], tmp[:, 0:256], AF.Exp, [tmp.res], [pt.res])
                else:
                    p.act(ptv[:, 0:256], ps[:, 0:256], AF.Exp, [ps.res], [pt.res])
                u['pt'] = (pt, ptv)

            def emit_PV(u):
                pt, ptv = u['pt']
                ob_ = Ob_ps[u['qi'] % 2]
                db_ = Db_ps[u['qi'] % 2]
                p.mm(ob_[:, 0:256], V[:, u['tk'], :], ptv[:, 0:256], u['first'], u['last'], [V.res, pt.res], [ob_.res],
                     inc=u['last'])
                p.mm(db_[:, 0:256], self.ones_b[:], ptv[:, 0:256], u['first'], u['last'], [self.ones_b.res, pt.res],
                     [db_.res], inc=True)
                if u['last']:
                    rec = self.next_stage()
                    p.recip(rec[:, 0:256], db_[:, 0:256], [db_.res], [rec.res])
                    p.tt('dve', O[:, u['qc0']:u['qc0'] + 256], ob_[:, 0:256], rec[:, 0:256], ALU.mult,
                         [ob_.res, rec.res], [O.res])

            pend = []
            for i, u in enumerate(units):
                emit_S(i, u)
                pend.append(u)
                if len(pend) > 2:
                    emit_PV(pend.pop(0))
            while pend:
                emit_PV(pend.pop(0))
            if not last:
                p.dma('sp', S4[h * 128:(h + 1) * 128, C0:C1], O[:, C0:C1], [O.res], [S4r])
            p.dma('sp', S4[h * 128:(h + 1) * 128, L0:L1], O[:, L0:L1], [O.res], [S4r])
        p.barrier()
        self.out_proj(self.na_w_o, 'S4', last)


LRU_C = 8.0
NEG = -30000.0


def rope_tables():
    t = np.arange(NLAT)
    freqs = (10000.0 ** (-np.arange(0, 64, 2, dtype=np.float32) / np.float32(64))).astype(np.float32)
    d = np.arange(128)
    pos = np.where(d[:, None] < 64, (t // 64)[None, :], (t % 64)[None, :]).astype(np.float32)
    ang = (pos * freqs[d % 32][:, None]).astype(np.float32)
    cos = np.ones((128, TP), np.float32)
    sin = np.zeros((128, TP), np.float32)
    cos[:, L0:L1] = np.cos(ang)
    sgn = np.where((d % 64) < 32, -1.0, 1.0).astype(np.float32)
    sin[:, L0:L1] = np.sin(ang) * sgn[:, None]
    sc = np.float32(128.0 ** -0.5)
    return np.ascontiguousarray(np.stack([cos * sc, sin * sc, cos, sin]).astype(np.float32))


def na_table(rpb):
    H = rpb.shape[0]
    krl = np.arange(128) // 64
    kc = np.arange(128) % 64
    u = np.arange(16)
    qc = np.arange(64)
    dr = 14 + krl[:, None] - u[None, :]
    dc = np.clip(kc[:, None] - qc[None, :] + 15, 0, 30)
    cstart = np.clip(qc - 8, 0, 48)
    cmask = (kc[:, None] >= cstart[None, :]) & (kc[:, None] < cstart[None, :] + 16)
    drc = np.clip(dr, 0, 14)
    g = rpb[:, drc[:, :, None], dc[:, None, :]]
    out = np.full((H, 2, 128, 16, 64), NEG, np.float32)
    for t, (lo, hi) in enumerate(((3, 10), (0, 14))):
        valid = ((dr >= lo) & (dr <= hi))[:, :, None] & cmask[:, None, :]
        out[:, t] = np.where(valid[None], g, np.float32(NEG))
    return out


def prep_mixers(inp, b, layers=(0, 1, 2, 3)):
    m = {}
    if 0 in layers:
        w_in = inp['ml_w_in'][0]
        d = np.arange(128)
        perm = np.where((d % 64) < 32, d + 32, d - 32)
        cols = (np.arange(16)[:, None] * 128 + perm[None, :]).reshape(-1)
        m['ml_w_in'] = w_in
        m['ml_w_qkp'] = np.ascontiguousarray(w_in[:, cols])
        m['ml_w_gate'] = inp['ml_w_gate'][0]
        m['ml_bg'] = np.ascontiguousarray(inp['ml_b_gate'][0].reshape(32, 1))
        m['ml_hg'] = np.ascontiguousarray(inp['ml_head_g'][0].reshape(1, 2048))
        m['ml_w_out'] = inp['ml_w_out'][0]
        m['rope'] = rope_tables()
        m['tri'] = np.stack([np.triu(np.ones((128, 128), np.float32)), np.tril(np.ones((128, 128), np.float32))])
    if 1 in layers:
        m['na_w_qkv'] = inp['na_w_qkv'][0]
        m['na_g'] = np.ascontiguousarray(np.stack([inp['na_q_g'][0], inp['na_k_g'][0]], axis=1).astype(np.float32))
        m['na_tab'] = na_table(inp['na_rpb'][0])
        m['na_w_o'] = inp['na_w_o'][0]
    if 2 in layers:
        m['cv_w_pw1'] = inp['cv_w_pw1'][0]
        m['cv_dwT'] = np.ascontiguousarray(np.transpose(colT(inp['cv_dw'][0]), (0, 2, 1)))
        m['cv_vT'] = colT(np.stack([inp['cv_dw_b'][0], inp['cv_ln_g'][0], inp['cv_ln_b'][0]]))
        m['cv_w_pw2'] = inp['cv_w_pw2'][0]
    if 3 in layers:
        m['lr_w_in'] = inp['lr_w_in'][0]
        m['lr_cvT'] = colT(np.concatenate([inp['lr_conv'][0], inp['lr_conv_b']], axis=0))
        m['lr_w_gate'] = inp['lr_w_gate'][0]
        m['lr_bgT'] = colT(inp['lr_b_gate'][0])
        m['lr_lamT'] = colT(inp['lr_lambda'][0])
        m['lr_w_out'] = inp['lr_w_out'][0]
    return m
```

```python
import contextlib
import numpy as np
import concourse.bass as bass
import concourse.mybir as mybir
from concourse.bass_utils import run_bass_kernel_spmd

F32 = mybir.dt.float32
BF16 = mybir.dt.bfloat16
ALU = mybir.AluOpType
AF = mybir.ActivationFunctionType

D = 2048
KC = 16
DFF = 5632
FC = 44
PAD = 16
NCTX = 256
NLAT = 4096
C0 = PAD
C1 = C0 + NCTX
L0 = C1 + 2 * PAD
L1 = L0 + NLAT
TP = L1 + PAD
EPS = 1e-6
DEPTH = 4


class Res:
    __slots__ = ('name', 'w', 'r', 'multi')

    def __init__(self, name, multi=False):
        self.name = name
        self.w = {}
        self.r = {}
        self.multi = multi


class DSem:
    __slots__ = ('sem', 'val')

    def __init__(self, sem):
        self.sem = sem
        self.val = 0


class Buf:
    def __init__(self, ap, name):
        self.ap = ap
        self.res = Res(name)

    def __getitem__(self, idx):
        return self.ap[idx]


class Prog:
    QS = ('pe', 'dve', 'act', 'pool', 'sp')

    def __init__(self, nc, stack):
        self.nc = nc
        self.stack = stack
        self.q = {k: [] for k in self.QS}
        self.csem = {k: stack.enter_context(nc.semaphore('c_' + k)) for k in self.QS}
        self.cnt = {k: 0 for k in self.QS}
        self.pending = {k: False for k in self.QS}
        self.seen = {k: {} for k in self.QS}
        self.rings = {}
        self.ringpos = {}
        for q, n in (('sp', 40), ('pool', 16), ('act', 8)):
            self.rings[q] = [DSem(stack.enter_context(nc.semaphore('d_%s%d' % (q, i)))) for i in range(n)]
            self.ringpos[q] = 0
        self.ninstr = 0

    def sb(self, name, shape, dtype):
        return Buf(self.stack.enter_context(self.nc.sbuf_tensor(name, shape, dtype))[:], name)

    def ps(self, name, shape, dtype):
        return Buf(self.stack.enter_context(self.nc.psum_tensor(name, shape, dtype))[:], name)

    def op(self, q, fn, reads=(), writes=(), inc=True, dsem=None):
        deps = {}

        def add(d):
            for k, sv in d.items():
                if k not in deps or deps[k][1] < sv[1]:
                    deps[k] = sv

        for r in reads:
            add(r.w)
        for w in writes:
            add(w.w)
            add(w.r)
        if dsem is not None and dsem.val > 0:
            add({id(dsem.sem): (dsem.sem, dsem.val)})
        own = id(self.csem[q])
        if q == 'pe' and own in deps:
            del deps[own]
        seen = self.seen[q]
        waits = []
        for k, (s, v) in deps.items():
            if seen.get(k, 0) >= v:
                continue
            seen[k] = v
            waits.append((s, v))
        if dsem is not None:
            dsem.val += 16
            tick = (dsem.sem, dsem.val)
            incinfo = (dsem.sem, 16)
        else:
            tick = (self.csem[q], self.cnt[q] + 1)
            if inc:
                self.cnt[q] += 1
                incinfo = (self.csem[q], 1)
                self.pending[q] = False
            else:
                incinfo = None
                self.pending[q] = True
        k = id(tick[0])
        for r in reads:
            if k not in r.r or r.r[k][1] < tick[1]:
                r.r[k] = tick
        for w in writes:
            if w.multi:
                if k not in w.w or w.w[k][1] < tick[1]:
                    w.w[k] = tick
            else:
                w.w = {k: tick}
                w.r = {}
        self.q[q].append((waits, fn, incinfo))
        self.ninstr += 1

    def dma(self, q, out, in_, reads=(), writes=(), **kw):
        ring = self.rings[q]
        ds = ring[self.ringpos[q] % len(ring)]
        self.ringpos[q] += 1
        self.op(q, lambda e: e.dma_start(out=out, in_=in_, **kw), reads, writes, dsem=ds)

    def barrier(self):
        for q in self.QS:
            assert not self.pending[q], q
        targets = [(self.csem[k], self.cnt[k]) for k in self.QS if self.cnt[k] > 0]
        for ring in self.rings.values():
            targets += [(d.sem, d.val) for d in ring if d.val > 0]
        for q in self.QS:
            seen = self.seen[q]
            waits = []
            for (s, v) in targets:
                if q == 'pe' and s is self.csem['pe']:
                    continue
                if seen.get(id(s), 0) >= v:
                    continue
                seen[id(s)] = v
                waits.append((s, v))
            if waits:
                self.q[q].append((waits, None, None))

    def emit(self):
        nc = self.nc
        names = {'pe': 'tensor', 'dve': 'vector', 'act': 'scalar', 'pool': 'gpsimd', 'sp': 'sync'}
        with nc.Block() as block:
            for k in self.QS:
                lst = self.q[k]

                def body(eng, lst=lst):
                    for waits, fn, incinfo in lst:
                        for (s, v) in waits:
                            eng.wait_ge(s, v)
                        if fn is None:
                            continue
                        ins = fn(eng)
                        if incinfo is not None:
                            ins.then_inc(incinfo[0], incinfo[1])

                getattr(block, names[k])(body)

    def mm(self, out, lhsT, rhs, start, stop, reads, writes, inc):
        self.op('pe', lambda e: e.matmul(out, lhsT=lhsT, rhs=rhs, start=start, stop=stop), reads, writes, inc=inc)

    def tr(self, out, in_, ident, reads, writes, inc=True):
        self.op('pe', lambda e: e.transpose(out, in_, ident), reads, writes, inc=inc)

    def act(self, out, in_, func, reads, writes, **kw):
        self.op('act', lambda e: e.activation(out=out, in_=in_, func=func, **kw), reads, writes)

    def tt(self, q, out, in0, in1, op, reads, writes):
        self.op(q, lambda e: e.tensor_tensor(out=out, in0=in0, in1=in1, op=op), reads, writes)

    def ts(self, q, out, in0, s1, s2, op0, op1, reads, writes):
        if op1 is None:
            self.op(q, lambda e: e.tensor_scalar(out=out, in0=in0, scalar1=s1, scalar2=None, op0=op0), reads, writes)
        else:
            self.op(q, lambda e: e.tensor_scalar(out=out, in0=in0, scalar1=s1, scalar2=s2, op0=op0, op1=op1), reads, writes)

    def stt(self, q, out, in0, scalar, in1, op0, op1, reads, writes):
        self.op(q, lambda e: e.scalar_tensor_tensor(out=out, in0=in0, scalar=scalar, in1=in1, op0=op0, op1=op1), reads, writes)

    def copy(self, q, out, in_, reads, writes):
        if q == 'act_copy':
            self.op('act', lambda e: e.activation(out=out, in_=in_, func=AF.Copy), reads, writes)
        else:
            self.op(q, lambda e: e.tensor_copy(out=out, in_=in_), reads, writes)

    def memset(self, q, ap, val, writes):
        self.op(q, lambda e: e.memset(ap, val), (), writes)

    def recip(self, out, in_, reads, writes):
        self.op('dve', lambda e: e.reciprocal(out=out, in_=in_), reads, writes)


def colT(v):
    v = np.asarray(v, np.float32)
    F = v.shape[-1]
    r = v.reshape(v.shape[:-1] + (F // 128, 128))
    r = np.moveaxis(r, -1, 0)
    return np.ascontiguousarray(r)


def tiles_plain():
    t = [(C0, NCTX, 1)]
    for i in range(8):
        t.append((L0 + 512 * i, 512, 0))
    return t


def tiles_n(n, lat_only=False):
    t = [] if lat_only else [(C0, NCTX, 1)]
    s = 0
    while s < NLAT:
        w = min(n, NLAT - s)
        t.append((L0 + s, w, 0))
        s += w
    return t


class MK:
    def __init__(self, layers=(0, 1, 2, 3), debug=None, skip_mixer=False, last_layer=3):
        self.layers = layers
        self.debug = debug
        self.skip_mixer = skip_mixer
        self.last_layer = last_layer
        self.nc = bass.Bass("TRN2", target_bir_lowering=False)
        self.dr = {}

    def din(self, name, shape, dtype=F32):
        t = self.nc.dram_tensor(name, list(shape), dtype, kind="ExternalInput")
        self.dr[name] = t.ap()
        return self.dr[name]

    def dscr(self, name, shape, dtype):
        t = self.nc.dram_tensor(name, list(shape), dtype)
        a = t.ap()
        a_res = Res(name, multi=True)
        return a, a_res

    def build(self):
        nc = self.nc
        with contextlib.ExitStack() as stack:
            self.p = p = Prog(nc, stack)
            self.declare_io()
            self.alloc(stack)
            self.prologue()
            for l in self.layers:
                self.layer(l)
            self.epilogue_out()
            p.emit()
        return nc

    def declare_io(self):
        nc = self.nc
        self.xT_in = self.din('xT', [D, TP])
        self.cT = self.din('cT', [128, KC, 2])
        self.ada_w = self.din('ada_w', [DEPTH, D, 6 * D])
        self.abT = self.din('abT', [128, DEPTH, 96])
        self.ngT = self.din('ngT', [128, DEPTH, 2, KC])
        self.ffn_w_gu = self.din('ffn_w_gu', [DEPTH, D, 2 * DFF])
        self.ffn_cwT = self.din('ffn_cwT', [128, DEPTH, 3, FC])
        self.ffn_w_down = self.din('ffn_w_down', [DEPTH, DFF, D])
        self.in_res = Res('inputs', multi=True)
        self.ident_in = self.din('ident', [128, 128])
        self.declare_mixer_io()
        out = nc.dram_tensor('outT', [D, NLAT], F32, kind="ExternalOutput")
        self.outT = out.ap()
        self.out_res = Res('outT', multi=True)
        self.xT, self.xT_res = self.dscr('xT_s', [D, TP], F32)
        self.hT, self.hT_res = self.dscr('hT_s', [D, TP], BF16)
        self.HID, self.HID_res = self.dscr('hid_s', [10, 128, FC, 456], BF16)

    def declare_mixer_io(self):
        pass

    def alloc(self, stack):
        p = self.p
        self.ARENA = 172 * 1024
        self.arena = p.sb('arena', [128, self.ARENA // 4], F32)
        self.stage = [p.sb('stg%d' % i, [128, 512], F32) for i in range(8)]
        self.stage_i = 0
        self.psum = [p.ps('ps%d' % i, [128, 512], F32) for i in range(8)]
        self.psum_i = 0
        self.ident_f = p.sb('ident_f', [128, 128], F32)
        self.ident_b = p.sb('ident_b', [128, 128], BF16)
        self.ones_b = p.sb('ones_b', [128, 128], BF16)
        self.ones_f = p.sb('ones_f', [128, 128], F32)
        self.eps_t = p.sb('eps_t', [128, 1], F32)
        self.modT = p.sb('modT', [128, 96, 2], F32)
        self.lv = p.sb('lv', [128, 6, KC, 2], F32)
        self.ngs = p.sb('ngs', [128, DEPTH, 2, KC], F32)
        self.abs_ = p.sb('abs', [128, DEPTH, 96], F32)
        self.cws = p.sb('cws', [128, DEPTH, 3, FC], F32)
        self.scT = p.sb('scT', [128, KC, 2], F32)
        self.W = [self.aview('W%d' % i, i * 45056, [128, 22528], BF16) for i in range(2)]
        self.X = [self.aview('X%d' % i, 90112 + i * 40960, [128, 20480], BF16) for i in range(2)]
        self.WW = self.aview('WW', 0, [128, 45056], BF16)
        self.w_i = 0
        self.x_i = 0
        self.EXTRA = 90112 + 2 * 40960

    def aview(self, name, off, shape, dtype):
        nbytes = int(np.prod(shape[1:])) * (4 if dtype == F32 else 2)
        assert off % 4 == 0 and nbytes % 4 == 0 and off + nbytes <= self.ARENA, (name, off, nbytes)
        ap = self.arena.ap[:, off // 4:(off + nbytes) // 4]
        if dtype != F32:
            ap = ap.bitcast(dtype)
        if len(shape) == 3:
            ap = ap.rearrange('p (a b) -> p a b', a=shape[1])
        elif len(shape) == 4:
            ap = ap.rearrange('p (a b c) -> p a b c', a=shape[1], b=shape[2])
        return Buf(ap, name)

    def next_stage(self):
        b = self.stage[self.stage_i % len(self.stage)]
        self.stage_i += 1
        return b

    def next_psum(self):
        b = self.psum[self.psum_i % 8]
        self.psum_i += 1
        return b

    def prologue(self):
        p = self.p
        nc = self.nc
        p.memset('dve', self.ones_b[:], 1.0, [self.ones_b.res])
        p.memset('dve', self.ones_f[:], 1.0, [self.ones_f.res])
        p.memset('dve', self.eps_t[:], EPS, [self.eps_t.res])
        p.dma('sp', self.ident_f[:], self.ident_in, [self.in_res], [self.ident_f.res])
        p.copy('dve', self.ident_b[:], self.ident_f[:], [self.ident_f.res], [self.ident_b.res])
        p.dma('sp', self.ngs[:], self.ngT, [self.in_res], [self.ngs.res])
        p.dma('sp', self.abs_[:], self.abT, [self.in_res], [self.abs_.res])
        p.dma('sp', self.cws[:], self.ffn_cwT, [self.in_res], [self.cws.res])
        p.dma('sp', self.scT[:], self.cT, [self.in_res], [self.scT.res])
        p.act(self.scT[:], self.scT[:], AF.Silu, [self.scT.res], [self.scT.res])
        xin = self.xT_in.rearrange('(kc p) t -> p kc t', p=128)
        xs = self.xT.rearrange('(kc p) t -> p kc t', p=128)
        nchunk = 8
        wch = TP // nchunk
        tb = [self.aview('cp%d' % i, i * 36864, [128, KC, wch], F32) for i in range(2)]
        assert KC * wch * 4 <= 36864
        for i in range(nchunk):
            b = tb[i % 2]
            p.dma('sp', b[:], xin[:, :, i * wch:(i + 1) * wch], [self.in_res], [b.res])
            p.dma('sp', xs[:, :, i * wch:(i + 1) * wch], b[:], [b.res], [self.xT_res])
        zt = self.aview('zt', 80000, [128, KC, 2 * PAD], BF16)
        p.memset('dve', zt[:], 0.0, [zt.res])
        hs = self.hT.rearrange('(kc p) t -> p kc t', p=128)
        p.dma('sp', hs[:, :, 0:PAD], zt[:, :, 0:PAD], [zt.res], [self.hT_res])
        p.dma('sp', hs[:, :, C1:L0], zt[:, :, :], [zt.res], [self.hT_res])
        p.dma('sp', hs[:, :, L1:TP], zt[:, :, 0:PAD], [zt.res], [self.hT_res])
        p.barrier()

    def mods(self, l):
        p = self.p
        p.barrier()
        wb = [self.aview('aw%d' % i, i * 32768, [128, KC, 512], F32) for i in range(2)]
        ps = self.next_psum()
        psv = ps.ap[:, 0:192].rearrange('p (j s) -> p j s', s=2)
        for blk in range(24):
            b = wb[blk % 2]
            src = self.ada_w[l, :, blk * 512:(blk + 1) * 512].rearrange('(kc p) n -> p kc n', p=128)
            p.dma('sp', b[:], src, [self.in_res], [b.res])
            for jj in range(4):
                j = blk * 4 + jj
                for kc in range(KC):
                    p.mm(psv[:, j, :], b[:, kc, jj * 128:(jj + 1) * 128], self.scT[:, kc, :], kc == 0, kc == KC - 1,
                         [b.res, self.scT.res], [ps.res], inc=(kc == KC - 1))
        for s in range(2):
            p.tt('dve', self.modT[:, :, s], psv[:, :, s], self.abs_[:, l, :], ALU.add,
                 [ps.res, self.abs_.res], [self.modT.res])
        m = self.modT
        lv = self.lv
        for half in range(2):
            base = half * 48
            for s in range(2):
                p.stt('dve', lv[:, half * 3 + 0, :, s], m[:, base + 16:base + 32, s], 1.0, self.ngs[:, l, half, :],
                      ALU.add, ALU.mult, [m.res, self.ngs.res], [lv.res])
                p.copy('dve', lv[:, half * 3 + 1, :, s], m[:, base:base + 16, s], [m.res], [lv.res])
                p.copy('dve', lv[:, half * 3 + 2, :, s], m[:, base + 32:base + 48, s], [m.res], [lv.res])
        p.barrier()

    def norm(self, l, half, lat_only=False):
        p = self.p
        p.barrier()
        xin = [self.aview('nx%d' % i, i * 32768, [128, KC, 512], F32) for i in range(2)]
        sq = self.aview('nsq', 65536, [128, KC, 512], BF16)
        ob = [self.aview('nob%d' % i, 81920 + i * 16384, [128, KC, 512], BF16) for i in range(2)]
        rs = self.aview('nrs', 114688, [128, 512], F32)
        tmp = [self.aview('ntmp%d' % i, 116736 + i * 2048, [128, 512], F32) for i in range(4)]
        xs = self.xT.rearrange('(kc p) t -> p kc t', p=128)
        hs = self.hT.rearrange('(kc p) t -> p kc t', p=128)
        lv = self.lv
        tl = tiles_plain()
        if lat_only:
            tl = tl[1:]
        for ti, (c0, w, s) in enumerate(tl):
            xb = xin[ti % 2]
            o = ob[ti % 2]
            p.dma('sp', xb[:, :, 0:w], xs[:, :, c0:c0 + w], [self.xT_res], [xb.res])
            p.act(sq[:, :, 0:w], xb[:, :, 0:w], AF.Square, [xb.res], [sq.res])
            ps = self.next_psum()
            for kc in range(KC):
                p.mm(ps[:, 0:w], self.ones_b[:], sq[:, kc, 0:w], kc == 0, kc == KC - 1, [sq.res, self.ones_b.res],
                     [ps.res], inc=(kc == KC - 1))
            p.act(rs[:, 0:w], ps[:, 0:w], AF.Sqrt, [ps.res, self.eps_t.res], [rs.res], bias=self.eps_t[:, 0:1],
                  scale=1.0 / D)
            p.recip(rs[:, 0:w], rs[:, 0:w], [rs.res], [rs.res])
            for kc in range(KC):
                t = tmp[kc % 4]
                p.tt('dve', t[:, 0:w], xb[:, kc, 0:w], rs[:, 0:w], ALU.mult, [xb.res, rs.res], [t.res])
                p.act(o[:, kc, 0:w], t[:, 0:w], AF.Identity, [t.res, lv.res], [o.res],
                      scale=lv[:, half * 3 + 0, kc, s:s + 1], bias=lv[:, half * 3 + 1, kc, s:s + 1])
            p.dma('sp', hs[:, :, c0:c0 + w], o[:, :, 0:w], [o.res], [self.hT_res])
        p.barrier()

    def linear(self, xsrc, xres, Kc, wblocks, tiles, epi, xload=None, defer=False, wide=False):
        p = self.p
        pend = None
        wbufs = {}
        xbufs = {}

        def load_w(bi):
            blk = wblocks[bi]
            ncols = sum(n for _, n in blk['loads'])
            if wide:
                assert Kc * ncols <= 45056, (Kc, ncols)
                wres = [self.W[0].res, self.W[1].res]
                wv = self.WW.ap[:, 0:Kc * ncols].rearrange('p (kc n) -> p kc n', kc=Kc)
            else:
                wb = self.W[self.w_i % 2]
                self.w_i += 1
                assert Kc * ncols <= 22528, (Kc, ncols)
                wres = [wb.res]
                wv = wb.ap[:, 0:Kc * ncols].rearrange('p (kc n) -> p kc n', kc=Kc)
            off = 0
            for (src, n) in blk['loads']:
                p.dma('pool', wv[:, :, off:off + n], src.rearrange('(kc p) n -> p kc n', p=128),
                      [self.in_res], wres)
                off += n
            wbufs[bi] = (wres, wv)

        def load_x(bi, ti):
            tile = tiles[ti]
            c0, w, s, hl, hr = tile
            wt = w + hl + hr
            xb = self.X[self.x_i % 2]
            self.x_i += 1
            assert Kc * wt <= 20480
            xv = xb.ap[:, 0:Kc * wt].rearrange('p (kc t) -> p kc t', kc=Kc)
            if xload is not None:
                xload(ti, tile, xv, xb)
            else:
                p.dma('pool', xv, xsrc[:, c0 - hl:c0 + w + hr].rearrange('(kc p) t -> p kc t', p=128),
                      [xres], [xb.res])
            xbufs[(bi, ti)] = (xb, xv)

        its = [(bi, ti) for bi in range(len(wblocks)) for ti in range(len(tiles))]
        load_w(0)
        load_x(*its[0])
        for n, (bi, ti) in enumerate(its):
            if n + 1 < len(its):
                load_x(*its[n + 1])
            if bi + 1 < len(wblocks) and not wide and ti == 0:
                load_w(bi + 1)
            blk = wblocks[bi]
            wres, wv = wbufs[bi]
            xb, xv = xbufs.pop((bi, ti))
            tile = tiles[ti]
            c0, w, s, hl, hr = tile
            wt = w + hl + hr
            for gi, grp in enumerate(blk['groups']):
                pss = []
                for (co, cw) in grp:
                    ps = self.next_psum()
                    for kc in range(Kc):
                        p.mm(ps[0:cw, 0:wt], wv[:, kc, co:co + cw], xv[:, kc, :], kc == 0, kc == Kc - 1,
                             wres + [xb.res], [ps.res], inc=(kc == Kc - 1))
                    pss.append(ps)
                if defer:
                    if pend is not None:
                        epi(*pend)
                    pend = (bi, gi, ti, tile, pss)
                else:
                    epi(bi, gi, ti, tile, pss)
            if wide and bi + 1 < len(wblocks) and ti == len(tiles) - 1:
                load_w(bi + 1)
        if pend is not None:
            epi(*pend)

    def epi_residual(self, gate_kind, chunk_of, col_map=None):
        p = self.p
        xs = self.xT

        def epi(bi, gi, ti, tile, pss):
            c0, w, s, hl, hr = tile
            dc = chunk_of(bi, gi)
            ps = pss[0]
            st = self.next_stage()
            p.dma('pool', st[:, 0:w], xs[dc * 128:(dc + 1) * 128, c0:c0 + w], [self.xT_res], [st.res])
            p.stt('dve', st[:, 0:w], ps[:, hl:hl + w], self.lv[:, gate_kind, dc, s:s + 1], st[:, 0:w], ALU.mult, ALU.add,
                  [ps.res, st.res, self.lv.res], [st.res])
            p.dma('sp', xs[dc * 128:(dc + 1) * 128, c0:c0 + w], st[:, 0:w], [st.res], [self.xT_res])

        return epi

    def ffn(self, l, lat_only=False):
        p = self.p
        base = tiles_n(456, lat_only)
        tiles = [(c0, w, s, 1, 1) for (c0, w, s) in base]
        toff = 1 if lat_only else 0
        wblocks = []
        f = 0
        while f < FC:
            nf = 1 if f == 0 else min(5, FC - f)
            wblocks.append(dict(
                loads=[(self.ffn_w_gu[l, :, f * 128:(f + nf) * 128], nf * 128),
                       (self.ffn_w_gu[l, :, DFF + f * 128:DFF + (f + nf) * 128], nf * 128)],
                groups=[[(j * 128, 128), (nf * 128 + j * 128, 128)] for j in range(nf)], f0=f))
            f += nf
        cws = self.cws

        def epi_up(bi, gi, ti, tile, pss):
            c0, w, s, hl, hr = tile
            fch = wblocks[bi]['f0'] + gi
            pg, pu = pss
            t1 = self.next_stage()
            sg = self.next_stage()
            hb = self.next_stage()
            hbv = hb.ap.bitcast(BF16)
            p.ts('dve', t1[:, 0:w], pg[:, 1:w + 1], cws[:, l, 1, fch:fch + 1], None, ALU.mult, None,
                 [pg.res, cws.res], [t1.res])
            p.stt('dve', t1[:, 0:w], pg[:, 0:w], cws[:, l, 0, fch:fch + 1], t1[:, 0:w], ALU.mult, ALU.add,
                  [pg.res, cws.res, t1.res], [t1.res])
            p.stt('dve', t1[:, 0:w], pg[:, 2:w + 2], cws[:, l, 2, fch:fch + 1], t1[:, 0:w], ALU.mult, ALU.add,
                  [pg.res, cws.res, t1.res], [t1.res])
            p.act(sg[:, 0:w], t1[:, 0:w], AF.Silu, [t1.res], [sg.res])
            p.tt('dve', hbv[:, 0:w], sg[:, 0:w], pu[:, 1:w + 1], ALU.mult, [sg.res, pu.res], [hb.res])
            p.dma('sp', self.HID[ti + toff, :, fch, 0:w], hbv[:, 0:w], [hb.res], [self.HID_res])

        self.linear(self.hT, self.hT_res, KC, wblocks, tiles, epi_up)
        dtiles = [(c0, w, s, 0, 0) for (c0, w, s) in base]
        dblocks = []
        for (b0, nb) in ((0, 8), (8, 8)):
            dblocks.append(dict(loads=[(self.ffn_w_down[l, :, b0 * 128:(b0 + nb) * 128], nb * 128)],
                                groups=[[(j * 128, 128)] for j in range(nb)], c0=b0))

        def xload(ti, tile, xv, xb):
            c0, w, s, hl, hr = tile
            p.dma('pool', xv, self.HID[ti + toff, :, :, 0:w], [self.HID_res], [xb.res])

        self.linear(None, None, FC, dblocks, dtiles, self.epi_residual(5, lambda bi, gi: dblocks[bi]['c0'] + gi), xload=xload, wide=True)

    def layer(self, l):
        last = (l == self.last_layer)
        self.mods(l)
        if not self.skip_mixer:
            self.norm(l, 0)
            self.mixer(l, last)
        self.norm(l, 1, lat_only=last)
        self.ffn(l, lat_only=last)

    def mixer(self, l, last):
        raise NotImplementedError

    def epilogue_out(self):
        p = self.p
        p.barrier()
        xs = self.xT.rearrange('(kc p) t -> p kc t', p=128)
        os_ = self.outT.rearrange('(kc p) t -> p kc t', p=128)
        tb = [self.aview('ocp%d' % i, i * 32768, [128, KC, 512], F32) for i in range(2)]
        for i in range(8):
            b = tb[i % 2]
            p.dma('sp', b[:], xs[:, :, L0 + i * 512:L0 + (i + 1) * 512], [self.xT_res], [b.res])
            p.dma('sp', os_[:, :, i * 512:(i + 1) * 512], b[:], [b.res], [self.out_res])
        if self.debug:
            self.debug_out()
        p.barrier()

    def debug_out(self):
        pass


def prep_common(inp, b):
    xT = np.zeros((D, TP), np.float32)
    xT[:, C0:C1] = inp['ctx'][b].T
    xT[:, L0:L1] = inp['x'][b].T
    cT = np.stack([colT(inp['c'][b]), colT(inp['c_ctx'])], axis=-1)
    m = {
        'xT': xT,
        'cT': np.ascontiguousarray(cT),
        'ada_w': inp['ada_w'],
        'abT': colT(inp['ada_b']),
        'ngT': colT(np.stack([inp['norm_mix'], inp['norm_ffn']], axis=1)),
        'ffn_w_gu': inp['ffn_w_gu'],
        'ffn_cwT': colT(inp['ffn_conv']),
        'ffn_w_down': inp['ffn_w_down'],
        'ident': np.eye(128, dtype=np.float32),
    }
    return m


def kernel(**inputs):
    inp = {k: np.asarray(v) for k, v in inputs.items()}
    mk = MKFull()
    nc = mk.build()
    in_maps = []
    for b in range(8):
        m = prep_common(inp, b)
        m.update(prep_mixers(inp, b))
        in_maps.append(m)
    res = run_bass_kernel_spmd(nc, in_maps, core_ids=list(range(8)))
    out = np.stack([np.ascontiguousarray(r['outT'].T) for r in res.results], axis=0)
    return out.astype(np.float32)


class MKFull(MK):
    def declare_mixer_io(self):
        din = self.din
        L = self.layers
        if 0 in L:
            self.ml_w_in = din('ml_w_in', [D, 6144])
            self.ml_w_qkp = din('ml_w_qkp', [D, 2048])
            self.ml_w_gate = din('ml_w_gate', [D, 32])
            self.ml_bg = din('ml_bg', [32, 1])
            self.ml_hg = din('ml_hg', [1, 2048])
            self.ml_w_out = din('ml_w_out', [D, D])
            self.rope = din('rope', [4, 128, TP])
            self.tri = din('tri', [2, 128, 128])
        if 1 in L:
            self.na_w_qkv = din('na_w_qkv', [D, 6144])
            self.na_g = din('na_g', [128, 2])
            self.na_tab = din('na_tab', [16, 2, 128, 16, 64])
            self.na_w_o = din('na_w_o', [D, D])
        if 2 in L:
            self.cv_w_pw1 = din('cv_w_pw1', [D, 4096])
            self.cv_dwT = din('cv_dwT', [128, 16, 31])
            self.cv_vT = din('cv_vT', [128, 3, 16])
            self.cv_w_pw2 = din('cv_w_pw2', [D, D])
        if 3 in L:
            self.lr_w_in = din('lr_w_in', [D, 4096])
            self.lr_cvT = din('lr_cvT', [128, 5, 16])
            self.lr_w_gate = din('lr_w_gate', [4, 8, 256, 256])
            self.lr_bgT = din('lr_bgT', [128, 4, 16])
            self.lr_lamT = din('lr_lamT', [128, 2, 16])
            self.lr_w_out = din('lr_w_out', [D, D])
        self.S = {}
        self.Sres = {}
        for n in ('S1', 'S2', 'S3', 'S4', 'S5'):
            self.S[n], self.Sres[n] = self.dscr(n, [D, TP], BF16)
        self.F1, self.F1_res = self.dscr('F1', [D, TP], F32)

    def prologue(self):
        MK.prologue(self)
        p = self.p
        zt = self.aview('zt2', 80000, [128, KC, 2 * PAD], BF16)
        p.memset('dve', zt[:], 0.0, [zt.res])
        hs = self.S['S1'].rearrange('(kc p) t -> p kc t', p=128)
        p.dma('sp', hs[:, :, 0:PAD], zt[:, :, 0:PAD], [zt.res], [self.Sres['S1']])
        p.dma('sp', hs[:, :, C1:L0], zt[:, :, :], [zt.res], [self.Sres['S1']])
        p.dma('sp', hs[:, :, L1:TP], zt[:, :, 0:PAD], [zt.res], [self.Sres['S1']])
        p.barrier()

    def mixer(self, l, last):
        [self.mixer_ml, self.mixer_na, self.mixer_cv, self.mixer_lr][l % 4](l, last)

    def bfstage(self):
        st = self.next_stage()
        return st, st.ap.bitcast(BF16)

    def plain(self, lat_only=False):
        t = [(c0, w, s, 0, 0) for (c0, w, s) in tiles_plain()]
        return t[1:] if lat_only else t

    def out_proj(self, w_ap, src, last):
        blocks = [dict(loads=[(w_ap[:, 0:1024], 1024), (w_ap[:, 1024:2048], 1024)],
                       groups=[[(j * 128, 128)] for j in range(16)], c0=0)]
        self.linear(self.S[src], self.Sres[src], KC, blocks, self.plain(last),
                    self.epi_residual(2, lambda bi, gi: gi), wide=True)

    def mixer_cv(self, l, last):
        p = self.p
        S1, S1r = self.S['S1'], self.Sres['S1']
        S2, S2r = self.S['S2'], self.Sres['S2']
        blocks = []
        j = 0
        while j < 16:
            nf = min(5, 16 - j)
            blocks.append(dict(loads=[(self.cv_w_pw1[:, j * 128:(j + nf) * 128], nf * 128),
                                      (self.cv_w_pw1[:, D + j * 128:D + (j + nf) * 128], nf * 128)],
                               groups=[[(i * 128, 128), (nf * 128 + i * 128, 128)] for i in range(nf)], f0=j))
            j += nf

        def epiA(bi, gi, ti, tile, pss):
            c0, w, s, hl, hr = tile
            jc = blocks[bi]['f0'] + gi
            pa, pg = pss
            sg = self.next_stage()
            hb, hbv = self.bfstage()
            p.act(sg[:, 0:w], pg[:, 0:w], AF.Sigmoid, [pg.res], [sg.res])
            p.tt('dve', hbv[:, 0:w], pa[:, 0:w], sg[:, 0:w], ALU.mult, [pa.res, sg.res], [hb.res])
            p.dma('sp', S1[jc * 128:(jc + 1) * 128, c0:c0 + w], hbv[:, 0:w], [hb.res], [S1r])

        self.linear(self.hT, self.hT_res, KC, blocks, self.plain(), epiA)
        p.barrier()
        dg = [self.aview('dg%d' % i, i * 8192, [128, 31, 128], BF16) for i in range(2)]
        xt = [self.aview('cxt%d' % i, 16384 + i * 2048, [128, 544], BF16) for i in range(4)]
        dwT = self.aview('dwT', 24576, [128, 16, 31], F32)
        vT = self.aview('vT', 28672, [128, 3, 16], F32)
        p.dma('sp', dwT[:], self.cv_dwT, [self.in_res], [dwT.res])
        p.dma('sp', vT[:], self.cv_vT, [self.in_res], [vT.res])
        n = 0
        for jc in range(16):
            d = dg[jc % 2]
            for k in range(31):
                if k % 2 == 0:
                    p.ts('dve', d[:, k, :], self.ident_b[:], dwT[:, jc, k:k + 1], None, ALU.mult, None,
                         [self.ident_b.res, dwT.res], [d.res])
                else:
                    p.act(d[:, k, :], self.ident_b[:], AF.Copy, [self.ident_b.res, dwT.res], [d.res],
                          scale=dwT[:, jc, k:k + 1])
            for (c0, w, s, _, _) in self.plain():
                x = xt[n % 4]
                n += 1
                p.dma('pool', x[:, 0:w + 30], S1[jc * 128:(jc + 1) * 128, c0 - 15:c0 + w + 15], [S1r], [x.res])
                ps = self.next_psum()
                for k in range(31):
                    p.mm(ps[:, 0:w], d[:, k, :], x[:, k:k + w], k == 0, k == 30, [d.res, x.res], [ps.res], inc=(k == 30))
                st = self.next_stage()
                p.act(st[:, 0:w], ps[:, 0:w], AF.Identity, [ps.res, vT.res], [st.res], bias=vT[:, 0, jc:jc + 1], scale=1.0)
                p.dma('sp', self.F1[jc * 128:(jc + 1) * 128, c0:c0 + w], st[:, 0:w], [st.res], [self.F1_res])
        p.barrier()
        yin = [self.aview('ly%d' % i, i * 32768, [128, KC, 512], F32) for i in range(2)]
        sq = self.aview('lsq', 65536, [128, KC, 512], BF16)
        ob = [self.aview('lob%d' % i, 81920 + i * 16384, [128, KC, 512], BF16) for i in range(2)]
        sm = [self.aview('lsm%d' % i, 114688 + i * 2048, [128, 512], F32) for i in range(7)]
        vT = self.aview('vT2', 131072, [128, 3, 16], F32)
        p.dma('sp', vT[:], self.cv_vT, [self.in_res], [vT.res])
        mean, m2, rstd = sm[0], sm[1], sm[2]
        tmp = sm[3:7]
        ys = self.F1.rearrange('(kc p) t -> p kc t', p=128)
        os_ = S2.rearrange('(kc p) t -> p kc t', p=128)
        for ti, (c0, w, s, _, _) in enumerate(self.plain()):
            y = yin[ti % 2]
            o = ob[ti % 2]
            p.dma('sp', y[:, :, 0:w], ys[:, :, c0:c0 + w], [self.F1_res], [y.res])
            p.act(sq[:, :, 0:w], y[:, :, 0:w], AF.Square, [y.res], [sq.res])
            ps1 = self.next_psum()
            ps2 = self.next_psum()
            for kc in range(KC):
                p.mm(ps1[:, 0:w], self.ones_f[:], y[:, kc, 0:w], kc == 0, kc == KC - 1, [y.res, self.ones_f.res],
                     [ps1.res], inc=(kc == KC - 1))
            for kc in range(KC):
                p.mm(ps2[:, 0:w], self.ones_b[:], sq[:, kc, 0:w], kc == 0, kc == KC - 1, [sq.res, self.ones_b.res],
                     [ps2.res], inc=(kc == KC - 1))
            p.act(mean[:, 0:w], ps1[:, 0:w], AF.Copy, [ps1.res], [mean.res], scale=1.0 / D)
            p.tt('dve', m2[:, 0:w], mean[:, 0:w], mean[:, 0:w], ALU.mult, [mean.res], [m2.res])
            p.stt('dve', m2[:, 0:w], ps2[:, 0:w], 1.0 / D, m2[:, 0:w], ALU.mult, ALU.subtract, [ps2.res, m2.res], [m2.res])
            p.act(rstd[:, 0:w], m2[:, 0:w], AF.Sqrt, [m2.res, self.eps_t.res], [rstd.res], bias=self.eps_t[:, 0:1], scale=1.0)
            p.recip(rstd[:, 0:w], rstd[:, 0:w], [rstd.res], [rstd.res])
            for kc in range(KC):
                t = tmp[kc % 4]
                p.tt('dve', t[:, 0:w], y[:, kc, 0:w], mean[:, 0:w], ALU.subtract, [y.res, mean.res], [t.res])
                p.tt('dve', t[:, 0:w], t[:, 0:w], rstd[:, 0:w], ALU.mult, [t.res, rstd.res], [t.res])
                p.act(o[:, kc, 0:w], t[:, 0:w], AF.Silu, [t.res, vT.res], [o.res], scale=vT[:, 1, kc:kc + 1],
                      bias=vT[:, 2, kc:kc + 1])
            p.dma('sp', os_[:, :, c0:c0 + w], o[:, :, 0:w], [o.res], [S2r])
        p.barrier()
        self.out_proj(self.cv_w_pw2, 'S2', last)

    def mixer_lr(self, l, last):
        p = self.p
        S1, S1r = self.S['S1'], self.Sres['S1']
        S2, S2r = self.S['S2'], self.Sres['S2']
        S3, S3r = self.S['S3'], self.Sres['S3']
        cv = self.aview('lrcv', self.EXTRA, [128, 5, 16], F32)
        p.dma('sp', cv[:], self.lr_cvT, [self.in_res], [cv.res])
        tiles = [(c0, w, s, 2, 1) for (c0, w, s) in tiles_n(456)]
        blocks = [dict(loads=[(self.lr_w_in[:, b * 1024:(b + 1) * 1024], 1024)], groups=[[(j * 128, 128)] for j in range(8)])
                  for b in range(4)]

        def epiA(bi, gi, ti, tile, pss):
            c0, w, s, hl, hr = tile
            ps = pss[0]
            if bi < 2:
                c = bi * 8 + gi
                st, stv = self.bfstage()
                p.act(stv[:, 0:w], ps[:, hl:hl + w], AF.Gelu_apprx_tanh, [ps.res], [st.res])
                p.dma('sp', S2[c * 128:(c + 1) * 128, c0:c0 + w], stv[:, 0:w], [st.res], [S2r])
            else:
                c = (bi - 2) * 8 + gi
                t = self.next_stage()
                tb, tbv = self.bfstage()
                p.ts('dve', t[:, 0:w], ps[:, 0:w], cv[:, 0, c:c + 1], cv[:, 4, c:c + 1], ALU.mult, ALU.add,
                     [ps.res, cv.res], [t.res])
                for k in range(1, 4):
                    p.stt('dve', t[:, 0:w], ps[:, k:k + w], cv[:, k, c:c + 1], t[:, 0:w], ALU.mult, ALU.add,
                          [ps.res, cv.res, t.res], [t.res])
                p.act(tbv[:, 0:w], t[:, 0:w], AF.Copy, [t.res], [tb.res])
                p.dma('sp', self.F1[c * 128:(c + 1) * 128, c0:c0 + w], t[:, 0:w], [t.res], [self.F1_res])
                p.dma('sp', S1[c * 128:(c + 1) * 128, c0:c0 + w], tbv[:, 0:w], [tb.res], [S1r])

        self.linear(self.hT, self.hT_res, KC, blocks, tiles, epiA)
        p.barrier()
        SEQ = TP * 4
        A = [self.aview('lrA%d' % d, (2 * d) * SEQ, [128, TP], F32) for d in range(2)]
        Bv = [self.aview('lrB%d' % d, (2 * d + 1) * SEQ, [128, TP], F32) for d in range(2)]
        ubf = self.aview('lrubf', 4 * SEQ, [128, 2, TP], BF16)
        uf = self.aview('lruf', 5 * SEQ, [128, TP], F32)
        gg = self.aview('lrgg', 6 * SEQ, [128, TP], BF16)
        ob = self.aview('lrob', 6 * SEQ + TP * 2, [128, TP], BF16)
        o2 = 7 * SEQ
        wg = [self.aview('lrwg%d' % i, o2 + i * 2048, [128, 4, 2, 128], BF16) for i in range(2)]
        bgT = self.aview('lrbg', o2 + 4096, [128, 4, 16], F32)
        lam = self.aview('lrlam', o2 + 4096 + 256, [128, 2, 16], F32)
        kap = self.aview('lrkap', o2 + 4096 + 512, [128, 2, 2, 16], F32)
        p.dma('sp', bgT[:], self.lr_bgT, [self.in_res], [bgT.res])
        p.dma('sp', lam[:], self.lr_lamT, [self.in_res], [lam.res])
        p.act(lam[:], lam[:], AF.Exp, [lam.res], [lam.res], scale=-1.0)
        p.act(lam[:], lam[:], AF.Ln, [lam.res, self.ones_f.res], [lam.res], bias=self.ones_f[:, 0:1], scale=1.0)
        p.ts('dve', kap[:, 0, :, :], lam[:], -LRU_C, None, ALU.mult, None, [lam.res], [kap.res])
        p.ts('dve', kap[:, 1, :, :], lam[:], -2.0 * LRU_C, None, ALU.mult, None, [lam.res], [kap.res])
        pl = self.plain()

        def rev(b, a0, a1):
            return b.ap[:, a0:a1][:, ::-1]

        for c in range(16):
            nb, jo = c // 2, c % 2
            w = wg[c % 2]
            for t in range(4):
                p.dma('pool', w[:, t, :, :],
                      self.lr_w_gate[t, nb, :, jo * 128:(jo + 1) * 128].rearrange('(ki p) j -> p ki j', p=128),
                      [self.in_res], [w.res])
            if jo == 0:
                p.dma('sp', ubf[:], S1[nb * 256:(nb + 1) * 256, :].rearrange('(ki p) t -> p ki t', p=128), [S1r], [ubf.res])
            p.dma('sp', uf[:], self.F1[c * 128:(c + 1) * 128, :], [self.F1_res], [uf.res])
            p.dma('sp', gg[:], S2[c * 128:(c + 1) * 128, :], [S2r], [gg.res])
            for (c0, w_, s, _, _) in pl:
                pss = [self.next_psum() for _ in range(4)]
                for t in range(4):
                    for ki in range(2):
                        p.mm(pss[t][:, 0:w_], w[:, t, ki, :], ubf[:, ki, c0:c0 + w_], ki == 0, ki == 1,
                             [w.res, ubf.res], [pss[t].res], inc=(ki == 1))
                rr = [self.next_stage() for _ in range(2)]
                ii = [self.next_stage() for _ in range(2)]
                r2s = [self.next_stage() for _ in range(2)]
                for d in range(2):
                    p.act(rr[d][:, 0:w_], pss[2 * d][:, 0:w_], AF.Sigmoid, [pss[2 * d].res, bgT.res], [rr[d].res],
                          bias=bgT[:, 2 * d, c:c + 1], scale=1.0)
                    p.act(ii[d][:, 0:w_], pss[2 * d + 1][:, 0:w_], AF.Sigmoid, [pss[2 * d + 1].res, bgT.res], [ii[d].res],
                          bias=bgT[:, 2 * d + 1, c:c + 1], scale=1.0)
                for d in range(2):
                    p.act(A[d][:, c0:c0 + w_], rr[d][:, 0:w_], AF.Exp, [rr[d].res, kap.res], [A[d].res],
                          scale=kap[:, 0, d, c:c + 1])
                    p.tt('dve', r2s[d][:, 0:w_], A[d][:, c0:c0 + w_], A[d][:, c0:c0 + w_], ALU.mult, [A[d].res], [r2s[d].res])
                for d in range(2):
                    p.act(r2s[d][:, 0:w_], r2s[d][:, 0:w_], AF.Sqrt, [r2s[d].res, self.ones_f.res], [r2s[d].res], scale=-1.0,
                          bias=self.ones_f[:, 0:1])
                    p.tt('pool', ii[d][:, 0:w_], ii[d][:, 0:w_], r2s[d][:, 0:w_], ALU.mult, [ii[d].res, r2s[d].res], [ii[d].res])
                    p.tt('dve', Bv[d][:, c0:c0 + w_], ii[d][:, 0:w_], uf[:, c0:c0 + w_], ALU.mult, [ii[d].res, uf.res], [Bv[d].res])

            def scan(out, a, b, init, reads, writes):
                p.op('dve', lambda e: e.tensor_tensor_scan(out=out, data0=a, data1=b, initial=init, op0=ALU.mult,
                                                          op1=ALU.add), reads, writes)

            rw = ([A[0].res, Bv[0].res], [Bv[0].res])
            scan(Bv[0][:, C0:C1], A[0][:, C0:C1], Bv[0][:, C0:C1], 0.0, *rw)
            scan(Bv[0][:, L0:L1], A[0][:, L0:L1], Bv[0][:, L0:L1], Bv[0][:, C1 - 1:C1], *rw)
            rw = ([A[1].res, Bv[1].res], [Bv[1].res])
            scan(rev(Bv[1], C0, C1), rev(A[1], C0, C1), rev(Bv[1], C0, C1), 0.0, *rw)
            scan(rev(Bv[1], L0, L1), rev(A[1], L0, L1), rev(Bv[1], L0, L1), Bv[1][:, C0:C0 + 1], *rw)
            segs = [(L0 + i * 1024, L0 + (i + 1) * 1024) for i in range(4)]
            if not last:
                segs = [(C0, C1)] + segs
            for (a0, a1) in segs:
                p.tt('dve', Bv[0][:, a0:a1], Bv[0][:, a0:a1], Bv[1][:, a0:a1], ALU.add, [Bv[0].res, Bv[1].res], [Bv[0].res])
                p.tt('dve', ob[:, a0:a1], Bv[0][:, a0:a1], gg[:, a0:a1], ALU.mult, [Bv[0].res, gg.res], [ob.res])
            if not last:
                p.dma('sp', S3[c * 128:(c + 1) * 128, C0:C1], ob[:, C0:C1], [ob.res], [S3r])
            p.dma('sp', S3[c * 128:(c + 1) * 128, L0:L1], ob[:, L0:L1], [ob.res], [S3r])
        p.barrier()
        self.out_proj(self.lr_w_out, 'S3', last)


    def mixer_ml(self, l, last):
        p = self.p
        S1, S1r = self.S['S1'], self.Sres['S1']
        S2, S2r = self.S['S2'], self.Sres['S2']
        S3, S3r = self.S['S3'], self.Sres['S3']
        S4, S4r = self.S['S4'], self.Sres['S4']
        S5, S5r = self.S['S5'], self.Sres['S5']
        bg = self.aview('mlbg', self.EXTRA, [128, 1], F32)
        p.dma('sp', bg[0:32, :], self.ml_bg, [self.in_res], [bg.res])
        blocks = []
        for which in range(2):
            for (h0, nh) in ((0, 5), (5, 3)):
                cb = which * 1024 + h0 * 128
                blocks.append(dict(kind='qk', which=which, f0=h0,
                                   loads=[(self.ml_w_in[:, cb:cb + nh * 128], nh * 128),
                                          (self.ml_w_qkp[:, cb:cb + nh * 128], nh * 128)],
                                   groups=[[(i * 128, 128), (nh * 128 + i * 128, 128)] for i in range(nh)]))
        for b in range(2):
            blocks.append(dict(kind='v', f0=b * 8, loads=[(self.ml_w_in[:, 2048 + b * 1024:2048 + (b + 1) * 1024], 1024)],
                               groups=[[(i * 128, 128)] for i in range(8)]))
        blocks.append(dict(kind='o', f0=0, loads=[(self.ml_w_in[:, 4096:5120], 1024)],
                           groups=[[(i * 128, 128)] for i in range(8)]))
        blocks.append(dict(kind='o', f0=8, loads=[(self.ml_w_in[:, 5120:6144], 1024), (self.ml_w_gate[:, 0:32], 32)],
                           groups=[[(i * 128, 128)] for i in range(8)] + [[(1024, 32)]]))

        def epiA(bi, gi, ti, tile, pss):
            c0, w, s, hl, hr = tile
            blk = blocks[bi]
            if blk['kind'] == 'qk':
                which = blk['which']
                h = blk['f0'] + gi
                pa, pb = pss
                ct = self.next_stage()
                sn = self.next_stage()
                p.dma('pool', ct[:, 0:w], self.rope[2 * which, :, c0:c0 + w], [self.in_res], [ct.res])
                p.dma('pool', sn[:, 0:w], self.rope[2 * which + 1, :, c0:c0 + w], [self.in_res], [sn.res])
                p.tt('dve', ct[:, 0:w], pa[:, 0:w], ct[:, 0:w], ALU.mult, [pa.res, ct.res], [ct.res])
                p.tt('dve', sn[:, 0:w], pb[:, 0:w], sn[:, 0:w], ALU.mult, [pb.res, sn.res], [sn.res])
                ob, obv = self.bfstage()
                p.tt('dve', obv[:, 0:w], ct[:, 0:w], sn[:, 0:w], ALU.add, [ct.res, sn.res], [ob.res])
                dst, dr = (S1, S1r) if which == 0 else (S2, S2r)
                p.dma('sp', dst[h * 128:(h + 1) * 128, c0:c0 + w], obv[:, 0:w], [ob.res], [dr])
            elif blk['kind'] == 'v':
                c = blk['f0'] + gi
                ob, obv = self.bfstage()
                p.act(obv[:, 0:w], pss[0][:, 0:w], AF.Copy, [pss[0].res], [ob.res])
                p.dma('sp', S3[c * 128:(c + 1) * 128, c0:c0 + w], obv[:, 0:w], [ob.res], [S3r])
            else:
                ps = pss[0]
                if gi < 8:
                    c = blk['f0'] + gi
                    ob, obv = self.bfstage()
                    p.act(obv[:, 0:w], ps[:, 0:w], AF.Sigmoid, [ps.res], [ob.res])
                    p.dma('sp', S4[c * 128:(c + 1) * 128, c0:c0 + w], obv[:, 0:w], [ob.res], [S4r])
                else:
                    st = self.next_stage()
                    p.act(st[0:32, 0:w], ps[0:32, 0:w], AF.Identity, [ps.res, bg.res], [st.res], bias=bg[0:32, 0:1], scale=1.0)
                    p.dma('sp', self.F1[0:32, c0:c0 + w], st[0:32, 0:w], [st.res], [self.F1_res])

        self.linear(self.hT, self.hT_res, KC, blocks, self.plain(), epiA)
        p.barrier()
        gT = self.aview('mlgT', 0, [128, TP], F32)
        qT = self.aview('mlq', 17664, [128, TP], BF16)
        kT = self.aview('mlk', 26496, [128, TP], BF16)
        vT = self.aview('mlv', 35328, [128, 2, TP], BF16)
        so = self.aview('mlso', 52992, [128, 2, TP], BF16)
        Hs = self.aview('mlH', 70656, [128, 34, 256], F32)
        hgrow = self.aview('mlhgr', 70656, [128, 2048], F32)
        Ob = self.aview('mlO', 105472, [128, 2, TP], BF16)
        hgB = self.aview('mlhgB', 123136, [128, 2048], F32)
        Gt = self.aview('mlGt', 131328, [128, 34, 32], F32)
        NL = self.aview('mlNL', 135680, [128, 34, 32], F32)
        LF = [self.aview('mlLF%d' % d, 140032 + d * 1088, [128, 34, 8], F32) for d in range(2)]
        Wd = [self.aview('mlW%d' % d, 142208 + d * 1088, [128, 34, 8], F32) for d in range(2)]
        Ed = [self.aview('mlE%d' % d, 144384 + d * 1088, [128, 34, 8], F32) for d in range(2)]
        ELd = [self.aview('mlEL%d' % d, 146560 + d * 1088, [128, 34, 8], F32) for d in range(2)]
        Cs = [self.aview('mlC%d' % d, 148736 + d * 1032, [128, 258], F32) for d in range(2)]
        Cb = [self.aview('mlCb%d' % d, 150800 + d * 516, [128, 258], BF16) for d in range(2)]
        Vta = [self.aview('mlVta%d' % d, 151832 + d * 516, [128, 258], BF16) for d in range(4)]
        tri = self.aview('mltri', 153896, [128, 2, 128], F32)
        ssb = [self.aview('mlss%d' % i, 154920 + i * 16, [128, 4], F32) for i in range(8)]
        p.dma('sp', tri[:], self.tri.rearrange('a p t -> p a t'), [self.in_res], [tri.res])
        p.dma('sp', gT[0:32, :], self.F1[0:32, :], [self.F1_res], [gT.res])
        p.dma('sp', hgrow[0:1, :], self.ml_hg, [self.in_res], [hgrow.res])
        for i in range(4):
            ps = self.next_psum()
            p.mm(ps[:, :], self.ones_f[0:1, :], hgrow[0:1, i * 512:(i + 1) * 512], True, True, [self.ones_f.res, hgrow.res],
                 [ps.res], inc=True)
            p.copy('dve', hgB[:, i * 512:(i + 1) * 512], ps[:, :], [ps.res], [hgB.res])

        def tcol(ck):
            return C0 + 128 * ck if ck < 2 else L0 + 128 * (ck - 2)

        for g in range(9):
            cks = list(range(g * 4, min(34, g * 4 + 4)))
            ps = self.next_psum()
            for i, ck in enumerate(cks):
                p.tr(ps[:, i * 32:(i + 1) * 32], gT[0:32, tcol(ck):tcol(ck) + 128], self.ident_f[0:32, 0:32],
                     [gT.res, self.ident_f.res], [ps.res], inc=(i == len(cks) - 1))
            n = len(cks)
            p.copy('dve', Gt[:, g * 4:g * 4 + n, :], ps[:, 0:n * 32].rearrange('p (a b) -> p a b', a=n), [ps.res], [Gt.res])
        p.act(NL[:], Gt[:], AF.Exp, [Gt.res], [NL.res], scale=-1.0)
        p.act(NL[:], NL[:], AF.Ln, [NL.res, self.ones_f.res], [NL.res], bias=self.ones_f[:, 0:1], scale=1.0)
        for d in range(2):
            p.ts('dve', LF[d][:], NL[:, :, 16 * d + 8:16 * d + 16], -1.0, None, ALU.mult, None, [NL.res], [LF[d].res])
        for d in range(2):
            lf2 = LF[d].ap.rearrange('p a b -> p (a b)')
            psB = self.next_psum()
            psT = self.next_psum()
            p.mm(psB[:, 0:272], tri[:, d, :], lf2, True, True, [tri.res, LF[d].res], [psB.res], inc=True)
            p.mm(psT[:, 0:272], self.ones_f[:], lf2, True, True, [self.ones_f.res, LF[d].res], [psT.res], inc=True)
            w2 = Wd[d].ap.rearrange('p a b -> p (a b)')
            p.tt('dve', Wd[d][:], Gt[:, :, 16 * d:16 * d + 8], psB[:, 0:272].rearrange('p (a b) -> p a b', a=34), ALU.subtract,
                 [Gt.res, psB.res], [Wd[d].res])
            p.act(w2, w2, AF.Exp, [Wd[d].res], [Wd[d].res])
            p.act(Ed[d].ap.rearrange('p a b -> p (a b)'), psB[:, 0:272], AF.Exp, [psB.res], [Ed[d].res])
            p.act(ELd[d].ap.rearrange('p a b -> p (a b)'), psT[:, 0:272], AF.Exp, [psT.res], [ELd[d].res])
        for v4 in Vta:
            p.memset('dve', v4[:], 1.0, [v4.res])
        p.barrier()
        order = [list(range(34)), [1, 0] + list(range(33, 1, -1))]
        vi = 0
        for h in range(8):
            p.dma('sp', qT[:], S1[h * 128:(h + 1) * 128, :], [S1r], [qT.res])
            p.dma('sp', kT[:], S2[h * 128:(h + 1) * 128, :], [S2r], [kT.res])
            p.dma('sp', vT[:], S3[2 * h * 128:(2 * h + 2) * 128, :].rearrange('(j p) t -> p j t', p=128), [S3r], [vT.res])
            p.dma('sp', so[:], S4[2 * h * 128:(2 * h + 2) * 128, :].rearrange('(j p) t -> p j t', p=128), [S4r], [so.res])
            p.memset('dve', Hs[:], 0.0, [Hs.res])
            for d in range(2):
                p.memset('dve', Cs[d][:], 0.0, [Cs[d].res])
                p.memset('dve', Cb[d][:], 0.0, [Cb[d].res])
            for i in range(34):
                for d in range(2):
                    ck = order[d][i]
                    col = tcol(ck)
                    wcol = Wd[d][:, ck, h:h + 1]
                    ecol = Ed[d][:, ck, h:h + 1]
                    elcol = ELd[d][:, ck, h:h + 1]
                    va = Vta[vi % 4]
                    vi += 1
                    tp = self.next_psum()
                    tpv = tp.ap.bitcast(BF16)
                    p.tr(tpv[:, 0:128], kT[:, col:col + 128], self.ident_b[:], [kT.res, self.ident_b.res], [tp.res], inc=False)
                    p.tr(tpv[:, 128:256], vT[:, 0, col:col + 128], self.ident_b[:], [vT.res], [tp.res], inc=False)
                    p.tr(tpv[:, 256:384], vT[:, 1, col:col + 128], self.ident_b[:], [vT.res], [tp.res], inc=True)
                    ktw, ktwv = self.bfstage()
                    p.act(ktwv[:, 0:128], tpv[:, 0:128], AF.Copy, [tp.res, Wd[d].res], [ktw.res], scale=wcol)
                    p.copy('dve', va[:, 0:256], tpv[:, 128:384], [tp.res], [va.res])
                    ps_s = self.next_psum()
                    p.mm(ps_s[:, 0:128], kT[:, col:col + 128], qT[:, col:col + 128], True, True, [kT.res, qT.res],
                         [ps_s.res], inc=True)
                    pt, ptv = self.bfstage()
                    p.stt('dve', ptv[:, 0:128], ps_s[:, 0:128], wcol, tri[:, d, :], ALU.mult, ALU.mult,
                          [ps_s.res, Wd[d].res, tri.res], [pt.res])
                    ps_n = self.next_psum()
                    p.mm(ps_n[:, 0:257], ptv[:, 0:128], va[:, 0:257], True, False, [pt.res, va.res], [ps_n.res], inc=False)
                    p.mm(ps_n[:, 0:257], qT[:, col:col + 128], Cb[d][:, 0:257], False, True, [qT.res, Cb[d].res], [ps_n.res],
                         inc=True)
                    sm = ssb[vi % 8]
                    p.act(sm[:, 0:1], ps_n[:, 256:257], AF.Abs, [ps_n.res], [sm.res])
                    p.recip(sm[:, 1:2], sm[:, 0:1], [sm.res], [sm.res])
                    p.tt('dve', sm[:, 2:3], sm[:, 1:2], ecol, ALU.min, [sm.res, Ed[d].res], [sm.res])
                    p.stt('dve', Hs[:, ck, :], ps_n[:, 0:256], sm[:, 2:3], Hs[:, ck, :], ALU.mult, ALU.add,
                          [ps_n.res, sm.res, Hs.res], [Hs.res])
                    ps_c = self.next_psum()
                    p.mm(ps_c[:, 0:257], ktwv[:, 0:128], va[:, 0:257], True, True, [ktw.res, va.res], [ps_c.res], inc=True)
                    p.tt('dve', Cs[d][:, 0:257], ps_c[:, 0:257], Cs[d][:, 0:257], ALU.add, [ps_c.res, Cs[d].res], [Cs[d].res])
                    p.act(Cb[d][:, 0:257], Cs[d][:, 0:257], AF.Copy, [Cs[d].res, ELd[d].res], [Cb[d].res], scale=elcol)
                    p.ts('dve', Cs[d][:, 0:257], Cs[d][:, 0:257], elcol, None, ALU.mult, None, [Cs[d].res, ELd[d].res], [Cs[d].res])
            for ck in range(34):
                col = tcol(ck)
                sm = ssb[ck % 8]
                junk = self.next_stage()
                p.op('act', lambda e, o_=junk[:, 0:256], i_=Hs[:, ck, :], a_=sm[:, 0:1]: e.activation(
                    out=o_, in_=i_, func=AF.Square, accum_out=a_), [Hs.res], [junk.res, sm.res])
                p.act(sm[:, 1:2], sm[:, 0:1], AF.Sqrt, [sm.res, self.eps_t.res], [sm.res], bias=self.eps_t[:, 0:1], scale=1.0 / 256)
                p.recip(sm[:, 2:3], sm[:, 1:2], [sm.res], [sm.res])
                hn = self.next_stage()
                p.stt('dve', hn[:, 0:256], Hs[:, ck, :], sm[:, 2:3], hgB[:, h * 256:(h + 1) * 256], ALU.mult, ALU.mult,
                      [Hs.res, sm.res, hgB.res], [hn.res])
                ps_t = self.next_psum()
                p.tr(ps_t[:, 0:128], hn[:, 0:128], self.ident_f[:], [hn.res, self.ident_f.res], [ps_t.res], inc=False)
                p.tr(ps_t[:, 128:256], hn[:, 128:256], self.ident_f[:], [hn.res], [ps_t.res], inc=True)
                p.tt('dve', Ob[:, :, col:col + 128], ps_t[:, 0:256].rearrange('p (j t) -> p j t', j=2), so[:, :, col:col + 128],
                     ALU.mult, [ps_t.res, so.res], [Ob.res])
            dst = S5[2 * h * 128:(2 * h + 2) * 128, :].rearrange('(j p) t -> p j t', p=128)
            p.dma('sp', dst[:, :, C0:C1], Ob[:, :, C0:C1], [Ob.res], [S5r])
            p.dma('sp', dst[:, :, L0:L1], Ob[:, :, L0:L1], [Ob.res], [S5r])
        p.barrier()
        self.out_proj(self.ml_w_out, 'S5', last)


    def mixer_na(self, l, last):
        p = self.p
        S1, S1r = self.S['S1'], self.Sres['S1']
        S2, S2r = self.S['S2'], self.Sres['S2']
        S3, S3r = self.S['S3'], self.Sres['S3']
        S4, S4r = self.S['S4'], self.Sres['S4']
        gq = self.aview('nag', self.EXTRA, [128, 2], F32)
        p.dma('sp', gq[:], self.na_g, [self.in_res], [gq.res])
        p.ts('dve', gq[:, 0:1], gq[:, 0:1], 128.0 ** -0.5, None, ALU.mult, None, [gq.res], [gq.res])
        blocks = []
        c = 0
        while c < 48:
            n = min(10, 48 - c)
            blocks.append(dict(loads=[(self.na_w_qkv[:, c * 128:(c + n) * 128], n * 128)],
                               groups=[[(i * 128, 128)] for i in range(n)], c0=c))
            c += n

        def epiA(bi, gi, ti, tile, pss):
            c0, w, s, hl, hr = tile
            cg = blocks[bi]['c0'] + gi
            ps = pss[0]
            if cg < 32:
                which, h = cg // 16, cg % 16
                sq, sqv = self.bfstage()
                p.act(sqv[:, 0:w], ps[:, 0:w], AF.Square, [ps.res], [sq.res])
                ps2 = self.next_psum()
                p.mm(ps2[:, 0:w], self.ones_b[:], sqv[:, 0:w], True, True, [sq.res, self.ones_b.res], [ps2.res], inc=True)
                rs = self.next_stage()
                p.act(rs[:, 0:w], ps2[:, 0:w], AF.Sqrt, [ps2.res, self.eps_t.res], [rs.res], bias=self.eps_t[:, 0:1],
                      scale=1.0 / 128)
                p.recip(rs[:, 0:w], rs[:, 0:w], [rs.res], [rs.res])
                ob, obv = self.bfstage()
                p.stt('dve', obv[:, 0:w], ps[:, 0:w], gq[:, which:which + 1], rs[:, 0:w], ALU.mult, ALU.mult,
                      [ps.res, gq.res, rs.res], [ob.res])
                dst, dr = (S1, S1r) if which == 0 else (S2, S2r)
                p.dma('sp', dst[h * 128:(h + 1) * 128, c0:c0 + w], obv[:, 0:w], [ob.res], [dr])
            else:
                cc = cg - 32
                ob, obv = self.bfstage()
                p.act(obv[:, 0:w], ps[:, 0:w], AF.Copy, [ps.res], [ob.res])
                p.dma('sp', S3[cc * 128:(cc + 1) * 128, c0:c0 + w], obv[:, 0:w], [ob.res], [S3r])

        self.linear(self.hT, self.hT_res, KC, blocks, self.plain(), epiA, defer=True)
        p.barrier()
        SEQ = TP * 2
        qT = [self.aview('naq%d' % i, i * SEQ, [128, TP], BF16) for i in range(2)]
        kT = [self.aview('nak%d' % i, (2 + i) * SEQ, [128, TP], BF16) for i in range(2)]
        vT = [self.aview('nav%d' % i, (4 + i) * SEQ, [128, TP], BF16) for i in range(2)]
        Ob = [self.aview('nao%d' % i, (6 + i) * SEQ, [128, TP], BF16) for i in range(2)]
        o2 = 8 * SEQ
        Vt = [self.aview('naVt%d' % i, o2 + i * 8704, [128, 34, 128], BF16) for i in range(2)]
        o3 = o2 + 2 * 8704
        tab = [self.aview('natab%d' % i, o3 + i * 8192, [128, 2, 16, 64], F32) for i in range(2)]
        Sb = self.psum[0:4]
        Ob_ps = self.psum[4:6]
        Db_ps = self.psum[6:8]

        def tcol(tk):
            return C0 + 128 * tk if tk < 2 else L0 + 128 * (tk - 2)

        def load_head(h):
            p.dma('pool', qT[h % 2][:], S1[h * 128:(h + 1) * 128, :], [S1r], [qT[h % 2].res])
            p.dma('pool', kT[h % 2][:], S2[h * 128:(h + 1) * 128, :], [S2r], [kT[h % 2].res])
            p.dma('pool', vT[h % 2][:], S3[h * 128:(h + 1) * 128, :], [S3r], [vT[h % 2].res])
            p.dma('pool', tab[h % 2][:], self.na_tab[h].rearrange('a p u c -> p a u c'), [self.in_res], [tab[h % 2].res])

        load_head(0)
        for h in range(16):
            q, k, v, O, V, tb = qT[h % 2], kT[h % 2], vT[h % 2], Ob[h % 2], Vt[h % 2], tab[h % 2]
            if h + 1 < 16:
                load_head(h + 1)
            for g in range(9):
                tks = list(range(g * 4, min(34, g * 4 + 4)))
                ps = Sb[g % 4]
                psv = ps.ap.bitcast(BF16)
                for i, tk in enumerate(tks):
                    p.tr(psv[:, i * 128:(i + 1) * 128], v[:, tcol(tk):tcol(tk) + 128], self.ident_b[:],
                         [v.res, self.ident_b.res], [ps.res], inc=(i == len(tks) - 1))
                n = len(tks)
                p.copy('dve' if g % 2 else 'act_copy', V[:, g * 4:g * 4 + n, :],
                       psv[:, 0:n * 128].rearrange('p (a b) -> p a b', a=n), [ps.res], [V.res])
            units = []
            qts = ([] if last else [(C0, None)]) + [(L0 + 256 * j, j) for j in range(16)]
            for qi, (qc0, j) in enumerate(qts):
                keys = []
                if j is not None:
                    if j == 0:
                        kts, ti_ = range(0, 4), 1
                    elif j == 15:
                        kts, ti_ = range(28, 32), 1
                    else:
                        kts, ti_ = range(2 * j - 2, 2 * j + 4), 0
                    for kt in kts:
                        keys.append((2 + kt, (ti_, 7 - 2 * kt + 4 * j)))
                keys += [(0, None), (1, None)]
                for ki, (tk, bias) in enumerate(keys):
                    units.append(dict(qi=qi, qc0=qc0, tk=tk, bias=bias, first=(ki == 0), last=(ki == len(keys) - 1)))

            def emit_S(i, u):
                ps = Sb[i % 4]
                kc_ = tcol(u['tk'])
                p.mm(ps[:, 0:256], k[:, kc_:kc_ + 128], q[:, u['qc0']:u['qc0'] + 256], True, True, [k.res, q.res], [ps.res],
                     inc=True)
                pt, ptv = self.bfstage()
                if u['bias'] is not None:
                    ti_, u0 = u['bias']
                    tmp = self.next_stage()
                    p.tt('dve', tmp.ap[:, 0:256].rearrange('p (a b) -> p a b', a=4),
                         ps.ap[:, 0:256].rearrange('p (a b) -> p a b', a=4), tb[:, ti_, u0:u0 + 4, :], ALU.add,
                         [ps.res, tb.res], [tmp.res])
                    p.act(ptv[:, 0:256], tmp[:, 0:256], AF.Exp, [tmp.res], [pt.res])
                else:
                    p.act(ptv[:, 0:256], ps[:, 0:256], AF.Exp, [ps.res], [pt.res])
                u['pt'] = (pt, ptv)

            def emit_PV(u):
                pt, ptv = u['pt']
                ob_ = Ob_ps[u['qi'] % 2]
                db_ = Db_ps[u['qi'] % 2]
                p.mm(ob_[:, 0:256], V[:, u['tk'], :], ptv[:, 0:256], u['first'], u['last'], [V.res, pt.res], [ob_.res],
                     inc=u['last'])
                p.mm(db_[:, 0:256], self.ones_b[:], ptv[:, 0:256], u['first'], u['last'], [self.ones_b.res, pt.res],
                     [db_.res], inc=True)
                if u['last']:
                    rec = self.next_stage()
                    p.recip(rec[:, 0:256], db_[:, 0:256], [db_.res], [rec.res])
                    p.tt('dve', O[:, u['qc0']:u['qc0'] + 256], ob_[:, 0:256], rec[:, 0:256], ALU.mult,
                         [ob_.res, rec.res], [O.res])

            pend = []
            for i, u in enumerate(units):
                emit_S(i, u)
                pend.append(u)
                if len(pend) > 2:
                    emit_PV(pend.pop(0))
            while pend:
                emit_PV(pend.pop(0))
            if not last:
                p.dma('sp', S4[h * 128:(h + 1) * 128, C0:C1], O[:, C0:C1], [O.res], [S4r])
            p.dma('sp', S4[h * 128:(h + 1) * 128, L0:L1], O[:, L0:L1], [O.res], [S4r])
        p.barrier()
        self.out_proj(self.na_w_o, 'S4', last)


LRU_C = 8.0
NEG = -30000.0


def rope_tables():
    t = np.arange(NLAT)
    freqs = (10000.0 ** (-np.arange(0, 64, 2, dtype=np.float32) / np.float32(64))).astype(np.float32)
    d = np.arange(128)
    pos = np.where(d[:, None] < 64, (t // 64)[None, :], (t % 64)[None, :]).astype(np.float32)
    ang = (pos * freqs[d % 32][:, None]).astype(np.float32)
    cos = np.ones((128, TP), np.float32)
    sin = np.zeros((128, TP), np.float32)
    cos[:, L0:L1] = np.cos(ang)
    sgn = np.where((d % 64) < 32, -1.0, 1.0).astype(np.float32)
    sin[:, L0:L1] = np.sin(ang) * sgn[:, None]
    sc = np.float32(128.0 ** -0.5)
    return np.ascontiguousarray(np.stack([cos * sc, sin * sc, cos, sin]).astype(np.float32))


def na_table(rpb):
    H = rpb.shape[0]
    krl = np.arange(128) // 64
    kc = np.arange(128) % 64
    u = np.arange(16)
    qc = np.arange(64)
    dr = 14 + krl[:, None] - u[None, :]
    dc = np.clip(kc[:, None] - qc[None, :] + 15, 0, 30)
    cstart = np.clip(qc - 8, 0, 48)
    cmask = (kc[:, None] >= cstart[None, :]) & (kc[:, None] < cstart[None, :] + 16)
    drc = np.clip(dr, 0, 14)
    g = rpb[:, drc[:, :, None], dc[:, None, :]]
    out = np.full((H, 2, 128, 16, 64), NEG, np.float32)
    for t, (lo, hi) in enumerate(((3, 10), (0, 14))):
        valid = ((dr >= lo) & (dr <= hi))[:, :, None] & cmask[:, None, :]
        out[:, t] = np.where(valid[None], g, np.float32(NEG))
    return out


def prep_mixers(inp, b, layers=(0, 1, 2, 3)):
    m = {}
    if 0 in layers:
        w_in = inp['ml_w_in'][0]
        d = np.arange(128)
        perm = np.where((d % 64) < 32, d + 32, d - 32)
        cols = (np.arange(16)[:, None] * 128 + perm[None, :]).reshape(-1)
        m['ml_w_in'] = w_in
        m['ml_w_qkp'] = np.ascontiguousarray(w_in[:, cols])
        m['ml_w_gate'] = inp['ml_w_gate'][0]
        m['ml_bg'] = np.ascontiguousarray(inp['ml_b_gate'][0].reshape(32, 1))
        m['ml_hg'] = np.ascontiguousarray(inp['ml_head_g'][0].reshape(1, 2048))
        m['ml_w_out'] = inp['ml_w_out'][0]
        m['rope'] = rope_tables()
        m['tri'] = np.stack([np.triu(np.ones((128, 128), np.float32)), np.tril(np.ones((128, 128), np.float32))])
    if 1 in layers:
        m['na_w_qkv'] = inp['na_w_qkv'][0]
        m['na_g'] = np.ascontiguousarray(np.stack([inp['na_q_g'][0], inp['na_k_g'][0]], axis=1).astype(np.float32))
        m['na_tab'] = na_table(inp['na_rpb'][0])
        m['na_w_o'] = inp['na_w_o'][0]
    if 2 in layers:
        m['cv_w_pw1'] = inp['cv_w_pw1'][0]
        m['cv_dwT'] = np.ascontiguousarray(np.transpose(colT(inp['cv_dw'][0]), (0, 2, 1)))
        m['cv_vT'] = colT(np.stack([inp['cv_dw_b'][0], inp['cv_ln_g'][0], inp['cv_ln_b'][0]]))
        m['cv_w_pw2'] = inp['cv_w_pw2'][0]
    if 3 in layers:
        m['lr_w_in'] = inp['lr_w_in'][0]
        m['lr_cvT'] = colT(np.concatenate([inp['lr_conv'][0], inp['lr_conv_b']], axis=0))
        m['lr_w_gate'] = inp['lr_w_gate'][0]
        m['lr_bgT'] = colT(inp['lr_b_gate'][0])
        m['lr_lamT'] = colT(inp['lr_lambda'][0])
        m['lr_w_out'] = inp['lr_w_out'][0]
    return m
```

```python
import contextlib
import numpy as np
import concourse.bass as bass
import concourse.mybir as mybir
from concourse.bass_utils import run_bass_kernel_spmd

F32 = mybir.dt.float32
BF16 = mybir.dt.bfloat16
ALU = mybir.AluOpType
AF = mybir.ActivationFunctionType

D = 2048
KC = 16
DFF = 5632
FC = 44
PAD = 16
NCTX = 256
NLAT = 4096
C0 = PAD
C1 = C0 + NCTX
L0 = C1 + 2 * PAD
L1 = L0 + NLAT
TP = L1 + PAD
EPS = 1e-6
DEPTH = 4


class Res:
    __slots__ = ('name', 'w', 'r', 'multi')

    def __init__(self, name, multi=False):
        self.name = name
        self.w = {}
        self.r = {}
        self.multi = multi


class DSem:
    __slots__ = ('sem', 'val')

    def __init__(self, sem):
        self.sem = sem
        self.val = 0


class Buf:
    def __init__(self, ap, name):
        self.ap = ap
        self.res = Res(name)

    def __getitem__(self, idx):
        return self.ap[idx]


class Prog:
    QS = ('pe', 'dve', 'act', 'pool', 'sp')

    def __init__(self, nc, stack):
        self.nc = nc
        self.stack = stack
        self.q = {k: [] for k in self.QS}
        self.csem = {k: stack.enter_context(nc.semaphore('c_' + k)) for k in self.QS}
        self.cnt = {k: 0 for k in self.QS}
        self.pending = {k: False for k in self.QS}
        self.seen = {k: {} for k in self.QS}
        self.rings = {}
        self.ringpos = {}
        for q, n in (('sp', 40), ('pool', 16), ('act', 8)):
            self.rings[q] = [DSem(stack.enter_context(nc.semaphore('d_%s%d' % (q, i)))) for i in range(n)]
            self.ringpos[q] = 0
        self.ninstr = 0

    def sb(self, name, shape, dtype):
        return Buf(self.stack.enter_context(self.nc.sbuf_tensor(name, shape, dtype))[:], name)

    def ps(self, name, shape, dtype):
        return Buf(self.stack.enter_context(self.nc.psum_tensor(name, shape, dtype))[:], name)

    def op(self, q, fn, reads=(), writes=(), inc=True, dsem=None):
        deps = {}

        def add(d):
            for k, sv in d.items():
                if k not in deps or deps[k][1] < sv[1]:
                    deps[k] = sv

        for r in reads:
            add(r.w)
        for w in writes:
            add(w.w)
            add(w.r)
        if dsem is not None and dsem.val > 0:
            add({id(dsem.sem): (dsem.sem, dsem.val)})
        own = id(self.csem[q])
        if q == 'pe' and own in deps:
            del deps[own]
        seen = self.seen[q]
        waits = []
        for k, (s, v) in deps.items():
            if seen.get(k, 0) >= v:
                continue
            seen[k] = v
            waits.append((s, v))
        if dsem is not None:
            dsem.val += 16
            tick = (dsem.sem, dsem.val)
            incinfo = (dsem.sem, 16)
        else:
            tick = (self.csem[q], self.cnt[q] + 1)
            if inc:
                self.cnt[q] += 1
                incinfo = (self.csem[q], 1)
                self.pending[q] = False
            else:
                incinfo = None
                self.pending[q] = True
        k = id(tick[0])
        for r in reads:
            if k not in r.r or r.r[k][1] < tick[1]:
                r.r[k] = tick
        for w in writes:
            if w.multi:
                if k not in w.w or w.w[k][1] < tick[1]:
                    w.w[k] = tick
            else:
                w.w = {k: tick}
                w.r = {}
        self.q[q].append((waits, fn, incinfo))
        self.ninstr += 1

    def dma(self, q, out, in_, reads=(), writes=(), **kw):
        ring = self.rings[q]
        ds = ring[self.ringpos[q] % len(ring)]
        self.ringpos[q] += 1
        self.op(q, lambda e: e.dma_start(out=out, in_=in_, **kw), reads, writes, dsem=ds)

    def barrier(self):
        for q in self.QS:
            assert not self.pending[q], q
        targets = [(self.csem[k], self.cnt[k]) for k in self.QS if self.cnt[k] > 0]
        for ring in self.rings.values():
            targets += [(d.sem, d.val) for d in ring if d.val > 0]
        for q in self.QS:
            seen = self.seen[q]
            waits = []
            for (s, v) in targets:
                if q == 'pe' and s is self.csem['pe']:
                    continue
                if seen.get(id(s), 0) >= v:
                    continue
                seen[id(s)] = v
                waits.append((s, v))
            if waits:
                self.q[q].append((waits, None, None))

    def emit(self):
        nc = self.nc
        names = {'pe': 'tensor', 'dve': 'vector', 'act': 'scalar', 'pool': 'gpsimd', 'sp': 'sync'}
        with nc.Block() as block:
            for k in self.QS:
                lst = self.q[k]

                def body(eng, lst=lst):
                    for waits, fn, incinfo in lst:
                        for (s, v) in waits:
                            eng.wait_ge(s, v)
                        if fn is None:
                            continue
                        ins = fn(eng)
                        if incinfo is not None:
                            ins.then_inc(incinfo[0], incinfo[1])

                getattr(block, names[k])(body)

    def mm(self, out, lhsT, rhs, start, stop, reads, writes, inc):
        self.op('pe', lambda e: e.matmul(out, lhsT=lhsT, rhs=rhs, start=start, stop=stop), reads, writes, inc=inc)

    def tr(self, out, in_, ident, reads, writes, inc=True):
        self.op('pe', lambda e: e.transpose(out, in_, ident), reads, writes, inc=inc)

    def act(self, out, in_, func, reads, writes, **kw):
        self.op('act', lambda e: e.activation(out=out, in_=in_, func=func, **kw), reads, writes)

    def tt(self, q, out, in0, in1, op, reads, writes):
        self.op(q, lambda e: e.tensor_tensor(out=out, in0=in0, in1=in1, op=op), reads, writes)

    def ts(self, q, out, in0, s1, s2, op0, op1, reads, writes):
        if op1 is None:
            self.op(q, lambda e: e.tensor_scalar(out=out, in0=in0, scalar1=s1, scalar2=None, op0=op0), reads, writes)
        else:
            self.op(q, lambda e: e.tensor_scalar(out=out, in0=in0, scalar1=s1, scalar2=s2, op0=op0, op1=op1), reads, writes)

    def stt(self, q, out, in0, scalar, in1, op0, op1, reads, writes):
        self.op(q, lambda e: e.scalar_tensor_tensor(out=out, in0=in0, scalar=scalar, in1=in1, op0=op0, op1=op1), reads, writes)

    def copy(self, q, out, in_, reads, writes):
        if q == 'act_copy':
            self.op('act', lambda e: e.activation(out=out, in_=in_, func=AF.Copy), reads, writes)
        else:
            self.op(q, lambda e: e.tensor_copy(out=out, in_=in_), reads, writes)

    def memset(self, q, ap, val, writes):
        self.op(q, lambda e: e.memset(ap, val), (), writes)

    def recip(self, out, in_, reads, writes):
        self.op('dve', lambda e: e.reciprocal(out=out, in_=in_), reads, writes)


def colT(v):
    v = np.asarray(v, np.float32)
    F = v.shape[-1]
    r = v.reshape(v.shape[:-1] + (F // 128, 128))
    r = np.moveaxis(r, -1, 0)
    return np.ascontiguousarray(r)


def tiles_plain():
    t = [(C0, NCTX, 1)]
    for i in range(8):
        t.append((L0 + 512 * i, 512, 0))
    return t


def tiles_n(n, lat_only=False):
    t = [] if lat_only else [(C0, NCTX, 1)]
    s = 0
    while s < NLAT:
        w = min(n, NLAT - s)
        t.append((L0 + s, w, 0))
        s += w
    return t


class MK:
    def __init__(self, layers=(0, 1, 2, 3), debug=None, skip_mixer=False, last_layer=3):
        self.layers = layers
        self.debug = debug
        self.skip_mixer = skip_mixer
        self.last_layer = last_layer
        self.nc = bass.Bass("TRN2", target_bir_lowering=False)
        self.dr = {}

    def din(self, name, shape, dtype=F32):
        t = self.nc.dram_tensor(name, list(shape), dtype, kind="ExternalInput")
        self.dr[name] = t.ap()
        return self.dr[name]

    def dscr(self, name, shape, dtype):
        t = self.nc.dram_tensor(name, list(shape), dtype)
        a = t.ap()
        a_res = Res(name, multi=True)
        return a, a_res

    def build(self):
        nc = self.nc
        with contextlib.ExitStack() as stack:
            self.p = p = Prog(nc, stack)
            self.declare_io()
            self.alloc(stack)
            self.prologue()
            for l in self.layers:
                self.layer(l)
            self.epilogue_out()
            p.emit()
        return nc

    def declare_io(self):
        nc = self.nc
        self.xT_in = self.din('xT', [D, TP])
        self.cT = self.din('cT', [128, KC, 2])
        self.ada_w = self.din('ada_w', [DEPTH, D, 6 * D])
        self.abT = self.din('abT', [128, DEPTH, 96])
        self.ngT = self.din('ngT', [128, DEPTH, 2, KC])
        self.ffn_w_gu = self.din('ffn_w_gu', [DEPTH, D, 2 * DFF])
        self.ffn_cwT = self.din('ffn_cwT', [128, DEPTH, 3, FC])
        self.ffn_w_down = self.din('ffn_w_down', [DEPTH, DFF, D])
        self.in_res = Res('inputs', multi=True)
        self.ident_in = self.din('ident', [128, 128])
        self.declare_mixer_io()
        out = nc.dram_tensor('outT', [D, NLAT], F32, kind="ExternalOutput")
        self.outT = out.ap()
        self.out_res = Res('outT', multi=True)
        self.xT, self.xT_res = self.dscr('xT_s', [D, TP], F32)
        self.hT, self.hT_res = self.dscr('hT_s', [D, TP], BF16)
        self.HID, self.HID_res = self.dscr('hid_s', [10, 128, FC, 456], BF16)

    def declare_mixer_io(self):
        pass

    def alloc(self, stack):
        p = self.p
        self.ARENA = 172 * 1024
        self.arena = p.sb('arena', [128, self.ARENA // 4], F32)
        self.stage = [p.sb('stg%d' % i, [128, 512], F32) for i in range(8)]
        self.stage_i = 0
        self.psum = [p.ps('ps%d' % i, [128, 512], F32) for i in range(8)]
        self.psum_i = 0
        self.ident_f = p.sb('ident_f', [128, 128], F32)
        self.ident_b = p.sb('ident_b', [128, 128], BF16)
        self.ones_b = p.sb('ones_b', [128, 128], BF16)
        self.ones_f = p.sb('ones_f', [128, 128], F32)
        self.eps_t = p.sb('eps_t', [128, 1], F32)
        self.modT = p.sb('modT', [128, 96, 2], F32)
        self.lv = p.sb('lv', [128, 6, KC, 2], F32)
        self.ngs = p.sb('ngs', [128, DEPTH, 2, KC], F32)
        self.abs_ = p.sb('abs', [128, DEPTH, 96], F32)
        self.cws = p.sb('cws', [128, DEPTH, 3, FC], F32)
        self.scT = p.sb('scT', [128, KC, 2], F32)
        self.W = [self.aview('W%d' % i, i * 45056, [128, 22528], BF16) for i in range(2)]
        self.X = [self.aview('X%d' % i, 90112 + i * 40960, [128, 20480], BF16) for i in range(2)]
        self.WW = self.aview('WW', 0, [128, 45056], BF16)
        self.w_i = 0
        self.x_i = 0
        self.EXTRA = 90112 + 2 * 40960

    def aview(self, name, off, shape, dtype):
        nbytes = int(np.prod(shape[1:])) * (4 if dtype == F32 else 2)
        assert off % 4 == 0 and nbytes % 4 == 0 and off + nbytes <= self.ARENA, (name, off, nbytes)
        ap = self.arena.ap[:, off // 4:(off + nbytes) // 4]
        if dtype != F32:
            ap = ap.bitcast(dtype)
        if len(shape) == 3:
            ap = ap.rearrange('p (a b) -> p a b', a=shape[1])
        elif len(shape) == 4:
            ap = ap.rearrange('p (a b c) -> p a b c', a=shape[1], b=shape[2])
        return Buf(ap, name)

    def next_stage(self):
        b = self.stage[self.stage_i % len(self.stage)]
        self.stage_i += 1
        return b

    def next_psum(self):
        b = self.psum[self.psum_i % 8]
        self.psum_i += 1
        return b

    def prologue(self):
        p = self.p
        nc = self.nc
        p.memset('dve', self.ones_b[:], 1.0, [self.ones_b.res])
        p.memset('dve', self.ones_f[:], 1.0, [self.ones_f.res])
        p.memset('dve', self.eps_t[:], EPS, [self.eps_t.res])
        p.dma('sp', self.ident_f[:], self.ident_in, [self.in_res], [self.ident_f.res])
        p.copy('dve', self.ident_b[:], self.ident_f[:], [self.ident_f.res], [self.ident_b.res])
        p.dma('sp', self.ngs[:], self.ngT, [self.in_res], [self.ngs.res])
        p.dma('sp', self.abs_[:], self.abT, [self.in_res], [self.abs_.res])
        p.dma('sp', self.cws[:], self.ffn_cwT, [self.in_res], [self.cws.res])
        p.dma('sp', self.scT[:], self.cT, [self.in_res], [self.scT.res])
        p.act(self.scT[:], self.scT[:], AF.Silu, [self.scT.res], [self.scT.res])
        self.x_rd = (self.xT_in, self.in_res)
        zt = self.aview('zt', 80000, [128, KC, 2 * PAD], BF16)
        p.memset('dve', zt[:], 0.0, [zt.res])
        hs = self.hT.rearrange('(kc p) t -> p kc t', p=128)
        p.dma('sp', hs[:, :, 0:PAD], zt[:, :, 0:PAD], [zt.res], [self.hT_res])
        p.dma('sp', hs[:, :, C1:L0], zt[:, :, :], [zt.res], [self.hT_res])
        p.dma('sp', hs[:, :, L1:TP], zt[:, :, 0:PAD], [zt.res], [self.hT_res])
        p.barrier()

    def mods(self, l):
        p = self.p
        p.barrier()
        wb = [self.aview('aw%d' % i, i * 32768, [128, KC, 512], F32) for i in range(2)]
        ps = self.next_psum()
        psv = ps.ap[:, 0:192].rearrange('p (j s) -> p j s', s=2)
        for blk in range(24):
            b = wb[blk % 2]
            src = self.ada_w[l, :, blk * 512:(blk + 1) * 512].rearrange('(kc p) n -> p kc n', p=128)
            p.dma('sp', b[:], src, [self.in_res], [b.res])
            for jj in range(4):
                j = blk * 4 + jj
                for kc in range(KC):
                    p.mm(psv[:, j, :], b[:, kc, jj * 128:(jj + 1) * 128], self.scT[:, kc, :], kc == 0, kc == KC - 1,
                         [b.res, self.scT.res], [ps.res], inc=(kc == KC - 1))
        for s in range(2):
            p.tt('dve', self.modT[:, :, s], psv[:, :, s], self.abs_[:, l, :], ALU.add,
                 [ps.res, self.abs_.res], [self.modT.res])
        m = self.modT
        lv = self.lv
        for half in range(2):
            base = half * 48
            for s in range(2):
                p.stt('dve', lv[:, half * 3 + 0, :, s], m[:, base + 16:base + 32, s], 1.0, self.ngs[:, l, half, :],
                      ALU.add, ALU.mult, [m.res, self.ngs.res], [lv.res])
                p.copy('dve', lv[:, half * 3 + 1, :, s], m[:, base:base + 16, s], [m.res], [lv.res])
                p.copy('dve', lv[:, half * 3 + 2, :, s], m[:, base + 32:base + 48, s], [m.res], [lv.res])
        p.barrier()

    def norm(self, l, half, lat_only=False):
        p = self.p
        p.barrier()
        xin = [self.aview('nx%d' % i, i * 32768, [128, KC, 512], F32) for i in range(2)]
        sq = self.aview('nsq', 65536, [128, KC, 512], BF16)
        ob = [self.aview('nob%d' % i, 81920 + i * 16384, [128, KC, 512], BF16) for i in range(2)]
        rs = self.aview('nrs', 114688, [128, 512], F32)
        tmp = [self.aview('ntmp%d' % i, 116736 + i * 2048, [128, 512], F32) for i in range(4)]
        xs = self.x_rd[0].rearrange('(kc p) t -> p kc t', p=128)
        xs_res = self.x_rd[1]
        hs = self.hT.rearrange('(kc p) t -> p kc t', p=128)
        lv = self.lv
        tl = tiles_plain()
        if lat_only:
            tl = tl[1:]
        for ti, (c0, w, s) in enumerate(tl):
            xb = xin[ti % 2]
            o = ob[ti % 2]
            p.dma('pool', xb[:, :, 0:w], xs[:, :, c0:c0 + w], [xs_res], [xb.res])
            p.act(sq[:, :, 0:w], xb[:, :, 0:w], AF.Square, [xb.res], [sq.res])
            ps = self.next_psum()
            for kc in range(KC):
                p.mm(ps[:, 0:w], self.ones_b[:], sq[:, kc, 0:w], kc == 0, kc == KC - 1, [sq.res, self.ones_b.res],
                     [ps.res], inc=(kc == KC - 1))
            p.act(rs[:, 0:w], ps[:, 0:w], AF.Sqrt, [ps.res, self.eps_t.res], [rs.res], bias=self.eps_t[:, 0:1],
                  scale=1.0 / D)
            p.recip(rs[:, 0:w], rs[:, 0:w], [rs.res], [rs.res])
            for kc in range(KC):
                t = tmp[kc % 4]
                p.tt('dve', t[:, 0:w], xb[:, kc, 0:w], rs[:, 0:w], ALU.mult, [xb.res, rs.res], [t.res])
                p.act(o[:, kc, 0:w], t[:, 0:w], AF.Identity, [t.res, lv.res], [o.res],
                      scale=lv[:, half * 3 + 0, kc, s:s + 1], bias=lv[:, half * 3 + 1, kc, s:s + 1])
            p.dma('sp', hs[:, :, c0:c0 + w], o[:, :, 0:w], [o.res], [self.hT_res])
        p.barrier()

    def linear(self, xsrc, xres, Kc, wblocks, tiles, epi, xload=None, defer=False, wide=False):
        p = self.p
        pend = None
        wbufs = {}
        xbufs = {}

        def load_w(bi):
            blk = wblocks[bi]
            ncols = sum(n for _, n in blk['loads'])
            if wide:
                assert Kc * ncols <= 45056, (Kc, ncols)
                wres = [self.W[0].res, self.W[1].res]
                wv = self.WW.ap[:, 0:Kc * ncols].rearrange('p (kc n) -> p kc n', kc=Kc)
            else:
                wb = self.W[self.w_i % 2]
                self.w_i += 1
                assert Kc * ncols <= 22528, (Kc, ncols)
                wres = [wb.res]
                wv = wb.ap[:, 0:Kc * ncols].rearrange('p (kc n) -> p kc n', kc=Kc)
            off = 0
            for (src, n) in blk['loads']:
                p.dma('pool', wv[:, :, off:off + n], src.rearrange('(kc p) n -> p kc n', p=128),
                      [self.in_res], wres)
                off += n
            wbufs[bi] = (wres, wv)

        def load_x(bi, ti):
            tile = tiles[ti]
            c0, w, s, hl, hr = tile
            wt = w + hl + hr
            xb = self.X[self.x_i % 2]
            self.x_i += 1
            assert Kc * wt <= 20480
            xv = xb.ap[:, 0:Kc * wt].rearrange('p (kc t) -> p kc t', kc=Kc)
            if xload is not None:
                xload(ti, tile, xv, xb)
            else:
                p.dma('pool', xv, xsrc[:, c0 - hl:c0 + w + hr].rearrange('(kc p) t -> p kc t', p=128),
                      [xres], [xb.res])
            xbufs[(bi, ti)] = (xb, xv)

        its = [(bi, ti) for bi in range(len(wblocks)) for ti in range(len(tiles))]
        load_w(0)
        load_x(*its[0])
        for n, (bi, ti) in enumerate(its):
            if n + 1 < len(its):
                load_x(*its[n + 1])
            if bi + 1 < len(wblocks) and not wide and ti == 0:
                load_w(bi + 1)
            blk = wblocks[bi]
            wres, wv = wbufs[bi]
            xb, xv = xbufs.pop((bi, ti))
            tile = tiles[ti]
            c0, w, s, hl, hr = tile
            wt = w + hl + hr
            for gi, grp in enumerate(blk['groups']):
                pss = []
                for (co, cw) in grp:
                    ps = self.next_psum()
                    for kc in range(Kc):
                        p.mm(ps[0:cw, 0:wt], wv[:, kc, co:co + cw], xv[:, kc, :], kc == 0, kc == Kc - 1,
                             wres + [xb.res], [ps.res], inc=(kc == Kc - 1))
                    pss.append(ps)
                if defer:
                    if pend is not None:
                        epi(*pend)
                    pend = (bi, gi, ti, tile, pss)
                else:
                    epi(bi, gi, ti, tile, pss)
            if wide and bi + 1 < len(wblocks) and ti == len(tiles) - 1:
                load_w(bi + 1)
        if pend is not None:
            epi(*pend)

    def epi_residual(self, gate_kind, chunk_of, to_out=False):
        p = self.p
        src, src_res = self.x_rd
        if to_out:
            dst, dst_res, doff = self.outT, self.out_res, -L0
        else:
            dst, dst_res, doff = self.xT, self.xT_res, 0
        cnt = [0]

        def epi(bi, gi, ti, tile, pss):
            c0, w, s, hl, hr = tile
            dc = chunk_of(bi, gi)
            ps = pss[0]
            st = self.next_stage()
            p.dma('pool', st[:, 0:w], src[dc * 128:(dc + 1) * 128, c0:c0 + w], [src_res], [st.res])
            p.stt('dve', st[:, 0:w], ps[:, hl:hl + w], self.lv[:, gate_kind, dc, s:s + 1], st[:, 0:w], ALU.mult, ALU.add,
                  [ps.res, st.res, self.lv.res], [st.res])
            p.dma('sp', dst[dc * 128:(dc + 1) * 128, c0 + doff:c0 + doff + w], st[:, 0:w], [st.res], [dst_res])

        return epi

    def ffn(self, l, lat_only=False, final=False):
        p = self.p
        base = tiles_n(456, lat_only)
        tiles = [(c0, w, s, 1, 1) for (c0, w, s) in base]
        toff = 1 if lat_only else 0
        wblocks = []
        f = 0
        while f < FC:
            nf = 1 if f == 0 else min(5, FC - f)
            wblocks.append(dict(
                loads=[(self.ffn_w_gu[l, :, f * 128:(f + nf) * 128], nf * 128),
                       (self.ffn_w_gu[l, :, DFF + f * 128:DFF + (f + nf) * 128], nf * 128)],
                groups=[[(j * 128, 128), (nf * 128 + j * 128, 128)] for j in range(nf)], f0=f))
            f += nf
        cws = self.cws

        def epi_up(bi, gi, ti, tile, pss):
            c0, w, s, hl, hr = tile
            fch = wblocks[bi]['f0'] + gi
            pg, pu = pss
            t1 = self.next_stage()
            sg = self.next_stage()
            hb = self.next_stage()
            hbv = hb.ap.bitcast(BF16)
            p.ts('dve', t1[:, 0:w], pg[:, 1:w + 1], cws[:, l, 1, fch:fch + 1], None, ALU.mult, None,
                 [pg.res, cws.res], [t1.res])
            p.stt('dve', t1[:, 0:w], pg[:, 0:w], cws[:, l, 0, fch:fch + 1], t1[:, 0:w], ALU.mult, ALU.add,
                  [pg.res, cws.res, t1.res], [t1.res])
            p.stt('dve', t1[:, 0:w], pg[:, 2:w + 2], cws[:, l, 2, fch:fch + 1], t1[:, 0:w], ALU.mult, ALU.add,
                  [pg.res, cws.res, t1.res], [t1.res])
            p.act(sg[:, 0:w], t1[:, 0:w], AF.Silu, [t1.res], [sg.res])
            p.tt('dve', hbv[:, 0:w], sg[:, 0:w], pu[:, 1:w + 1], ALU.mult, [sg.res, pu.res], [hb.res])
            p.dma('sp', self.HID[ti + toff, :, fch, 0:w], hbv[:, 0:w], [hb.res], [self.HID_res])

        self.linear(self.hT, self.hT_res, KC, wblocks, tiles, epi_up)
        dtiles = [(c0, w, s, 0, 0) for (c0, w, s) in base]
        dblocks = []
        for (b0, nb) in ((0, 8), (8, 8)):
            dblocks.append(dict(loads=[(self.ffn_w_down[l, :, b0 * 128:(b0 + nb) * 128], nb * 128)],
                                groups=[[(j * 128, 128)] for j in range(nb)], c0=b0))

        def xload(ti, tile, xv, xb):
            c0, w, s, hl, hr = tile
            p.dma('pool', xv, self.HID[ti + toff, :, :, 0:w], [self.HID_res], [xb.res])

        self.linear(None, None, FC, dblocks, dtiles, self.epi_residual(5, lambda bi, gi: dblocks[bi]['c0'] + gi, to_out=final), xload=xload, wide=True)

    def layer(self, l):
        last = (l == self.last_layer)
        self.mods(l)
        if not self.skip_mixer:
            self.norm(l, 0)
            self.mixer(l, last)
        self.norm(l, 1, lat_only=last)
        self.ffn(l, lat_only=last, final=(last and l == self.layers[-1]))
        if self.skip_mixer:
            self.x_rd = (self.xT, self.xT_res)

    def mixer(self, l, last):
        raise NotImplementedError

    def epilogue_out(self):
        p = self.p
        p.barrier()
        if not (self.layers[-1] == self.last_layer):
            xs = self.xT.rearrange('(kc p) t -> p kc t', p=128)
            os_ = self.outT.rearrange('(kc p) t -> p kc t', p=128)
            tb = [self.aview('ocp%d' % i, i * 32768, [128, KC, 512], F32) for i in range(2)]
            for i in range(8):
                b = tb[i % 2]
                p.dma('sp', b[:], xs[:, :, L0 + i * 512:L0 + (i + 1) * 512], [self.xT_res], [b.res])
                p.dma('sp', os_[:, :, i * 512:(i + 1) * 512], b[:], [b.res], [self.out_res])
            p.barrier()

    def debug_out(self):
        pass


def prep_common(inp, b):
    xT = np.zeros((D, TP), np.float32)
    xT[:, C0:C1] = inp['ctx'][b].T
    xT[:, L0:L1] = inp['x'][b].T
    cT = np.stack([colT(inp['c'][b]), colT(inp['c_ctx'])], axis=-1)
    m = {
        'xT': xT,
        'cT': np.ascontiguousarray(cT),
        'ada_w': inp['ada_w'],
        'abT': colT(inp['ada_b']),
        'ngT': colT(np.stack([inp['norm_mix'], inp['norm_ffn']], axis=1)),
        'ffn_w_gu': inp['ffn_w_gu'],
        'ffn_cwT': colT(inp['ffn_conv']),
        'ffn_w_down': inp['ffn_w_down'],
        'ident': np.eye(128, dtype=np.float32),
    }
    return m


def kernel(**inputs):
    inp = {k: np.asarray(v) for k, v in inputs.items()}
    mk = MKFull()
    nc = mk.build()
    in_maps = []
    for b in range(8):
        m = prep_common(inp, b)
        m.update(prep_mixers(inp, b))
        in_maps.append(m)
    res = run_bass_kernel_spmd(nc, in_maps, core_ids=list(range(8)))
    out = np.stack([np.ascontiguousarray(r['outT'].T) for r in res.results], axis=0)
    return out.astype(np.float32)


class MKFull(MK):
    def declare_mixer_io(self):
        din = self.din
        L = self.layers
        if 0 in L:
            self.ml_w_in = din('ml_w_in', [D, 6144])
            self.ml_w_qkp = din('ml_w_qkp', [D, 2048])
            self.ml_w_gate = din('ml_w_gate', [D, 32])
            self.ml_bg = din('ml_bg', [32, 1])
            self.ml_hg = din('ml_hg', [1, 2048])
            self.ml_w_out = din('ml_w_out', [D, D])
            self.rope = din('rope', [4, 128, TP])
            self.tri = din('tri', [2, 128, 128])
        if 1 in L:
            self.na_w_qkv = din('na_w_qkv', [D, 6144])
            self.na_g = din('na_g', [128, 2])
            self.na_tab = din('na_tab', [16, 2, 128, 16, 64])
            self.na_w_o = din('na_w_o', [D, D])
        if 2 in L:
            self.cv_w_pw1 = din('cv_w_pw1', [D, 4096])
            self.cv_dwT = din('cv_dwT', [128, 16, 31])
            self.cv_vT = din('cv_vT', [128, 3, 16])
            self.cv_w_pw2 = din('cv_w_pw2', [D, D])
        if 3 in L:
            self.lr_w_in = din('lr_w_in', [D, 4096])
            self.lr_cvT = din('lr_cvT', [128, 5, 16])
            self.lr_w_gate = din('lr_w_gate', [4, 8, 256, 256])
            self.lr_bgT = din('lr_bgT', [128, 4, 16])
            self.lr_lamT = din('lr_lamT', [128, 2, 16])
            self.lr_w_out = din('lr_w_out', [D, D])
        self.S = {}
        self.Sres = {}
        for n in ('S1', 'S2', 'S3', 'S4', 'S5'):
            self.S[n], self.Sres[n] = self.dscr(n, [D, TP], BF16)
        self.F1, self.F1_res = self.dscr('F1', [D, TP], F32)

    def prologue(self):
        MK.prologue(self)
        p = self.p
        zt = self.aview('zt2', 80000, [128, KC, 2 * PAD], BF16)
        p.memset('dve', zt[:], 0.0, [zt.res])
        hs = self.S['S1'].rearrange('(kc p) t -> p kc t', p=128)
        p.dma('sp', hs[:, :, 0:PAD], zt[:, :, 0:PAD], [zt.res], [self.Sres['S1']])
        p.dma('sp', hs[:, :, C1:L0], zt[:, :, :], [zt.res], [self.Sres['S1']])
        p.dma('sp', hs[:, :, L1:TP], zt[:, :, 0:PAD], [zt.res], [self.Sres['S1']])
        p.barrier()

    def mixer(self, l, last):
        [self.mixer_ml, self.mixer_na, self.mixer_cv, self.mixer_lr][l % 4](l, last)

    def bfstage(self):
        st = self.next_stage()
        return st, st.ap.bitcast(BF16)

    def plain(self, lat_only=False):
        t = [(c0, w, s, 0, 0) for (c0, w, s) in tiles_plain()]
        return t[1:] if lat_only else t

    def out_proj(self, w_ap, src, last):
        blocks = [dict(loads=[(w_ap[:, 0:1024], 1024), (w_ap[:, 1024:2048], 1024)],
                       groups=[[(j * 128, 128)] for j in range(16)], c0=0)]
        self.linear(self.S[src], self.Sres[src], KC, blocks, self.plain(last),
                    self.epi_residual(2, lambda bi, gi: gi), wide=True)
        self.x_rd = (self.xT, self.xT_res)

    def mixer_cv(self, l, last):
        p = self.p
        S1, S1r = self.S['S1'], self.Sres['S1']
        S2, S2r = self.S['S2'], self.Sres['S2']
        blocks = []
        j = 0
        while j < 16:
            nf = min(5, 16 - j)
            blocks.append(dict(loads=[(self.cv_w_pw1[:, j * 128:(j + nf) * 128], nf * 128),
                                      (self.cv_w_pw1[:, D + j * 128:D + (j + nf) * 128], nf * 128)],
                               groups=[[(i * 128, 128), (nf * 128 + i * 128, 128)] for i in range(nf)], f0=j))
            j += nf

        def epiA(bi, gi, ti, tile, pss):
            c0, w, s, hl, hr = tile
            jc = blocks[bi]['f0'] + gi
            pa, pg = pss
            sg = self.next_stage()
            hb, hbv = self.bfstage()
            p.act(sg[:, 0:w], pg[:, 0:w], AF.Sigmoid, [pg.res], [sg.res])
            p.tt('dve', hbv[:, 0:w], pa[:, 0:w], sg[:, 0:w], ALU.mult, [pa.res, sg.res], [hb.res])
            p.dma('sp', S1[jc * 128:(jc + 1) * 128, c0:c0 + w], hbv[:, 0:w], [hb.res], [S1r])

        self.linear(self.hT, self.hT_res, KC, blocks, self.plain(), epiA)
        p.barrier()
        dg = [self.aview('dg%d' % i, i * 8192, [128, 31, 128], BF16) for i in range(2)]
        xt = [self.aview('cxt%d' % i, 16384 + i * 2048, [128, 544], BF16) for i in range(4)]
        dwT = self.aview('dwT', 24576, [128, 16, 31], F32)
        vT = self.aview('vT', 28672, [128, 3, 16], F32)
        p.dma('sp', dwT[:], self.cv_dwT, [self.in_res], [dwT.res])
        p.dma('sp', vT[:], self.cv_vT, [self.in_res], [vT.res])
        n = 0
        for jc in range(16):
            d = dg[jc % 2]
            for k in range(31):
                if k % 2 == 0:
                    p.ts('dve', d[:, k, :], self.ident_b[:], dwT[:, jc, k:k + 1], None, ALU.mult, None,
                         [self.ident_b.res, dwT.res], [d.res])
                else:
                    p.act(d[:, k, :], self.ident_b[:], AF.Copy, [self.ident_b.res, dwT.res], [d.res],
                          scale=dwT[:, jc, k:k + 1])
            for (c0, w, s, _, _) in self.plain():
                x = xt[n % 4]
                n += 1
                p.dma('pool', x[:, 0:w + 30], S1[jc * 128:(jc + 1) * 128, c0 - 15:c0 + w + 15], [S1r], [x.res])
                ps = self.next_psum()
                for k in range(31):
                    p.mm(ps[:, 0:w], d[:, k, :], x[:, k:k + w], k == 0, k == 30, [d.res, x.res], [ps.res], inc=(k == 30))
                st = self.next_stage()
                p.act(st[:, 0:w], ps[:, 0:w], AF.Identity, [ps.res, vT.res], [st.res], bias=vT[:, 0, jc:jc + 1], scale=1.0)
                p.dma('sp', self.F1[jc * 128:(jc + 1) * 128, c0:c0 + w], st[:, 0:w], [st.res], [self.F1_res])
        p.barrier()
        yin = [self.aview('ly%d' % i, i * 32768, [128, KC, 512], F32) for i in range(2)]
        sq = self.aview('lsq', 65536, [128, KC, 512], BF16)
        ob = [self.aview('lob%d' % i, 81920 + i * 16384, [128, KC, 512], BF16) for i in range(2)]
        sm = [self.aview('lsm%d' % i, 114688 + i * 2048, [128, 512], F32) for i in range(7)]
        vT = self.aview('vT2', 131072, [128, 3, 16], F32)
        p.dma('sp', vT[:], self.cv_vT, [self.in_res], [vT.res])
        mean, m2, rstd = sm[0], sm[1], sm[2]
        tmp = sm[3:7]
        ys = self.F1.rearrange('(kc p) t -> p kc t', p=128)
        os_ = S2.rearrange('(kc p) t -> p kc t', p=128)
        for ti, (c0, w, s, _, _) in enumerate(self.plain()):
            y = yin[ti % 2]
            o = ob[ti % 2]
            p.dma('pool', y[:, :, 0:w], ys[:, :, c0:c0 + w], [self.F1_res], [y.res])
            p.act(sq[:, :, 0:w], y[:, :, 0:w], AF.Square, [y.res], [sq.res])
            ps1 = self.next_psum()
            ps2 = self.next_psum()
            for kc in range(KC):
                p.mm(ps1[:, 0:w], self.ones_f[:], y[:, kc, 0:w], kc == 0, kc == KC - 1, [y.res, self.ones_f.res],
                     [ps1.res], inc=(kc == KC - 1))
            for kc in range(KC):
                p.mm(ps2[:, 0:w], self.ones_b[:], sq[:, kc, 0:w], kc == 0, kc == KC - 1, [sq.res, self.ones_b.res],
                     [ps2.res], inc=(kc == KC - 1))
            p.act(mean[:, 0:w], ps1[:, 0:w], AF.Copy, [ps1.res], [mean.res], scale=1.0 / D)
            p.tt('dve', m2[:, 0:w], mean[:, 0:w], mean[:, 0:w], ALU.mult, [mean.res], [m2.res])
            p.stt('dve', m2[:, 0:w], ps2[:, 0:w], 1.0 / D, m2[:, 0:w], ALU.mult, ALU.subtract, [ps2.res, m2.res], [m2.res])
            p.act(rstd[:, 0:w], m2[:, 0:w], AF.Sqrt, [m2.res, self.eps_t.res], [rstd.res], bias=self.eps_t[:, 0:1], scale=1.0)
            p.recip(rstd[:, 0:w], rstd[:, 0:w], [rstd.res], [rstd.res])
            for kc in range(KC):
                t = tmp[kc % 4]
                p.tt('dve', t[:, 0:w], y[:, kc, 0:w], mean[:, 0:w], ALU.subtract, [y.res, mean.res], [t.res])
                p.tt('dve', t[:, 0:w], t[:, 0:w], rstd[:, 0:w], ALU.mult, [t.res, rstd.res], [t.res])
                p.act(o[:, kc, 0:w], t[:, 0:w], AF.Silu, [t.res, vT.res], [o.res], scale=vT[:, 1, kc:kc + 1],
                      bias=vT[:, 2, kc:kc + 1])
            p.dma('sp', os_[:, :, c0:c0 + w], o[:, :, 0:w], [o.res], [S2r])
        p.barrier()
        self.out_proj(self.cv_w_pw2, 'S2', last)

    def mixer_lr(self, l, last):
        p = self.p
        S1, S1r = self.S['S1'], self.Sres['S1']
        S2, S2r = self.S['S2'], self.Sres['S2']
        S3, S3r = self.S['S3'], self.Sres['S3']
        cv = self.aview('lrcv', self.EXTRA, [128, 5, 16], F32)
        p.dma('sp', cv[:], self.lr_cvT, [self.in_res], [cv.res])
        tiles = [(c0, w, s, 2, 1) for (c0, w, s) in tiles_n(456)]
        blocks = [dict(loads=[(self.lr_w_in[:, b * 1024:(b + 1) * 1024], 1024)], groups=[[(j * 128, 128)] for j in range(8)])
                  for b in range(4)]

        def epiA(bi, gi, ti, tile, pss):
            c0, w, s, hl, hr = tile
            ps = pss[0]
            if bi < 2:
                c = bi * 8 + gi
                st, stv = self.bfstage()
                p.act(stv[:, 0:w], ps[:, hl:hl + w], AF.Gelu_apprx_tanh, [ps.res], [st.res])
                p.dma('sp', S2[c * 128:(c + 1) * 128, c0:c0 + w], stv[:, 0:w], [st.res], [S2r])
            else:
                c = (bi - 2) * 8 + gi
                t = self.next_stage()
                tb, tbv = self.bfstage()
                p.ts('dve', t[:, 0:w], ps[:, 0:w], cv[:, 0, c:c + 1], cv[:, 4, c:c + 1], ALU.mult, ALU.add,
                     [ps.res, cv.res], [t.res])
                for k in range(1, 4):
                    p.stt('dve', t[:, 0:w], ps[:, k:k + w], cv[:, k, c:c + 1], t[:, 0:w], ALU.mult, ALU.add,
                          [ps.res, cv.res, t.res], [t.res])
                p.act(tbv[:, 0:w], t[:, 0:w], AF.Copy, [t.res], [tb.res])
                p.dma('sp', self.F1[c * 128:(c + 1) * 128, c0:c0 + w], t[:, 0:w], [t.res], [self.F1_res])
                p.dma('sp', S1[c * 128:(c + 1) * 128, c0:c0 + w], tbv[:, 0:w], [tb.res], [S1r])

        self.linear(self.hT, self.hT_res, KC, blocks, tiles, epiA)
        p.barrier()
        SEQ = TP * 4
        A = [self.aview('lrA%d' % d, (2 * d) * SEQ, [128, TP], F32) for d in range(2)]
        Bv = [self.aview('lrB%d' % d, (2 * d + 1) * SEQ, [128, TP], F32) for d in range(2)]
        ubf = self.aview('lrubf', 4 * SEQ, [128, 2, TP], BF16)
        uf = self.aview('lruf', 5 * SEQ, [128, TP], F32)
        gg = self.aview('lrgg', 6 * SEQ, [128, TP], BF16)
        ob = self.aview('lrob', 6 * SEQ + TP * 2, [128, TP], BF16)
        o2 = 7 * SEQ
        wg = [self.aview('lrwg%d' % i, o2 + i * 2048, [128, 4, 2, 128], BF16) for i in range(2)]
        bgT = self.aview('lrbg', o2 + 4096, [128, 4, 16], F32)
        lam = self.aview('lrlam', o2 + 4096 + 256, [128, 2, 16], F32)
        kap = self.aview('lrkap', o2 + 4096 + 512, [128, 2, 2, 16], F32)
        p.dma('sp', bgT[:], self.lr_bgT, [self.in_res], [bgT.res])
        p.dma('sp', lam[:], self.lr_lamT, [self.in_res], [lam.res])
        p.act(lam[:], lam[:], AF.Exp, [lam.res], [lam.res], scale=-1.0)
        p.act(lam[:], lam[:], AF.Ln, [lam.res, self.ones_f.res], [lam.res], bias=self.ones_f[:, 0:1], scale=1.0)
        p.ts('dve', kap[:, 0, :, :], lam[:], -LRU_C, None, ALU.mult, None, [lam.res], [kap.res])
        p.ts('dve', kap[:, 1, :, :], lam[:], -2.0 * LRU_C, None, ALU.mult, None, [lam.res], [kap.res])
        pl = self.plain()

        def rev(b, a0, a1):
            return b.ap[:, a0:a1][:, ::-1]

        for c in range(16):
            nb, jo = c // 2, c % 2
            w = wg[c % 2]
            for t in range(4):
                p.dma('pool', w[:, t, :, :],
                      self.lr_w_gate[t, nb, :, jo * 128:(jo + 1) * 128].rearrange('(ki p) j -> p ki j', p=128),
                      [self.in_res], [w.res])
            if jo == 0:
                p.dma('sp', ubf[:], S1[nb * 256:(nb + 1) * 256, :].rearrange('(ki p) t -> p ki t', p=128), [S1r], [ubf.res])
            p.dma('sp', uf[:], self.F1[c * 128:(c + 1) * 128, :], [self.F1_res], [uf.res])
            p.dma('sp', gg[:], S2[c * 128:(c + 1) * 128, :], [S2r], [gg.res])
            for (c0, w_, s, _, _) in pl:
                pss = [self.next_psum() for _ in range(4)]
                for t in range(4):
                    for ki in range(2):
                        p.mm(pss[t][:, 0:w_], w[:, t, ki, :], ubf[:, ki, c0:c0 + w_], ki == 0, ki == 1,
                             [w.res, ubf.res], [pss[t].res], inc=(ki == 1))
                rr = [self.next_stage() for _ in range(2)]
                ii = [self.next_stage() for _ in range(2)]
                r2s = [self.next_stage() for _ in range(2)]
                for d in range(2):
                    p.act(rr[d][:, 0:w_], pss[2 * d][:, 0:w_], AF.Sigmoid, [pss[2 * d].res, bgT.res], [rr[d].res],
                          bias=bgT[:, 2 * d, c:c + 1], scale=1.0)
                    p.act(ii[d][:, 0:w_], pss[2 * d + 1][:, 0:w_], AF.Sigmoid, [pss[2 * d + 1].res, bgT.res], [ii[d].res],
                          bias=bgT[:, 2 * d + 1, c:c + 1], scale=1.0)
                for d in range(2):
                    p.act(A[d][:, c0:c0 + w_], rr[d][:, 0:w_], AF.Exp, [rr[d].res, kap.res], [A[d].res],
                          scale=kap[:, 0, d, c:c + 1])
                    p.tt('dve', r2s[d][:, 0:w_], A[d][:, c0:c0 + w_], A[d][:, c0:c0 + w_], ALU.mult, [A[d].res], [r2s[d].res])
                for d in range(2):
                    p.act(r2s[d][:, 0:w_], r2s[d][:, 0:w_], AF.Sqrt, [r2s[d].res, self.ones_f.res], [r2s[d].res], scale=-1.0,
                          bias=self.ones_f[:, 0:1])
                    p.tt('pool', ii[d][:, 0:w_], ii[d][:, 0:w_], r2s[d][:, 0:w_], ALU.mult, [ii[d].res, r2s[d].res], [ii[d].res])
                    p.tt('dve', Bv[d][:, c0:c0 + w_], ii[d][:, 0:w_], uf[:, c0:c0 + w_], ALU.mult, [ii[d].res, uf.res], [Bv[d].res])

            def scan(out, a, b, init, reads, writes):
                p.op('dve', lambda e: e.tensor_tensor_scan(out=out, data0=a, data1=b, initial=init, op0=ALU.mult,
                                                          op1=ALU.add), reads, writes)

            rw = ([A[0].res, Bv[0].res], [Bv[0].res])
            scan(Bv[0][:, C0:C1], A[0][:, C0:C1], Bv[0][:, C0:C1], 0.0, *rw)
            scan(Bv[0][:, L0:L1], A[0][:, L0:L1], Bv[0][:, L0:L1], Bv[0][:, C1 - 1:C1], *rw)
            rw = ([A[1].res, Bv[1].res], [Bv[1].res])
            scan(rev(Bv[1], C0, C1), rev(A[1], C0, C1), rev(Bv[1], C0, C1), 0.0, *rw)
            scan(rev(Bv[1], L0, L1), rev(A[1], L0, L1), rev(Bv[1], L0, L1), Bv[1][:, C0:C0 + 1], *rw)
            segs = [(L0 + i * 1024, L0 + (i + 1) * 1024) for i in range(4)]
            if not last:
                segs = [(C0, C1)] + segs
            for (a0, a1) in segs:
                p.tt('dve', Bv[0][:, a0:a1], Bv[0][:, a0:a1], Bv[1][:, a0:a1], ALU.add, [Bv[0].res, Bv[1].res], [Bv[0].res])
                p.tt('dve', ob[:, a0:a1], Bv[0][:, a0:a1], gg[:, a0:a1], ALU.mult, [Bv[0].res, gg.res], [ob.res])
            if not last:
                p.dma('sp', S3[c * 128:(c + 1) * 128, C0:C1], ob[:, C0:C1], [ob.res], [S3r])
            p.dma('sp', S3[c * 128:(c + 1) * 128, L0:L1], ob[:, L0:L1], [ob.res], [S3r])
        p.barrier()
        self.out_proj(self.lr_w_out, 'S3', last)


    def mixer_ml(self, l, last):
        p = self.p
        S1, S1r = self.S['S1'], self.Sres['S1']
        S2, S2r = self.S['S2'], self.Sres['S2']
        S3, S3r = self.S['S3'], self.Sres['S3']
        S4, S4r = self.S['S4'], self.Sres['S4']
        S5, S5r = self.S['S5'], self.Sres['S5']
        bg = self.aview('mlbg', self.EXTRA, [128, 1], F32)
        p.dma('sp', bg[0:32, :], self.ml_bg, [self.in_res], [bg.res])
        blocks = []
        for which in range(2):
            for (h0, nh) in ((0, 5), (5, 3)):
                cb = which * 1024 + h0 * 128
                blocks.append(dict(kind='qk', which=which, f0=h0,
                                   loads=[(self.ml_w_in[:, cb:cb + nh * 128], nh * 128),
                                          (self.ml_w_qkp[:, cb:cb + nh * 128], nh * 128)],
                                   groups=[[(i * 128, 128), (nh * 128 + i * 128, 128)] for i in range(nh)]))
        for b in range(2):
            blocks.append(dict(kind='v', f0=b * 8, loads=[(self.ml_w_in[:, 2048 + b * 1024:2048 + (b + 1) * 1024], 1024)],
                               groups=[[(i * 128, 128)] for i in range(8)]))
        blocks.append(dict(kind='o', f0=0, loads=[(self.ml_w_in[:, 4096:5120], 1024)],
                           groups=[[(i * 128, 128)] for i in range(8)]))
        blocks.append(dict(kind='o', f0=8, loads=[(self.ml_w_in[:, 5120:6144], 1024), (self.ml_w_gate[:, 0:32], 32)],
                           groups=[[(i * 128, 128)] for i in range(8)] + [[(1024, 32)]]))

        def epiA(bi, gi, ti, tile, pss):
            c0, w, s, hl, hr = tile
            blk = blocks[bi]
            if blk['kind'] == 'qk':
                which = blk['which']
                h = blk['f0'] + gi
                pa, pb = pss
                ct = self.next_stage()
                sn = self.next_stage()
                p.dma('pool', ct[:, 0:w], self.rope[2 * which, :, c0:c0 + w], [self.in_res], [ct.res])
                p.dma('pool', sn[:, 0:w], self.rope[2 * which + 1, :, c0:c0 + w], [self.in_res], [sn.res])
                p.tt('dve', ct[:, 0:w], pa[:, 0:w], ct[:, 0:w], ALU.mult, [pa.res, ct.res], [ct.res])
                p.tt('dve', sn[:, 0:w], pb[:, 0:w], sn[:, 0:w], ALU.mult, [pb.res, sn.res], [sn.res])
                ob, obv = self.bfstage()
                p.tt('dve', obv[:, 0:w], ct[:, 0:w], sn[:, 0:w], ALU.add, [ct.res, sn.res], [ob.res])
                dst, dr = (S1, S1r) if which == 0 else (S2, S2r)
                p.dma('sp', dst[h * 128:(h + 1) * 128, c0:c0 + w], obv[:, 0:w], [ob.res], [dr])
            elif blk['kind'] == 'v':
                c = blk['f0'] + gi
                ob, obv = self.bfstage()
                p.act(obv[:, 0:w], pss[0][:, 0:w], AF.Copy, [pss[0].res], [ob.res])
                p.dma('sp', S3[c * 128:(c + 1) * 128, c0:c0 + w], obv[:, 0:w], [ob.res], [S3r])
            else:
                ps = pss[0]
                if gi < 8:
                    c = blk['f0'] + gi
                    ob, obv = self.bfstage()
                    p.act(obv[:, 0:w], ps[:, 0:w], AF.Sigmoid, [ps.res], [ob.res])
                    p.dma('sp', S4[c * 128:(c + 1) * 128, c0:c0 + w], obv[:, 0:w], [ob.res], [S4r])
                else:
                    st = self.next_stage()
                    p.act(st[0:32, 0:w], ps[0:32, 0:w], AF.Identity, [ps.res, bg.res], [st.res], bias=bg[0:32, 0:1], scale=1.0)
                    p.dma('sp', self.F1[0:32, c0:c0 + w], st[0:32, 0:w], [st.res], [self.F1_res])

        self.linear(self.hT, self.hT_res, KC, blocks, self.plain(), epiA)
        p.barrier()
        gT = self.aview('mlgT', 0, [128, TP], F32)
        qT = self.aview('mlq', 17664, [128, TP], BF16)
        kT = self.aview('mlk', 26496, [128, TP], BF16)
        vT = self.aview('mlv', 35328, [128, 2, TP], BF16)
        so = self.aview('mlso', 52992, [128, 2, TP], BF16)
        Hs = self.aview('mlH', 70656, [128, 34, 256], F32)
        hgrow = self.aview('mlhgr', 70656, [128, 2048], F32)
        Ob = self.aview('mlO', 105472, [128, 2, TP], BF16)
        hgB = self.aview('mlhgB', 123136, [128, 2048], F32)
        Gt = self.aview('mlGt', 131328, [128, 34, 32], F32)
        NL = self.aview('mlNL', 135680, [128, 34, 32], F32)
        LF = [self.aview('mlLF%d' % d, 140032 + d * 1088, [128, 34, 8], F32) for d in range(2)]
        Wd = [self.aview('mlW%d' % d, 142208 + d * 1088, [128, 34, 8], F32) for d in range(2)]
        Ed = [self.aview('mlE%d' % d, 144384 + d * 1088, [128, 34, 8], F32) for d in range(2)]
        ELd = [self.aview('mlEL%d' % d, 146560 + d * 1088, [128, 34, 8], F32) for d in range(2)]
        Cs = [self.aview('mlC%d' % d, 148736 + d * 1032, [128, 258], F32) for d in range(2)]
        Cb = [self.aview('mlCb%d' % d, 150800 + d * 516, [128, 258], BF16) for d in range(2)]
        Vta = [self.aview('mlVta%d' % d, 151832 + d * 516, [128, 258], BF16) for d in range(4)]
        tri = self.aview('mltri', 153896, [128, 2, 128], F32)
        ssb = [self.aview('mlss%d' % i, 154920 + i * 16, [128, 4], F32) for i in range(8)]
        p.dma('sp', tri[:], self.tri.rearrange('a p t -> p a t'), [self.in_res], [tri.res])
        p.dma('sp', gT[0:32, :], self.F1[0:32, :], [self.F1_res], [gT.res])
        p.dma('sp', hgrow[0:1, :], self.ml_hg, [self.in_res], [hgrow.res])
        for i in range(4):
            ps = self.next_psum()
            p.mm(ps[:, :], self.ones_f[0:1, :], hgrow[0:1, i * 512:(i + 1) * 512], True, True, [self.ones_f.res, hgrow.res],
                 [ps.res], inc=True)
            p.copy('dve', hgB[:, i * 512:(i + 1) * 512], ps[:, :], [ps.res], [hgB.res])

        def tcol(ck):
            return C0 + 128 * ck if ck < 2 else L0 + 128 * (ck - 2)

        for g in range(9):
            cks = list(range(g * 4, min(34, g * 4 + 4)))
            ps = self.next_psum()
            for i, ck in enumerate(cks):
                p.tr(ps[:, i * 32:(i + 1) * 32], gT[0:32, tcol(ck):tcol(ck) + 128], self.ident_f[0:32, 0:32],
                     [gT.res, self.ident_f.res], [ps.res], inc=(i == len(cks) - 1))
            n = len(cks)
            p.copy('dve', Gt[:, g * 4:g * 4 + n, :], ps[:, 0:n * 32].rearrange('p (a b) -> p a b', a=n), [ps.res], [Gt.res])
        p.act(NL[:], Gt[:], AF.Exp, [Gt.res], [NL.res], scale=-1.0)
        p.act(NL[:], NL[:], AF.Ln, [NL.res, self.ones_f.res], [NL.res], bias=self.ones_f[:, 0:1], scale=1.0)
        for d in range(2):
            p.ts('dve', LF[d][:], NL[:, :, 16 * d + 8:16 * d + 16], -1.0, None, ALU.mult, None, [NL.res], [LF[d].res])
        for d in range(2):
            lf2 = LF[d].ap.rearrange('p a b -> p (a b)')
            psB = self.next_psum()
            psT = self.next_psum()
            p.mm(psB[:, 0:272], tri[:, d, :], lf2, True, True, [tri.res, LF[d].res], [psB.res], inc=True)
            p.mm(psT[:, 0:272], self.ones_f[:], lf2, True, True, [self.ones_f.res, LF[d].res], [psT.res], inc=True)
            w2 = Wd[d].ap.rearrange('p a b -> p (a b)')
            p.tt('dve', Wd[d][:], Gt[:, :, 16 * d:16 * d + 8], psB[:, 0:272].rearrange('p (a b) -> p a b', a=34), ALU.subtract,
                 [Gt.res, psB.res], [Wd[d].res])
            p.act(w2, w2, AF.Exp, [Wd[d].res], [Wd[d].res])
            p.act(Ed[d].ap.rearrange('p a b -> p (a b)'), psB[:, 0:272], AF.Exp, [psB.res], [Ed[d].res])
            p.act(ELd[d].ap.rearrange('p a b -> p (a b)'), psT[:, 0:272], AF.Exp, [psT.res], [ELd[d].res])
        for v4 in Vta:
            p.memset('dve', v4[:], 1.0, [v4.res])
        p.barrier()
        order = [list(range(34)), [1, 0] + list(range(33, 1, -1))]
        vi = 0
        for h in range(8):
            p.dma('sp', qT[:], S1[h * 128:(h + 1) * 128, :], [S1r], [qT.res])
            p.dma('sp', kT[:], S2[h * 128:(h + 1) * 128, :], [S2r], [kT.res])
            p.dma('sp', vT[:], S3[2 * h * 128:(2 * h + 2) * 128, :].rearrange('(j p) t -> p j t', p=128), [S3r], [vT.res])
            p.dma('sp', so[:], S4[2 * h * 128:(2 * h + 2) * 128, :].rearrange('(j p) t -> p j t', p=128), [S4r], [so.res])
            p.memset('dve', Hs[:], 0.0, [Hs.res])
            for d in range(2):
                p.memset('dve', Cs[d][:], 0.0, [Cs[d].res])
                p.memset('dve', Cb[d][:], 0.0, [Cb[d].res])
            for i in range(34):
                for d in range(2):
                    ck = order[d][i]
                    col = tcol(ck)
                    wcol = Wd[d][:, ck, h:h + 1]
                    ecol = Ed[d][:, ck, h:h + 1]
                    elcol = ELd[d][:, ck, h:h + 1]
                    va = Vta[vi % 4]
                    vi += 1
                    tp = self.next_psum()
                    tpv = tp.ap.bitcast(BF16)
                    p.tr(tpv[:, 0:128], kT[:, col:col + 128], self.ident_b[:], [kT.res, self.ident_b.res], [tp.res], inc=False)
                    p.tr(tpv[:, 128:256], vT[:, 0, col:col + 128], self.ident_b[:], [vT.res], [tp.res], inc=False)
                    p.tr(tpv[:, 256:384], vT[:, 1, col:col + 128], self.ident_b[:], [vT.res], [tp.res], inc=True)
                    ktw, ktwv = self.bfstage()
                    p.act(ktwv[:, 0:128], tpv[:, 0:128], AF.Copy, [tp.res, Wd[d].res], [ktw.res], scale=wcol)
                    p.copy('dve', va[:, 0:256], tpv[:, 128:384], [tp.res], [va.res])
                    ps_s = self.next_psum()
                    p.mm(ps_s[:, 0:128], kT[:, col:col + 128], qT[:, col:col + 128], True, True, [kT.res, qT.res],
                         [ps_s.res], inc=True)
                    pt, ptv = self.bfstage()
                    p.stt('dve', ptv[:, 0:128], ps_s[:, 0:128], wcol, tri[:, d, :], ALU.mult, ALU.mult,
                          [ps_s.res, Wd[d].res, tri.res], [pt.res])
                    ps_n = self.next_psum()
                    p.mm(ps_n[:, 0:257], ptv[:, 0:128], va[:, 0:257], True, False, [pt.res, va.res], [ps_n.res], inc=False)
                    p.mm(ps_n[:, 0:257], qT[:, col:col + 128], Cb[d][:, 0:257], False, True, [qT.res, Cb[d].res], [ps_n.res],
                         inc=True)
                    sm = ssb[vi % 8]
                    p.act(sm[:, 0:1], ps_n[:, 256:257], AF.Abs, [ps_n.res], [sm.res])
                    p.recip(sm[:, 1:2], sm[:, 0:1], [sm.res], [sm.res])
                    p.tt('dve', sm[:, 2:3], sm[:, 1:2], ecol, ALU.min, [sm.res, Ed[d].res], [sm.res])
                    p.stt('dve', Hs[:, ck, :], ps_n[:, 0:256], sm[:, 2:3], Hs[:, ck, :], ALU.mult, ALU.add,
                          [ps_n.res, sm.res, Hs.res], [Hs.res])
                    ps_c = self.next_psum()
                    p.mm(ps_c[:, 0:257], ktwv[:, 0:128], va[:, 0:257], True, True, [ktw.res, va.res], [ps_c.res], inc=True)
                    p.tt('dve', Cs[d][:, 0:257], ps_c[:, 0:257], Cs[d][:, 0:257], ALU.add, [ps_c.res, Cs[d].res], [Cs[d].res])
                    p.act(Cb[d][:, 0:257], Cs[d][:, 0:257], AF.Copy, [Cs[d].res, ELd[d].res], [Cb[d].res], scale=elcol)
                    p.ts('dve', Cs[d][:, 0:257], Cs[d][:, 0:257], elcol, None, ALU.mult, None, [Cs[d].res, ELd[d].res], [Cs[d].res])
            for ck in range(34):
                col = tcol(ck)
                sm = ssb[ck % 8]
                junk = self.next_stage()
                p.op('act', lambda e, o_=junk[:, 0:256], i_=Hs[:, ck, :], a_=sm[:, 0:1]: e.activation(
                    out=o_, in_=i_, func=AF.Square, accum_out=a_), [Hs.res], [junk.res, sm.res])
                p.act(sm[:, 1:2], sm[:, 0:1], AF.Sqrt, [sm.res, self.eps_t.res], [sm.res], bias=self.eps_t[:, 0:1], scale=1.0 / 256)
                p.recip(sm[:, 2:3], sm[:, 1:2], [sm.res], [sm.res])
                hn = self.next_stage()
                p.stt('dve', hn[:, 0:256], Hs[:, ck, :], sm[:, 2:3], hgB[:, h * 256:(h + 1) * 256], ALU.mult, ALU.mult,
                      [Hs.res, sm.res, hgB.res], [hn.res])
                ps_t = self.next_psum()
                p.tr(ps_t[:, 0:128], hn[:, 0:128], self.ident_f[:], [hn.res, self.ident_f.res], [ps_t.res], inc=False)
                p.tr(ps_t[:, 128:256], hn[:, 128:256], self.ident_f[:], [hn.res], [ps_t.res], inc=True)
                p.tt('dve', Ob[:, :, col:col + 128], ps_t[:, 0:256].rearrange('p (j t) -> p j t', j=2), so[:, :, col:col + 128],
                     ALU.mult, [ps_t.res, so.res], [Ob.res])
            dst = S5[2 * h * 128:(2 * h + 2) * 128, :].rearrange('(j p) t -> p j t', p=128)
            p.dma('sp', dst[:, :, C0:C1], Ob[:, :, C0:C1], [Ob.res], [S5r])
            p.dma('sp', dst[:, :, L0:L1], Ob[:, :, L0:L1], [Ob.res], [S5r])
        p.barrier()
        self.out_proj(self.ml_w_out, 'S5', last)


    def mixer_na(self, l, last):
        p = self.p
        S1, S1r = self.S['S1'], self.Sres['S1']
        S2, S2r = self.S['S2'], self.Sres['S2']
        S3, S3r = self.S['S3'], self.Sres['S3']
        S4, S4r = self.S['S4'], self.Sres['S4']
        gq = self.aview('nag', self.EXTRA, [128, 2], F32)
        p.dma('sp', gq[:], self.na_g, [self.in_res], [gq.res])
        p.ts('dve', gq[:, 0:1], gq[:, 0:1], 128.0 ** -0.5, None, ALU.mult, None, [gq.res], [gq.res])
        blocks = []
        c = 0
        while c < 48:
            n = min(10, 48 - c)
            blocks.append(dict(loads=[(self.na_w_qkv[:, c * 128:(c + n) * 128], n * 128)],
                               groups=[[(i * 128, 128)] for i in range(n)], c0=c))
            c += n

        def epiA(bi, gi, ti, tile, pss):
            c0, w, s, hl, hr = tile
            cg = blocks[bi]['c0'] + gi
            ps = pss[0]
            if cg < 32:
                which, h = cg // 16, cg % 16
                sq, sqv = self.bfstage()
                p.act(sqv[:, 0:w], ps[:, 0:w], AF.Square, [ps.res], [sq.res])
                ps2 = self.next_psum()
                p.mm(ps2[:, 0:w], self.ones_b[:], sqv[:, 0:w], True, True, [sq.res, self.ones_b.res], [ps2.res], inc=True)
                rs = self.next_stage()
                p.act(rs[:, 0:w], ps2[:, 0:w], AF.Sqrt, [ps2.res, self.eps_t.res], [rs.res], bias=self.eps_t[:, 0:1],
                      scale=1.0 / 128)
                p.recip(rs[:, 0:w], rs[:, 0:w], [rs.res], [rs.res])
                ob, obv = self.bfstage()
                p.stt('dve', obv[:, 0:w], ps[:, 0:w], gq[:, which:which + 1], rs[:, 0:w], ALU.mult, ALU.mult,
                      [ps.res, gq.res, rs.res], [ob.res])
                dst, dr = (S1, S1r) if which == 0 else (S2, S2r)
                p.dma('sp', dst[h * 128:(h + 1) * 128, c0:c0 + w], obv[:, 0:w], [ob.res], [dr])
            else:
                cc = cg - 32
                ob, obv = self.bfstage()
                p.act(obv[:, 0:w], ps[:, 0:w], AF.Copy, [ps.res], [ob.res])
                p.dma('sp', S3[cc * 128:(cc + 1) * 128, c0:c0 + w], obv[:, 0:w], [ob.res], [S3r])

        self.linear(self.hT, self.hT_res, KC, blocks, self.plain(), epiA, defer=True)
        p.barrier()
        SEQ = TP * 2
        qT = [self.aview('naq%d' % i, i * SEQ, [128, TP], BF16) for i in range(2)]
        kT = [self.aview('nak%d' % i, (2 + i) * SEQ, [128, TP], BF16) for i in range(2)]
        vT = [self.aview('nav%d' % i, (4 + i) * SEQ, [128, TP], BF16) for i in range(2)]
        Ob = [self.aview('nao%d' % i, (6 + i) * SEQ, [128, TP], BF16) for i in range(2)]
        o2 = 8 * SEQ
        Vt = [self.aview('naVt%d' % i, o2 + i * 8704, [128, 34, 128], BF16) for i in range(2)]
        o3 = o2 + 2 * 8704
        tab = [self.aview('natab%d' % i, o3 + i * 8192, [128, 2, 16, 64], F32) for i in range(2)]
        Sb = self.psum[0:4]
        Ob_ps = self.psum[4:6]
        Db_ps = self.psum[6:8]

        def tcol(tk):
            return C0 + 128 * tk if tk < 2 else L0 + 128 * (tk - 2)

        def load_head(h):
            p.dma('pool', qT[h % 2][:], S1[h * 128:(h + 1) * 128, :], [S1r], [qT[h % 2].res])
            p.dma('pool', kT[h % 2][:], S2[h * 128:(h + 1) * 128, :], [S2r], [kT[h % 2].res])
            p.dma('pool', vT[h % 2][:], S3[h * 128:(h + 1) * 128, :], [S3r], [vT[h % 2].res])
            p.dma('pool', tab[h % 2][:], self.na_tab[h].rearrange('a p u c -> p a u c'), [self.in_res], [tab[h % 2].res])

        load_head(0)
        for h in range(16):
            q, k, v, O, V, tb = qT[h % 2], kT[h % 2], vT[h % 2], Ob[h % 2], Vt[h % 2], tab[h % 2]
            if h + 1 < 16:
                load_head(h + 1)
            for g in range(9):
                tks = list(range(g * 4, min(34, g * 4 + 4)))
                ps = Sb[g % 4]
                psv = ps.ap.bitcast(BF16)
                for i, tk in enumerate(tks):
                    p.tr(psv[:, i * 128:(i + 1) * 128], v[:, tcol(tk):tcol(tk) + 128], self.ident_b[:],
                         [v.res, self.ident_b.res], [ps.res], inc=(i == len(tks) - 1))
                n = len(tks)
                p.copy('dve' if g % 2 else 'act_copy', V[:, g * 4:g * 4 + n, :],
                       psv[:, 0:n * 128].rearrange('p (a b) -> p a b', a=n), [ps.res], [V.res])
            units = []
            qts = ([] if last else [(C0, None)]) + [(L0 + 256 * j, j) for j in range(16)]
            for qi, (qc0, j) in enumerate(qts):
                keys = []
                if j is not None:
                    if j == 0:
                        kts, ti_ = range(0, 4), 1
                    elif j == 15:
                        kts, ti_ = range(28, 32), 1
                    else:
                        kts, ti_ = range(2 * j - 2, 2 * j + 4), 0
                    for kt in kts:
                        keys.append((2 + kt, (ti_, 7 - 2 * kt + 4 * j)))
                keys += [(0, None), (1, None)]
                for ki, (tk, bias) in enumerate(keys):
                    units.append(dict(qi=qi, qc0=qc0, tk=tk, bias=bias, first=(ki == 0), last=(ki == len(keys) - 1)))

            def emit_S(i, u):
                ps = Sb[i % 4]
                kc_ = tcol(u['tk'])
                p.mm(ps[:, 0:256], k[:, kc_:kc_ + 128], q[:, u['qc0']:u['qc0'] + 256], True, True, [k.res, q.res], [ps.res],
                     inc=True)
                pt, ptv = self.bfstage()
                if u['bias'] is not None:
                    ti_, u0 = u['bias']
                    tmp = self.next_stage()
                    p.tt('dve', tmp.ap[:, 0:256].rearrange('p (a b) -> p a b', a=4),
                         ps.ap[:, 0:256].rearrange('p (a b) -> p a b', a=4), tb[:, ti_, u0:u0 + 4, :], ALU.add,
                         [ps.res, tb.res], [tmp.res])
                    p.act(ptv[:, 0:256], tmp[:, 0:256], AF.Exp, [tmp.res], [pt.res])
                else:
                    p.act(ptv[:, 0:256], ps[:, 0:256], AF.Exp, [ps.res], [pt.res])
                u['pt'] = (pt, ptv)

            def emit_PV(u):
                pt, ptv = u['pt']
                ob_ = Ob_ps[u['qi'] % 2]
                db_ = Db_ps[u['qi'] % 2]
                p.mm(ob_[:, 0:256], V[:, u['tk'], :], ptv[:, 0:256], u['first'], u['last'], [V.res, pt.res], [ob_.res],
                     inc=u['last'])
                p.mm(db_[:, 0:256], self.ones_b[:], ptv[:, 0:256], u['first'], u['last'], [self.ones_b.res, pt.res],
                     [db_.res], inc=True)
                if u['last']:
                    rec = self.next_stage()
                    p.recip(rec[:, 0:256], db_[:, 0:256], [db_.res], [rec.res])
                    p.tt('dve', O[:, u['qc0']:u['qc0'] + 256], ob_[:, 0:256], rec[:, 0:256], ALU.mult,
                         [ob_.res, rec.res], [O.res])

            pend = []
            for i, u in enumerate(units):
                emit_S(i, u)
                pend.append(u)
                if len(pend) > 2:
                    emit_PV(pend.pop(0))
            while pend:
                emit_PV(pend.pop(0))
            if not last:
                p.dma('sp', S4[h * 128:(h + 1) * 128, C0:C1], O[:, C0:C1], [O.res], [S4r])
            p.dma('sp', S4[h * 128:(h + 1) * 128, L0:L1], O[:, L0:L1], [O.res], [S4r])
        p.barrier()
        self.out_proj(self.na_w_o, 'S4', last)


LRU_C = 8.0
NEG = -30000.0


def rope_tables():
    t = np.arange(NLAT)
    freqs = (10000.0 ** (-np.arange(0, 64, 2, dtype=np.float32) / np.float32(64))).astype(np.float32)
    d = np.arange(128)
    pos = np.where(d[:, None] < 64, (t // 64)[None, :], (t % 64)[None, :]).astype(np.float32)
    ang = (pos * freqs[d % 32][:, None]).astype(np.float32)
    cos = np.ones((128, TP), np.float32)
    sin = np.zeros((128, TP), np.float32)
    cos[:, L0:L1] = np.cos(ang)
    sgn = np.where((d % 64) < 32, -1.0, 1.0).astype(np.float32)
    sin[:, L0:L1] = np.sin(ang) * sgn[:, None]
    sc = np.float32(128.0 ** -0.5)
    return np.ascontiguousarray(np.stack([cos * sc, sin * sc, cos, sin]).astype(np.float32))


def na_table(rpb):
    H = rpb.shape[0]
    krl = np.arange(128) // 64
    kc = np.arange(128) % 64
    u = np.arange(16)
    qc = np.arange(64)
    dr = 14 + krl[:, None] - u[None, :]
    dc = np.clip(kc[:, None] - qc[None, :] + 15, 0, 30)
    cstart = np.clip(qc - 8, 0, 48)
    cmask = (kc[:, None] >= cstart[None, :]) & (kc[:, None] < cstart[None, :] + 16)
    drc = np.clip(dr, 0, 14)
    g = rpb[:, drc[:, :, None], dc[:, None, :]]
    out = np.full((H, 2, 128, 16, 64), NEG, np.float32)
    for t, (lo, hi) in enumerate(((3, 10), (0, 14))):
        valid = ((dr >= lo) & (dr <= hi))[:, :, None] & cmask[:, None, :]
        out[:, t] = np.where(valid[None], g, np.float32(NEG))
    return out


def prep_mixers(inp, b, layers=(0, 1, 2, 3)):
    m = {}
    if 0 in layers:
        w_in = inp['ml_w_in'][0]
        d = np.arange(128)
        perm = np.where((d % 64) < 32, d + 32, d - 32)
        cols = (np.arange(16)[:, None] * 128 + perm[None, :]).reshape(-1)
        m['ml_w_in'] = w_in
        m['ml_w_qkp'] = np.ascontiguousarray(w_in[:, cols])
        m['ml_w_gate'] = inp['ml_w_gate'][0]
        m['ml_bg'] = np.ascontiguousarray(inp['ml_b_gate'][0].reshape(32, 1))
        m['ml_hg'] = np.ascontiguousarray(inp['ml_head_g'][0].reshape(1, 2048))
        m['ml_w_out'] = inp['ml_w_out'][0]
        m['rope'] = rope_tables()
        m['tri'] = np.stack([np.triu(np.ones((128, 128), np.float32)), np.tril(np.ones((128, 128), np.float32))])
    if 1 in layers:
        m['na_w_qkv'] = inp['na_w_qkv'][0]
        m['na_g'] = np.ascontiguousarray(np.stack([inp['na_q_g'][0], inp['na_k_g'][0]], axis=1).astype(np.float32))
        m['na_tab'] = na_table(inp['na_rpb'][0])
        m['na_w_o'] = inp['na_w_o'][0]
    if 2 in layers:
        m['cv_w_pw1'] = inp['cv_w_pw1'][0]
        m['cv_dwT'] = np.ascontiguousarray(np.transpose(colT(inp['cv_dw'][0]), (0, 2, 1)))
        m['cv_vT'] = colT(np.stack([inp['cv_dw_b'][0], inp['cv_ln_g'][0], inp['cv_ln_b'][0]]))
        m['cv_w_pw2'] = inp['cv_w_pw2'][0]
    if 3 in layers:
        m['lr_w_in'] = inp['lr_w_in'][0]
        m['lr_cvT'] = colT(np.concatenate([inp['lr_conv'][0], inp['lr_conv_b']], axis=0))
        m['lr_w_gate'] = inp['lr_w_gate'][0]
        m['lr_bgT'] = colT(inp['lr_b_gate'][0])
        m['lr_lamT'] = colT(inp['lr_lambda'][0])
        m['lr_w_out'] = inp['lr_w_out'][0]
    return m
```

```python
import contextlib
import numpy as np
import concourse.bass as bass
import concourse.mybir as mybir
from concourse.bass_utils import run_bass_kernel_spmd

F32 = mybir.dt.float32
BF16 = mybir.dt.bfloat16
ALU = mybir.AluOpType
AF = mybir.ActivationFunctionType

D = 2048
KC = 16
DFF = 5632
FC = 44
PAD = 16
NCTX = 256
NLAT = 4096
C0 = PAD
C1 = C0 + NCTX
L0 = C1 + 2 * PAD
L1 = L0 + NLAT
TP = L1 + PAD
EPS = 1e-6
DEPTH = 4


class Res:
    __slots__ = ('name', 'w', 'r', 'multi')

    def __init__(self, name, multi=False):
        self.name = name
        self.w = {}
        self.r = {}
        self.multi = multi


class DSem:
    __slots__ = ('sem', 'val')

    def __init__(self, sem):
        self.sem = sem
        self.val = 0


class Buf:
    def __init__(self, ap, name):
        self.ap = ap
        self.res = Res(name)

    def __getitem__(self, idx):
        return self.ap[idx]


class Prog:
    QS = ('pe', 'dve', 'act', 'pool', 'sp')

    def __init__(self, nc, stack):
        self.nc = nc
        self.stack = stack
        self.q = {k: [] for k in self.QS}
        self.csem = {k: stack.enter_context(nc.semaphore('c_' + k)) for k in self.QS}
        self.cnt = {k: 0 for k in self.QS}
        self.pending = {k: False for k in self.QS}
        self.seen = {k: {} for k in self.QS}
        self.rings = {}
        self.ringpos = {}
        for q, n in (('sp', 40), ('pool', 16), ('act', 8)):
            self.rings[q] = [DSem(stack.enter_context(nc.semaphore('d_%s%d' % (q, i)))) for i in range(n)]
            self.ringpos[q] = 0
        self.ninstr = 0

    def sb(self, name, shape, dtype):
        return Buf(self.stack.enter_context(self.nc.sbuf_tensor(name, shape, dtype))[:], name)

    def ps(self, name, shape, dtype):
        return Buf(self.stack.enter_context(self.nc.psum_tensor(name, shape, dtype))[:], name)

    def op(self, q, fn, reads=(), writes=(), inc=True, dsem=None):
        deps = {}

        def add(d):
            for k, sv in d.items():
                if k not in deps or deps[k][1] < sv[1]:
                    deps[k] = sv

        for r in reads:
            add(r.w)
        for w in writes:
            add(w.w)
            add(w.r)
        if dsem is not None and dsem.val > 0:
            add({id(dsem.sem): (dsem.sem, dsem.val)})
        own = id(self.csem[q])
        if q == 'pe' and own in deps:
            del deps[own]
        seen = self.seen[q]
        waits = []
        for k, (s, v) in deps.items():
            if seen.get(k, 0) >= v:
                continue
            seen[k] = v
            waits.append((s, v))
        if dsem is not None:
            dsem.val += 16
            tick = (dsem.sem, dsem.val)
            incinfo = (dsem.sem, 16)
        else:
            tick = (self.csem[q], self.cnt[q] + 1)
            if inc:
                self.cnt[q] += 1
                incinfo = (self.csem[q], 1)
                self.pending[q] = False
            else:
                incinfo = None
                self.pending[q] = True
        k = id(tick[0])
        for r in reads:
            if k not in r.r or r.r[k][1] < tick[1]:
                r.r[k] = tick
        for w in writes:
            if w.multi:
                if k not in w.w or w.w[k][1] < tick[1]:
                    w.w[k] = tick
            else:
                w.w = {k: tick}
                w.r = {}
        self.q[q].append((waits, fn, incinfo))
        self.ninstr += 1

    def dma(self, q, out, in_, reads=(), writes=(), **kw):
        ring = self.rings[q]
        ds = ring[self.ringpos[q] % len(ring)]
        self.ringpos[q] += 1
        self.op(q, lambda e: e.dma_start(out=out, in_=in_, **kw), reads, writes, dsem=ds)

    def barrier(self):
        for q in self.QS:
            assert not self.pending[q], q
        targets = [(self.csem[k], self.cnt[k]) for k in self.QS if self.cnt[k] > 0]
        for ring in self.rings.values():
            targets += [(d.sem, d.val) for d in ring if d.val > 0]
        for q in self.QS:
            seen = self.seen[q]
            waits = []
            for (s, v) in targets:
                if q == 'pe' and s is self.csem['pe']:
                    continue
                if seen.get(id(s), 0) >= v:
                    continue
                seen[id(s)] = v
                waits.append((s, v))
            if waits:
                self.q[q].append((waits, None, None))

    def emit(self):
        nc = self.nc
        names = {'pe': 'tensor', 'dve': 'vector', 'act': 'scalar', 'pool': 'gpsimd', 'sp': 'sync'}
        with nc.Block() as block:
            for k in self.QS:
                lst = self.q[k]

                def body(eng, lst=lst):
                    for waits, fn, incinfo in lst:
                        for (s, v) in waits:
                            eng.wait_ge(s, v)
                        if fn is None:
                            continue
                        ins = fn(eng)
                        if incinfo is not None:
                            ins.then_inc(incinfo[0], incinfo[1])

                getattr(block, names[k])(body)

    def mm(self, out, lhsT, rhs, start, stop, reads, writes, inc):
        self.op('pe', lambda e: e.matmul(out, lhsT=lhsT, rhs=rhs, start=start, stop=stop), reads, writes, inc=inc)

    def tr(self, out, in_, ident, reads, writes, inc=True):
        self.op('pe', lambda e: e.transpose(out, in_, ident), reads, writes, inc=inc)

    def act(self, out, in_, func, reads, writes, **kw):
        self.op('act', lambda e: e.activation(out=out, in_=in_, func=func, **kw), reads, writes)

    def tt(self, q, out, in0, in1, op, reads, writes):
        self.op(q, lambda e: e.tensor_tensor(out=out, in0=in0, in1=in1, op=op), reads, writes)

    def ts(self, q, out, in0, s1, s2, op0, op1, reads, writes):
        if op1 is None:
            self.op(q, lambda e: e.tensor_scalar(out=out, in0=in0, scalar1=s1, scalar2=None, op0=op0), reads, writes)
        else:
            self.op(q, lambda e: e.tensor_scalar(out=out, in0=in0, scalar1=s1, scalar2=s2, op0=op0, op1=op1), reads, writes)

    def stt(self, q, out, in0, scalar, in1, op0, op1, reads, writes):
        self.op(q, lambda e: e.scalar_tensor_tensor(out=out, in0=in0, scalar=scalar, in1=in1, op0=op0, op1=op1), reads, writes)

    def copy(self, q, out, in_, reads, writes):
        if q == 'act_copy':
            self.op('act', lambda e: e.activation(out=out, in_=in_, func=AF.Copy), reads, writes)
        else:
            self.op(q, lambda e: e.tensor_copy(out=out, in_=in_), reads, writes)

    def memset(self, q, ap, val, writes):
        self.op(q, lambda e: e.memset(ap, val), (), writes)

    def recip(self, out, in_, reads, writes):
        self.op('dve', lambda e: e.reciprocal(out=out, in_=in_), reads, writes)


def colT(v):
    v = np.asarray(v, np.float32)
    F = v.shape[-1]
    r = v.reshape(v.shape[:-1] + (F // 128, 128))
    r = np.moveaxis(r, -1, 0)
    return np.ascontiguousarray(r)


def tiles_plain():
    t = [(C0, NCTX, 1)]
    for i in range(8):
        t.append((L0 + 512 * i, 512, 0))
    return t


def tiles_n(n, lat_only=False):
    t = [] if lat_only else [(C0, NCTX, 1)]
    s = 0
    while s < NLAT:
        w = min(n, NLAT - s)
        t.append((L0 + s, w, 0))
        s += w
    return t


class MK:
    def __init__(self, layers=(0, 1, 2, 3), debug=None, skip_mixer=False, last_layer=3):
        self.layers = layers
        self.debug = debug
        self.skip_mixer = skip_mixer
        self.last_layer = last_layer
        self.nc = bass.Bass("TRN2", target_bir_lowering=False)
        self.dr = {}

    def din(self, name, shape, dtype=F32):
        t = self.nc.dram_tensor(name, list(shape), dtype, kind="ExternalInput")
        self.dr[name] = t.ap()
        return self.dr[name]

    def dscr(self, name, shape, dtype):
        t = self.nc.dram_tensor(name, list(shape), dtype)
        a = t.ap()
        a_res = Res(name, multi=True)
        return a, a_res

    def build(self):
        nc = self.nc
        with contextlib.ExitStack() as stack:
            self.p = p = Prog(nc, stack)
            self.declare_io()
            self.alloc(stack)
            self.prologue()
            for l in self.layers:
                self.layer(l)
            self.epilogue_out()
            p.emit()
        return nc

    def declare_io(self):
        nc = self.nc
        self.xT_in = self.din('xT', [D, TP])
        self.cT = self.din('cT', [128, KC, 2])
        self.ada_w = self.din('ada_w', [DEPTH, D, 6 * D])
        self.abT = self.din('abT', [128, DEPTH, 96])
        self.ngT = self.din('ngT', [128, DEPTH, 2, KC])
        self.ffn_w_gu = self.din('ffn_w_gu', [DEPTH, D, 2 * DFF])
        self.ffn_cwT = self.din('ffn_cwT', [128, DEPTH, 3, FC])
        self.ffn_w_down = self.din('ffn_w_down', [DEPTH, DFF, D])
        self.in_res = Res('inputs', multi=True)
        self.ident_in = self.din('ident', [128, 128])
        self.declare_mixer_io()
        out = nc.dram_tensor('outT', [D, NLAT], F32, kind="ExternalOutput")
        self.outT = out.ap()
        self.out_res = Res('outT', multi=True)
        self.xT, self.xT_res = self.dscr('xT_s', [D, TP], F32)
        self.hT, self.hT_res = self.dscr('hT_s', [D, TP], BF16)
        self.HID, self.HID_res = self.dscr('hid_s', [10, 128, FC, 456], BF16)

    def declare_mixer_io(self):
        pass

    def alloc(self, stack):
        p = self.p
        self.ARENA = 172 * 1024
        self.arena = p.sb('arena', [128, self.ARENA // 4], F32)
        self.stage = [p.sb('stg%d' % i, [128, 512], F32) for i in range(8)]
        self.stage_i = 0
        self.psum = [p.ps('ps%d' % i, [128, 512], F32) for i in range(8)]
        self.psum_i = 0
        self.ident_f = p.sb('ident_f', [128, 128], F32)
        self.ident_b = p.sb('ident_b', [128, 128], BF16)
        self.ones_b = p.sb('ones_b', [128, 128], BF16)
        self.ones_f = p.sb('ones_f', [128, 128], F32)
        self.eps_t = p.sb('eps_t', [128, 1], F32)
        self.modT = p.sb('modT', [128, 96, 2], F32)
        self.lv = p.sb('lv', [128, 6, KC, 2], F32)
        self.ngs = p.sb('ngs', [128, DEPTH, 2, KC], F32)
        self.abs_ = p.sb('abs', [128, DEPTH, 96], F32)
        self.cws = p.sb('cws', [128, DEPTH, 3, FC], F32)
        self.scT = p.sb('scT', [128, KC, 2], F32)
        self.W = [self.aview('W%d' % i, i * 45056, [128, 22528], BF16) for i in range(2)]
        self.X = [self.aview('X%d' % i, 90112 + i * 40960, [128, 20480], BF16) for i in range(2)]
        self.WW = self.aview('WW', 0, [128, 45056], BF16)
        self.w_i = 0
        self.x_i = 0
        self.EXTRA = 90112 + 2 * 40960

    def aview(self, name, off, shape, dtype):
        nbytes = int(np.prod(shape[1:])) * (4 if dtype == F32 else 2)
        assert off % 4 == 0 and nbytes % 4 == 0 and off + nbytes <= self.ARENA, (name, off, nbytes)
        ap = self.arena.ap[:, off // 4:(off + nbytes) // 4]
        if dtype != F32:
            ap = ap.bitcast(dtype)
        if len(shape) == 3:
            ap = ap.rearrange('p (a b) -> p a b', a=shape[1])
        elif len(shape) == 4:
            ap = ap.rearrange('p (a b c) -> p a b c', a=shape[1], b=shape[2])
        return Buf(ap, name)

    def next_stage(self):
        b = self.stage[self.stage_i % len(self.stage)]
        self.stage_i += 1
        return b

    def next_psum(self):
        b = self.psum[self.psum_i % 8]
        self.psum_i += 1
        return b

    def prologue(self):
        p = self.p
        nc = self.nc
        p.memset('dve', self.ones_b[:], 1.0, [self.ones_b.res])
        p.memset('dve', self.ones_f[:], 1.0, [self.ones_f.res])
        p.memset('dve', self.eps_t[:], EPS, [self.eps_t.res])
        p.dma('sp', self.ident_f[:], self.ident_in, [self.in_res], [self.ident_f.res])
        p.copy('dve', self.ident_b[:], self.ident_f[:], [self.ident_f.res], [self.ident_b.res])
        p.dma('sp', self.ngs[:], self.ngT, [self.in_res], [self.ngs.res])
        p.dma('sp', self.abs_[:], self.abT, [self.in_res], [self.abs_.res])
        p.dma('sp', self.cws[:], self.ffn_cwT, [self.in_res], [self.cws.res])
        p.dma('sp', self.scT[:], self.cT, [self.in_res], [self.scT.res])
        p.act(self.scT[:], self.scT[:], AF.Silu, [self.scT.res], [self.scT.res])
        self.x_rd = (self.xT_in, self.in_res)
        zt = self.aview('zt', 80000, [128, KC, 2 * PAD], BF16)
        p.memset('dve', zt[:], 0.0, [zt.res])
        hs = self.hT.rearrange('(kc p) t -> p kc t', p=128)
        p.dma('sp', hs[:, :, 0:PAD], zt[:, :, 0:PAD], [zt.res], [self.hT_res])
        p.dma('sp', hs[:, :, C1:L0], zt[:, :, :], [zt.res], [self.hT_res])
        p.dma('sp', hs[:, :, L1:TP], zt[:, :, 0:PAD], [zt.res], [self.hT_res])
        p.barrier()

    def mods(self, l):
        p = self.p
        p.barrier()
        wb = [self.aview('aw%d' % i, i * 16384, [128, KC, 512], BF16) for i in range(4)]
        scb = self.aview('scb', 65536, [128, KC, 2], BF16)
        p.copy('dve', scb[:], self.scT[:], [self.scT.res], [scb.res])
        ps = self.next_psum()
        psv = ps.ap[:, 0:192].rearrange('p (j s) -> p j s', s=2)
        for blk in range(24):
            b = wb[blk % 4]
            src = self.ada_w[l, :, blk * 512:(blk + 1) * 512].rearrange('(kc p) n -> p kc n', p=128)
            p.dma('pool', b[:], src, [self.in_res], [b.res])
            for jj in range(4):
                j = blk * 4 + jj
                for kc in range(KC):
                    p.mm(psv[:, j, :], b[:, kc, jj * 128:(jj + 1) * 128], scb[:, kc, :], kc == 0, kc == KC - 1,
                         [b.res, scb.res], [ps.res], inc=(kc == KC - 1))
        for s in range(2):
            p.tt('dve', self.modT[:, :, s], psv[:, :, s], self.abs_[:, l, :], ALU.add,
                 [ps.res, self.abs_.res], [self.modT.res])
        m = self.modT
        lv = self.lv
        for half in range(2):
            base = half * 48
            for s in range(2):
                p.stt('dve', lv[:, half * 3 + 0, :, s], m[:, base + 16:base + 32, s], 1.0, self.ngs[:, l, half, :],
                      ALU.add, ALU.mult, [m.res, self.ngs.res], [lv.res])
                p.copy('dve', lv[:, half * 3 + 1, :, s], m[:, base:base + 16, s], [m.res], [lv.res])
                p.copy('dve', lv[:, half * 3 + 2, :, s], m[:, base + 32:base + 48, s], [m.res], [lv.res])
        p.barrier()

    def norm(self, l, half, lat_only=False):
        p = self.p
        p.barrier()
        xin = [self.aview('nx%d' % i, i * 32768, [128, KC, 512], F32) for i in range(2)]
        sq = self.aview('nsq', 65536, [128, KC, 512], BF16)
        ob = [self.aview('nob%d' % i, 81920 + i * 16384, [128, KC, 512], BF16) for i in range(2)]
        rs = self.aview('nrs', 114688, [128, 512], F32)
        tmp = [self.aview('ntmp%d' % i, 116736 + i * 2048, [128, 512], F32) for i in range(4)]
        xs = self.x_rd[0].rearrange('(kc p) t -> p kc t', p=128)
        xs_res = self.x_rd[1]
        hs = self.hT.rearrange('(kc p) t -> p kc t', p=128)
        lv = self.lv
        tl = tiles_plain()
        if lat_only:
            tl = tl[1:]
        for ti, (c0, w, s) in enumerate(tl):
            xb = xin[ti % 2]
            o = ob[ti % 2]
            p.dma('pool', xb[:, :, 0:w], xs[:, :, c0:c0 + w], [xs_res], [xb.res])
            p.act(sq[:, :, 0:w], xb[:, :, 0:w], AF.Square, [xb.res], [sq.res])
            ps = self.next_psum()
            for kc in range(KC):
                p.mm(ps[:, 0:w], self.ones_b[:], sq[:, kc, 0:w], kc == 0, kc == KC - 1, [sq.res, self.ones_b.res],
                     [ps.res], inc=(kc == KC - 1))
            p.act(rs[:, 0:w], ps[:, 0:w], AF.Sqrt, [ps.res, self.eps_t.res], [rs.res], bias=self.eps_t[:, 0:1],
                  scale=1.0 / D)
            p.recip(rs[:, 0:w], rs[:, 0:w], [rs.res], [rs.res])
            for kc in range(KC):
                t = tmp[kc % 4]
                p.tt('dve', t[:, 0:w], xb[:, kc, 0:w], rs[:, 0:w], ALU.mult, [xb.res, rs.res], [t.res])
                p.act(o[:, kc, 0:w], t[:, 0:w], AF.Identity, [t.res, lv.res], [o.res],
                      scale=lv[:, half * 3 + 0, kc, s:s + 1], bias=lv[:, half * 3 + 1, kc, s:s + 1])
            p.dma('sp', hs[:, :, c0:c0 + w], o[:, :, 0:w], [o.res], [self.hT_res])
        p.barrier()

    def linear(self, xsrc, xres, Kc, wblocks, tiles, epi, xload=None, defer=False, wide=False):
        p = self.p
        pend = None
        wbufs = {}
        xbufs = {}

        def load_w(bi):
            blk = wblocks[bi]
            ncols = sum(n for _, n in blk['loads'])
            if wide:
                assert Kc * ncols <= 45056, (Kc, ncols)
                wres = [self.W[0].res, self.W[1].res]
                wv = self.WW.ap[:, 0:Kc * ncols].rearrange('p (kc n) -> p kc n', kc=Kc)
            else:
                wb = self.W[self.w_i % 2]
                self.w_i += 1
                assert Kc * ncols <= 22528, (Kc, ncols)
                wres = [wb.res]
                wv = wb.ap[:, 0:Kc * ncols].rearrange('p (kc n) -> p kc n', kc=Kc)
            off = 0
            for (src, n) in blk['loads']:
                p.dma('pool', wv[:, :, off:off + n], src.rearrange('(kc p) n -> p kc n', p=128),
                      [self.in_res], wres)
                off += n
            wbufs[bi] = (wres, wv)

        def load_x(bi, ti):
            tile = tiles[ti]
            c0, w, s, hl, hr = tile
            wt = w + hl + hr
            xb = self.X[self.x_i % 2]
            self.x_i += 1
            assert Kc * wt <= 20480
            xv = xb.ap[:, 0:Kc * wt].rearrange('p (kc t) -> p kc t', kc=Kc)
            if xload is not None:
                xload(ti, tile, xv, xb)
            else:
                p.dma('pool', xv, xsrc[:, c0 - hl:c0 + w + hr].rearrange('(kc p) t -> p kc t', p=128),
                      [xres], [xb.res])
            xbufs[(bi, ti)] = (xb, xv)

        its = [(bi, ti) for bi in range(len(wblocks)) for ti in range(len(tiles))]
        load_w(0)
        load_x(*its[0])
        for n, (bi, ti) in enumerate(its):
            if n + 1 < len(its):
                load_x(*its[n + 1])
            if bi + 1 < len(wblocks) and not wide and ti == 0:
                load_w(bi + 1)
            blk = wblocks[bi]
            wres, wv = wbufs[bi]
            xb, xv = xbufs.pop((bi, ti))
            tile = tiles[ti]
            c0, w, s, hl, hr = tile
            wt = w + hl + hr
            for gi, grp in enumerate(blk['groups']):
                pss = []
                for (co, cw) in grp:
                    ps = self.next_psum()
                    for kc in range(Kc):
                        p.mm(ps[0:cw, 0:wt], wv[:, kc, co:co + cw], xv[:, kc, :], kc == 0, kc == Kc - 1,
                             wres + [xb.res], [ps.res], inc=(kc == Kc - 1))
                    pss.append(ps)
                if defer:
                    if pend is not None:
                        epi(*pend)
                    pend = (bi, gi, ti, tile, pss)
                else:
                    epi(bi, gi, ti, tile, pss)
            if wide and bi + 1 < len(wblocks) and ti == len(tiles) - 1:
                load_w(bi + 1)
        if pend is not None:
            epi(*pend)

    def epi_residual(self, gate_kind, chunk_of, to_out=False):
        p = self.p
        src, src_res = self.x_rd
        if to_out:
            dst, dst_res, doff = self.outT, self.out_res, -L0
        else:
            dst, dst_res, doff = self.xT, self.xT_res, 0
        cnt = [0]

        def epi(bi, gi, ti, tile, pss):
            c0, w, s, hl, hr = tile
            dc = chunk_of(bi, gi)
            ps = pss[0]
            st = self.next_stage()
            p.dma('pool', st[:, 0:w], src[dc * 128:(dc + 1) * 128, c0:c0 + w], [src_res], [st.res])
            p.stt('dve', st[:, 0:w], ps[:, hl:hl + w], self.lv[:, gate_kind, dc, s:s + 1], st[:, 0:w], ALU.mult, ALU.add,
                  [ps.res, st.res, self.lv.res], [st.res])
            p.dma('sp', dst[dc * 128:(dc + 1) * 128, c0 + doff:c0 + doff + w], st[:, 0:w], [st.res], [dst_res])

        return epi

    def ffn(self, l, lat_only=False, final=False):
        p = self.p
        base = tiles_n(456, lat_only)
        tiles = [(c0, w, s, 1, 1) for (c0, w, s) in base]
        toff = 1 if lat_only else 0
        wblocks = []
        f = 0
        while f < FC:
            nf = 1 if f == 0 else min(5, FC - f)
            wblocks.append(dict(
                loads=[(self.ffn_w_gu[l, :, f * 128:(f + nf) * 128], nf * 128),
                       (self.ffn_w_gu[l, :, DFF + f * 128:DFF + (f + nf) * 128], nf * 128)],
                groups=[[(j * 128, 128), (nf * 128 + j * 128, 128)] for j in range(nf)], f0=f))
            f += nf
        cws = self.cws

        def epi_up(bi, gi, ti, tile, pss):
            c0, w, s, hl, hr = tile
            fch = wblocks[bi]['f0'] + gi
            pg, pu = pss
            t1 = self.next_stage()
            sg = self.next_stage()
            hb = self.next_stage()
            hbv = hb.ap.bitcast(BF16)
            p.ts('dve', t1[:, 0:w], pg[:, 1:w + 1], cws[:, l, 1, fch:fch + 1], None, ALU.mult, None,
                 [pg.res, cws.res], [t1.res])
            p.stt('dve', t1[:, 0:w], pg[:, 0:w], cws[:, l, 0, fch:fch + 1], t1[:, 0:w], ALU.mult, ALU.add,
                  [pg.res, cws.res, t1.res], [t1.res])
            p.stt('dve', t1[:, 0:w], pg[:, 2:w + 2], cws[:, l, 2, fch:fch + 1], t1[:, 0:w], ALU.mult, ALU.add,
                  [pg.res, cws.res, t1.res], [t1.res])
            p.act(sg[:, 0:w], t1[:, 0:w], AF.Silu, [t1.res], [sg.res])
            p.tt('dve', hbv[:, 0:w], sg[:, 0:w], pu[:, 1:w + 1], ALU.mult, [sg.res, pu.res], [hb.res])
            p.dma('sp', self.HID[ti + toff, :, fch, 0:w], hbv[:, 0:w], [hb.res], [self.HID_res])

        self.linear(self.hT, self.hT_res, KC, wblocks, tiles, epi_up)
        dtiles = [(c0, w, s, 0, 0) for (c0, w, s) in base]
        dblocks = []
        for (b0, nb) in ((0, 8), (8, 8)):
            dblocks.append(dict(loads=[(self.ffn_w_down[l, :, b0 * 128:(b0 + nb) * 128], nb * 128)],
                                groups=[[(j * 128, 128)] for j in range(nb)], c0=b0))

        def xload(ti, tile, xv, xb):
            c0, w, s, hl, hr = tile
            p.dma('pool', xv, self.HID[ti + toff, :, :, 0:w], [self.HID_res], [xb.res])

        self.linear(None, None, FC, dblocks, dtiles, self.epi_residual(5, lambda bi, gi: dblocks[bi]['c0'] + gi, to_out=final), xload=xload, wide=True)

    def layer(self, l):
        last = (l == self.last_layer)
        self.mods(l)
        if not self.skip_mixer:
            self.norm(l, 0)
            self.mixer(l, last)
        self.norm(l, 1, lat_only=last)
        self.ffn(l, lat_only=last, final=(last and l == self.layers[-1]))
        if self.skip_mixer:
            self.x_rd = (self.xT, self.xT_res)

    def mixer(self, l, last):
        raise NotImplementedError

    def epilogue_out(self):
        p = self.p
        p.barrier()
        if not (self.layers[-1] == self.last_layer):
            xs = self.xT.rearrange('(kc p) t -> p kc t', p=128)
            os_ = self.outT.rearrange('(kc p) t -> p kc t', p=128)
            tb = [self.aview('ocp%d' % i, i * 32768, [128, KC, 512], F32) for i in range(2)]
            for i in range(8):
                b = tb[i % 2]
                p.dma('sp', b[:], xs[:, :, L0 + i * 512:L0 + (i + 1) * 512], [self.xT_res], [b.res])
                p.dma('sp', os_[:, :, i * 512:(i + 1) * 512], b[:], [b.res], [self.out_res])
            p.barrier()

    def debug_out(self):
        pass


def prep_common(inp, b):
    xT = np.zeros((D, TP), np.float32)
    xT[:, C0:C1] = inp['ctx'][b].T
    xT[:, L0:L1] = inp['x'][b].T
    cT = np.stack([colT(inp['c'][b]), colT(inp['c_ctx'])], axis=-1)
    m = {
        'xT': xT,
        'cT': np.ascontiguousarray(cT),
        'ada_w': inp['ada_w'],
        'abT': colT(inp['ada_b']),
        'ngT': colT(np.stack([inp['norm_mix'], inp['norm_ffn']], axis=1)),
        'ffn_w_gu': inp['ffn_w_gu'],
        'ffn_cwT': colT(inp['ffn_conv']),
        'ffn_w_down': inp['ffn_w_down'],
        'ident': np.eye(128, dtype=np.float32),
    }
    return m


def kernel(**inputs):
    inp = {k: np.asarray(v) for k, v in inputs.items()}
    mk = MKFull()
    nc = mk.build()
    in_maps = []
    for b in range(8):
        m = prep_common(inp, b)
        m.update(prep_mixers(inp, b))
        in_maps.append(m)
    res = run_bass_kernel_spmd(nc, in_maps, core_ids=list(range(8)))
    out = np.stack([np.ascontiguousarray(r['outT'].T) for r in res.results], axis=0)
    return out.astype(np.float32)


class MKFull(MK):
    def declare_mixer_io(self):
        din = self.din
        L = self.layers
        if 0 in L:
            self.ml_w_in = din('ml_w_in', [D, 6144])
            self.ml_w_qkp = din('ml_w_qkp', [D, 2048])
            self.ml_w_gate = din('ml_w_gate', [D, 32])
            self.ml_bg = din('ml_bg', [32, 1])
            self.ml_hg = din('ml_hg', [1, 2048])
            self.ml_w_out = din('ml_w_out', [D, D])
            self.rope = din('rope', [4, 128, TP])
            self.tri = din('tri', [2, 128, 128])
        if 1 in L:
            self.na_w_qkv = din('na_w_qkv', [D, 6144])
            self.na_g = din('na_g', [128, 2])
            self.na_tab = din('na_tab', [16, 2, 128, 16, 64])
            self.na_w_o = din('na_w_o', [D, D])
        if 2 in L:
            self.cv_w_pw1 = din('cv_w_pw1', [D, 4096])
            self.cv_dwT = din('cv_dwT', [128, 16, 31])
            self.cv_vT = din('cv_vT', [128, 3, 16])
            self.cv_w_pw2 = din('cv_w_pw2', [D, D])
        if 3 in L:
            self.lr_w_in = din('lr_w_in', [D, 4096])
            self.lr_cvT = din('lr_cvT', [128, 5, 16])
            self.lr_w_gate = din('lr_w_gate', [4, 8, 256, 256])
            self.lr_bgT = din('lr_bgT', [128, 4, 16])
            self.lr_lamT = din('lr_lamT', [128, 2, 16])
            self.lr_w_out = din('lr_w_out', [D, D])
        self.S = {}
        self.Sres = {}
        for n in ('S1', 'S2', 'S3', 'S4', 'S5'):
            self.S[n], self.Sres[n] = self.dscr(n, [D, TP], BF16)
        self.F1, self.F1_res = self.dscr('F1', [D, TP], F32)

    def prologue(self):
        MK.prologue(self)
        p = self.p
        zt = self.aview('zt2', 80000, [128, KC, 2 * PAD], BF16)
        p.memset('dve', zt[:], 0.0, [zt.res])
        hs = self.S['S1'].rearrange('(kc p) t -> p kc t', p=128)
        p.dma('sp', hs[:, :, 0:PAD], zt[:, :, 0:PAD], [zt.res], [self.Sres['S1']])
        p.dma('sp', hs[:, :, C1:L0], zt[:, :, :], [zt.res], [self.Sres['S1']])
        p.dma('sp', hs[:, :, L1:TP], zt[:, :, 0:PAD], [zt.res], [self.Sres['S1']])
        p.barrier()

    def mixer(self, l, last):
        [self.mixer_ml, self.mixer_na, self.mixer_cv, self.mixer_lr][l % 4](l, last)

    def bfstage(self):
        st = self.next_stage()
        return st, st.ap.bitcast(BF16)

    def plain(self, lat_only=False):
        t = [(c0, w, s, 0, 0) for (c0, w, s) in tiles_plain()]
        return t[1:] if lat_only else t

    def out_proj(self, w_ap, src, last):
        blocks = [dict(loads=[(w_ap[:, 0:1024], 1024), (w_ap[:, 1024:2048], 1024)],
                       groups=[[(j * 128, 128)] for j in range(16)], c0=0)]
        self.linear(self.S[src], self.Sres[src], KC, blocks, self.plain(last),
                    self.epi_residual(2, lambda bi, gi: gi), wide=True)
        self.x_rd = (self.xT, self.xT_res)

    def mixer_cv(self, l, last):
        p = self.p
        S1, S1r = self.S['S1'], self.Sres['S1']
        S2, S2r = self.S['S2'], self.Sres['S2']
        blocks = []
        j = 0
        while j < 16:
            nf = min(5, 16 - j)
            blocks.append(dict(loads=[(self.cv_w_pw1[:, j * 128:(j + nf) * 128], nf * 128),
                                      (self.cv_w_pw1[:, D + j * 128:D + (j + nf) * 128], nf * 128)],
                               groups=[[(i * 128, 128), (nf * 128 + i * 128, 128)] for i in range(nf)], f0=j))
            j += nf

        def epiA(bi, gi, ti, tile, pss):
            c0, w, s, hl, hr = tile
            jc = blocks[bi]['f0'] + gi
            pa, pg = pss
            sg = self.next_stage()
            hb, hbv = self.bfstage()
            p.act(sg[:, 0:w], pg[:, 0:w], AF.Sigmoid, [pg.res], [sg.res])
            p.tt('dve', hbv[:, 0:w], pa[:, 0:w], sg[:, 0:w], ALU.mult, [pa.res, sg.res], [hb.res])
            p.dma('sp', S1[jc * 128:(jc + 1) * 128, c0:c0 + w], hbv[:, 0:w], [hb.res], [S1r])

        self.linear(self.hT, self.hT_res, KC, blocks, self.plain(), epiA)
        p.barrier()
        dg = [self.aview('dg%d' % i, i * 8192, [128, 31, 128], BF16) for i in range(2)]
        xt = [self.aview('cxt%d' % i, 16384 + i * 2048, [128, 544], BF16) for i in range(4)]
        dwT = self.aview('dwT', 24576, [128, 16, 31], F32)
        vT = self.aview('vT', 28672, [128, 3, 16], F32)
        p.dma('sp', dwT[:], self.cv_dwT, [self.in_res], [dwT.res])
        p.dma('sp', vT[:], self.cv_vT, [self.in_res], [vT.res])
        n = 0
        for jc in range(16):
            d = dg[jc % 2]
            for k in range(31):
                if k % 2 == 0:
                    p.ts('dve', d[:, k, :], self.ident_b[:], dwT[:, jc, k:k + 1], None, ALU.mult, None,
                         [self.ident_b.res, dwT.res], [d.res])
                else:
                    p.act(d[:, k, :], self.ident_b[:], AF.Copy, [self.ident_b.res, dwT.res], [d.res],
                          scale=dwT[:, jc, k:k + 1])
            for (c0, w, s, _, _) in self.plain():
                x = xt[n % 4]
                n += 1
                p.dma('pool', x[:, 0:w + 30], S1[jc * 128:(jc + 1) * 128, c0 - 15:c0 + w + 15], [S1r], [x.res])
                ps = self.next_psum()
                for k in range(31):
                    p.mm(ps[:, 0:w], d[:, k, :], x[:, k:k + w], k == 0, k == 30, [d.res, x.res], [ps.res], inc=(k == 30))
                st = self.next_stage()
                p.act(st[:, 0:w], ps[:, 0:w], AF.Identity, [ps.res, vT.res], [st.res], bias=vT[:, 0, jc:jc + 1], scale=1.0)
                p.dma('sp', self.F1[jc * 128:(jc + 1) * 128, c0:c0 + w], st[:, 0:w], [st.res], [self.F1_res])
        p.barrier()
        yin = [self.aview('ly%d' % i, i * 32768, [128, KC, 512], F32) for i in range(2)]
        sq = self.aview('lsq', 65536, [128, KC, 512], BF16)
        ob = [self.aview('lob%d' % i, 81920 + i * 16384, [128, KC, 512], BF16) for i in range(2)]
        sm = [self.aview('lsm%d' % i, 114688 + i * 2048, [128, 512], F32) for i in range(7)]
        vT = self.aview('vT2', 131072, [128, 3, 16], F32)
        p.dma('sp', vT[:], self.cv_vT, [self.in_res], [vT.res])
        mean, m2, rstd = sm[0], sm[1], sm[2]
        tmp = sm[3:7]
        ys = self.F1.rearrange('(kc p) t -> p kc t', p=128)
        os_ = S2.rearrange('(kc p) t -> p kc t', p=128)
        for ti, (c0, w, s, _, _) in enumerate(self.plain()):
            y = yin[ti % 2]
            o = ob[ti % 2]
            p.dma('pool', y[:, :, 0:w], ys[:, :, c0:c0 + w], [self.F1_res], [y.res])
            p.act(sq[:, :, 0:w], y[:, :, 0:w], AF.Square, [y.res], [sq.res])
            ps1 = self.next_psum()
            ps2 = self.next_psum()
            for kc in range(KC):
                p.mm(ps1[:, 0:w], self.ones_f[:], y[:, kc, 0:w], kc == 0, kc == KC - 1, [y.res, self.ones_f.res],
                     [ps1.res], inc=(kc == KC - 1))
            for kc in range(KC):
                p.mm(ps2[:, 0:w], self.ones_b[:], sq[:, kc, 0:w], kc == 0, kc == KC - 1, [sq.res, self.ones_b.res],
                     [ps2.res], inc=(kc == KC - 1))
            p.act(mean[:, 0:w], ps1[:, 0:w], AF.Copy, [ps1.res], [mean.res], scale=1.0 / D)
            p.tt('dve', m2[:, 0:w], mean[:, 0:w], mean[:, 0:w], ALU.mult, [mean.res], [m2.res])
            p.stt('dve', m2[:, 0:w], ps2[:, 0:w], 1.0 / D, m2[:, 0:w], ALU.mult, ALU.subtract, [ps2.res, m2.res], [m2.res])
            p.act(rstd[:, 0:w], m2[:, 0:w], AF.Sqrt, [m2.res, self.eps_t.res], [rstd.res], bias=self.eps_t[:, 0:1], scale=1.0)
            p.recip(rstd[:, 0:w], rstd[:, 0:w], [rstd.res], [rstd.res])
            for kc in range(KC):
                t = tmp[kc % 4]
                p.tt('dve', t[:, 0:w], y[:, kc, 0:w], mean[:, 0:w], ALU.subtract, [y.res, mean.res], [t.res])
                p.tt('dve', t[:, 0:w], t[:, 0:w], rstd[:, 0:w], ALU.mult, [t.res, rstd.res], [t.res])
                p.act(o[:, kc, 0:w], t[:, 0:w], AF.Silu, [t.res, vT.res], [o.res], scale=vT[:, 1, kc:kc + 1],
                      bias=vT[:, 2, kc:kc + 1])
            p.dma('sp', os_[:, :, c0:c0 + w], o[:, :, 0:w], [o.res], [S2r])
        p.barrier()
        self.out_proj(self.cv_w_pw2, 'S2', last)

    def mixer_lr(self, l, last):
        p = self.p
        S1, S1r = self.S['S1'], self.Sres['S1']
        S2, S2r = self.S['S2'], self.Sres['S2']
        S3, S3r = self.S['S3'], self.Sres['S3']
        cv = self.aview('lrcv', self.EXTRA, [128, 5, 16], F32)
        p.dma('sp', cv[:], self.lr_cvT, [self.in_res], [cv.res])
        tiles = [(c0, w, s, 2, 1) for (c0, w, s) in tiles_n(456)]
        blocks = [dict(loads=[(self.lr_w_in[:, b * 1024:(b + 1) * 1024], 1024)], groups=[[(j * 128, 128)] for j in range(8)])
                  for b in range(4)]

        def epiA(bi, gi, ti, tile, pss):
            c0, w, s, hl, hr = tile
            ps = pss[0]
            if bi < 2:
                c = bi * 8 + gi
                st, stv = self.bfstage()
                p.act(stv[:, 0:w], ps[:, hl:hl + w], AF.Gelu_apprx_tanh, [ps.res], [st.res])
                p.dma('sp', S2[c * 128:(c + 1) * 128, c0:c0 + w], stv[:, 0:w], [st.res], [S2r])
            else:
                c = (bi - 2) * 8 + gi
                t = self.next_stage()
                tb, tbv = self.bfstage()
                p.ts('dve', t[:, 0:w], ps[:, 0:w], cv[:, 0, c:c + 1], cv[:, 4, c:c + 1], ALU.mult, ALU.add,
                     [ps.res, cv.res], [t.res])
                for k in range(1, 4):
                    p.stt('dve', t[:, 0:w], ps[:, k:k + w], cv[:, k, c:c + 1], t[:, 0:w], ALU.mult, ALU.add,
                          [ps.res, cv.res, t.res], [t.res])
                p.act(tbv[:, 0:w], t[:, 0:w], AF.Copy, [t.res], [tb.res])
                p.dma('sp', self.F1[c * 128:(c + 1) * 128, c0:c0 + w], t[:, 0:w], [t.res], [self.F1_res])
                p.dma('sp', S1[c * 128:(c + 1) * 128, c0:c0 + w], tbv[:, 0:w], [tb.res], [S1r])

        self.linear(self.hT, self.hT_res, KC, blocks, tiles, epiA)
        p.barrier()
        SEQ = TP * 4
        A = [self.aview('lrA%d' % d, (2 * d) * SEQ, [128, TP], F32) for d in range(2)]
        Bv = [self.aview('lrB%d' % d, (2 * d + 1) * SEQ, [128, TP], F32) for d in range(2)]
        ubf = self.aview('lrubf', 4 * SEQ, [128, 2, TP], BF16)
        uf = self.aview('lruf', 5 * SEQ, [128, TP], F32)
        gg = self.aview('lrgg', 6 * SEQ, [128, TP], BF16)
        ob = self.aview('lrob', 6 * SEQ + TP * 2, [128, TP], BF16)
        o2 = 7 * SEQ
        wg = [self.aview('lrwg%d' % i, o2 + i * 2048, [128, 4, 2, 128], BF16) for i in range(2)]
        bgT = self.aview('lrbg', o2 + 4096, [128, 4, 16], F32)
        lam = self.aview('lrlam', o2 + 4096 + 256, [128, 2, 16], F32)
        kap = self.aview('lrkap', o2 + 4096 + 512, [128, 2, 2, 16], F32)
        p.dma('sp', bgT[:], self.lr_bgT, [self.in_res], [bgT.res])
        p.dma('sp', lam[:], self.lr_lamT, [self.in_res], [lam.res])
        p.act(lam[:], lam[:], AF.Exp, [lam.res], [lam.res], scale=-1.0)
        p.act(lam[:], lam[:], AF.Ln, [lam.res, self.ones_f.res], [lam.res], bias=self.ones_f[:, 0:1], scale=1.0)
        p.ts('dve', kap[:, 0, :, :], lam[:], -LRU_C, None, ALU.mult, None, [lam.res], [kap.res])
        p.ts('dve', kap[:, 1, :, :], lam[:], -2.0 * LRU_C, None, ALU.mult, None, [lam.res], [kap.res])
        pl = self.plain()

        def rev(b, a0, a1):
            return b.ap[:, a0:a1][:, ::-1]

        for c in range(16):
            nb, jo = c // 2, c % 2
            w = wg[c % 2]
            for t in range(4):
                p.dma('pool', w[:, t, :, :],
                      self.lr_w_gate[t, nb, :, jo * 128:(jo + 1) * 128].rearrange('(ki p) j -> p ki j', p=128),
                      [self.in_res], [w.res])
            if jo == 0:
                p.dma('sp', ubf[:], S1[nb * 256:(nb + 1) * 256, :].rearrange('(ki p) t -> p ki t', p=128), [S1r], [ubf.res])
            p.dma('sp', uf[:], self.F1[c * 128:(c + 1) * 128, :], [self.F1_res], [uf.res])
            p.dma('sp', gg[:], S2[c * 128:(c + 1) * 128, :], [S2r], [gg.res])
            for (c0, w_, s, _, _) in pl:
                pss = [self.next_psum() for _ in range(4)]
                for t in range(4):
                    for ki in range(2):
                        p.mm(pss[t][:, 0:w_], w[:, t, ki, :], ubf[:, ki, c0:c0 + w_], ki == 0, ki == 1,
                             [w.res, ubf.res], [pss[t].res], inc=(ki == 1))
                rr = [self.next_stage() for _ in range(2)]
                ii = [self.next_stage() for _ in range(2)]
                r2s = [self.next_stage() for _ in range(2)]
                for d in range(2):
                    p.act(rr[d][:, 0:w_], pss[2 * d][:, 0:w_], AF.Sigmoid, [pss[2 * d].res, bgT.res], [rr[d].res],
                          bias=bgT[:, 2 * d, c:c + 1], scale=1.0)
                    p.act(ii[d][:, 0:w_], pss[2 * d + 1][:, 0:w_], AF.Sigmoid, [pss[2 * d + 1].res, bgT.res], [ii[d].res],
                          bias=bgT[:, 2 * d + 1, c:c + 1], scale=1.0)
                for d in range(2):
                    p.act(A[d][:, c0:c0 + w_], rr[d][:, 0:w_], AF.Exp, [rr[d].res, kap.res], [A[d].res],
                          scale=kap[:, 0, d, c:c + 1])
                    p.tt('dve', r2s[d][:, 0:w_], A[d][:, c0:c0 + w_], A[d][:, c0:c0 + w_], ALU.mult, [A[d].res], [r2s[d].res])
                for d in range(2):
                    p.act(r2s[d][:, 0:w_], r2s[d][:, 0:w_], AF.Sqrt, [r2s[d].res, self.ones_f.res], [r2s[d].res], scale=-1.0,
                          bias=self.ones_f[:, 0:1])
                    p.tt('pool', ii[d][:, 0:w_], ii[d][:, 0:w_], r2s[d][:, 0:w_], ALU.mult, [ii[d].res, r2s[d].res], [ii[d].res])
                    p.tt('dve', Bv[d][:, c0:c0 + w_], ii[d][:, 0:w_], uf[:, c0:c0 + w_], ALU.mult, [ii[d].res, uf.res], [Bv[d].res])

            def scan(out, a, b, init, reads, writes):
                p.op('dve', lambda e: e.tensor_tensor_scan(out=out, data0=a, data1=b, initial=init, op0=ALU.mult,
                                                          op1=ALU.add), reads, writes)

            rw = ([A[0].res, Bv[0].res], [Bv[0].res])
            scan(Bv[0][:, C0:C1], A[0][:, C0:C1], Bv[0][:, C0:C1], 0.0, *rw)
            scan(Bv[0][:, L0:L1], A[0][:, L0:L1], Bv[0][:, L0:L1], Bv[0][:, C1 - 1:C1], *rw)
            rw = ([A[1].res, Bv[1].res], [Bv[1].res])
            scan(rev(Bv[1], C0, C1), rev(A[1], C0, C1), rev(Bv[1], C0, C1), 0.0, *rw)
            scan(rev(Bv[1], L0, L1), rev(A[1], L0, L1), rev(Bv[1], L0, L1), Bv[1][:, C0:C0 + 1], *rw)
            segs = [(L0 + i * 1024, L0 + (i + 1) * 1024) for i in range(4)]
            if not last:
                segs = [(C0, C1)] + segs
            for (a0, a1) in segs:
                p.tt('dve', Bv[0][:, a0:a1], Bv[0][:, a0:a1], Bv[1][:, a0:a1], ALU.add, [Bv[0].res, Bv[1].res], [Bv[0].res])
                p.tt('dve', ob[:, a0:a1], Bv[0][:, a0:a1], gg[:, a0:a1], ALU.mult, [Bv[0].res, gg.res], [ob.res])
            if not last:
                p.dma('sp', S3[c * 128:(c + 1) * 128, C0:C1], ob[:, C0:C1], [ob.res], [S3r])
            p.dma('sp', S3[c * 128:(c + 1) * 128, L0:L1], ob[:, L0:L1], [ob.res], [S3r])
        p.barrier()
        self.out_proj(self.lr_w_out, 'S3', last)


    def mixer_ml(self, l, last):
        p = self.p
        S1, S1r = self.S['S1'], self.Sres['S1']
        S2, S2r = self.S['S2'], self.Sres['S2']
        S3, S3r = self.S['S3'], self.Sres['S3']
        S4, S4r = self.S['S4'], self.Sres['S4']
        S5, S5r = self.S['S5'], self.Sres['S5']
        bg = self.aview('mlbg', self.EXTRA, [128, 1], F32)
        p.dma('sp', bg[0:32, :], self.ml_bg, [self.in_res], [bg.res])
        blocks = []
        for which in range(2):
            for (h0, nh) in ((0, 5), (5, 3)):
                cb = which * 1024 + h0 * 128
                blocks.append(dict(kind='qk', which=which, f0=h0,
                                   loads=[(self.ml_w_in[:, cb:cb + nh * 128], nh * 128),
                                          (self.ml_w_qkp[:, cb:cb + nh * 128], nh * 128)],
                                   groups=[[(i * 128, 128), (nh * 128 + i * 128, 128)] for i in range(nh)]))
        for b in range(2):
            blocks.append(dict(kind='v', f0=b * 8, loads=[(self.ml_w_in[:, 2048 + b * 1024:2048 + (b + 1) * 1024], 1024)],
                               groups=[[(i * 128, 128)] for i in range(8)]))
        blocks.append(dict(kind='o', f0=0, loads=[(self.ml_w_in[:, 4096:5120], 1024)],
                           groups=[[(i * 128, 128)] for i in range(8)]))
        blocks.append(dict(kind='o', f0=8, loads=[(self.ml_w_in[:, 5120:6144], 1024), (self.ml_w_gate[:, 0:32], 32)],
                           groups=[[(i * 128, 128)] for i in range(8)] + [[(1024, 32)]]))

        def epiA(bi, gi, ti, tile, pss):
            c0, w, s, hl, hr = tile
            blk = blocks[bi]
            if blk['kind'] == 'qk':
                which = blk['which']
                h = blk['f0'] + gi
                pa, pb = pss
                ct = self.next_stage()
                sn = self.next_stage()
                p.dma('pool', ct[:, 0:w], self.rope[2 * which, :, c0:c0 + w], [self.in_res], [ct.res])
                p.dma('pool', sn[:, 0:w], self.rope[2 * which + 1, :, c0:c0 + w], [self.in_res], [sn.res])
                p.tt('dve', ct[:, 0:w], pa[:, 0:w], ct[:, 0:w], ALU.mult, [pa.res, ct.res], [ct.res])
                p.tt('dve', sn[:, 0:w], pb[:, 0:w], sn[:, 0:w], ALU.mult, [pb.res, sn.res], [sn.res])
                ob, obv = self.bfstage()
                p.tt('dve', obv[:, 0:w], ct[:, 0:w], sn[:, 0:w], ALU.add, [ct.res, sn.res], [ob.res])
                dst, dr = (S1, S1r) if which == 0 else (S2, S2r)
                p.dma('sp', dst[h * 128:(h + 1) * 128, c0:c0 + w], obv[:, 0:w], [ob.res], [dr])
            elif blk['kind'] == 'v':
                c = blk['f0'] + gi
                ob, obv = self.bfstage()
                p.act(obv[:, 0:w], pss[0][:, 0:w], AF.Copy, [pss[0].res], [ob.res])
                p.dma('sp', S3[c * 128:(c + 1) * 128, c0:c0 + w], obv[:, 0:w], [ob.res], [S3r])
            else:
                ps = pss[0]
                if gi < 8:
                    c = blk['f0'] + gi
                    ob, obv = self.bfstage()
                    p.act(obv[:, 0:w], ps[:, 0:w], AF.Sigmoid, [ps.res], [ob.res])
                    p.dma('sp', S4[c * 128:(c + 1) * 128, c0:c0 + w], obv[:, 0:w], [ob.res], [S4r])
                else:
                    st = self.next_stage()
                    p.act(st[0:32, 0:w], ps[0:32, 0:w], AF.Identity, [ps.res, bg.res], [st.res], bias=bg[0:32, 0:1], scale=1.0)
                    p.dma('sp', self.F1[0:32, c0:c0 + w], st[0:32, 0:w], [st.res], [self.F1_res])

        self.linear(self.hT, self.hT_res, KC, blocks, self.plain(), epiA)
        p.barrier()
        gT = self.aview('mlgT', 0, [128, TP], F32)
        qT = self.aview('mlq', 17664, [128, TP], BF16)
        kT = self.aview('mlk', 26496, [128, TP], BF16)
        vT = self.aview('mlv', 35328, [128, 2, TP], BF16)
        so = self.aview('mlso', 52992, [128, 2, TP], BF16)
        Hs = self.aview('mlH', 70656, [128, 34, 256], F32)
        hgrow = self.aview('mlhgr', 70656, [128, 2048], F32)
        Ob = self.aview('mlO', 105472, [128, 2, TP], BF16)
        hgB = self.aview('mlhgB', 123136, [128, 2048], F32)
        Gt = self.aview('mlGt', 131328, [128, 34, 32], F32)
        NL = self.aview('mlNL', 135680, [128, 34, 32], F32)
        LF = [self.aview('mlLF%d' % d, 140032 + d * 1088, [128, 34, 8], F32) for d in range(2)]
        Wd = [self.aview('mlW%d' % d, 142208 + d * 1088, [128, 34, 8], F32) for d in range(2)]
        Ed = [self.aview('mlE%d' % d, 144384 + d * 1088, [128, 34, 8], F32) for d in range(2)]
        ELd = [self.aview('mlEL%d' % d, 146560 + d * 1088, [128, 34, 8], F32) for d in range(2)]
        Cs = [self.aview('mlC%d' % d, 148736 + d * 1032, [128, 258], F32) for d in range(2)]
        Cb = [self.aview('mlCb%d' % d, 150800 + d * 516, [128, 258], BF16) for d in range(2)]
        Vta = [self.aview('mlVta%d' % d, 151832 + d * 516, [128, 258], BF16) for d in range(4)]
        tri = self.aview('mltri', 153896, [128, 2, 128], F32)
        ssb = [self.aview('mlss%d' % i, 154920 + i * 16, [128, 4], F32) for i in range(8)]
        p.dma('sp', tri[:], self.tri.rearrange('a p t -> p a t'), [self.in_res], [tri.res])
        p.dma('sp', gT[0:32, :], self.F1[0:32, :], [self.F1_res], [gT.res])
        p.dma('sp', hgrow[0:1, :], self.ml_hg, [self.in_res], [hgrow.res])
        for i in range(4):
            ps = self.next_psum()
            p.mm(ps[:, :], self.ones_f[0:1, :], hgrow[0:1, i * 512:(i + 1) * 512], True, True, [self.ones_f.res, hgrow.res],
                 [ps.res], inc=True)
            p.copy('dve', hgB[:, i * 512:(i + 1) * 512], ps[:, :], [ps.res], [hgB.res])

        def tcol(ck):
            return C0 + 128 * ck if ck < 2 else L0 + 128 * (ck - 2)

        for g in range(9):
            cks = list(range(g * 4, min(34, g * 4 + 4)))
            ps = self.next_psum()
            for i, ck in enumerate(cks):
                p.tr(ps[:, i * 32:(i + 1) * 32], gT[0:32, tcol(ck):tcol(ck) + 128], self.ident_f[0:32, 0:32],
                     [gT.res, self.ident_f.res], [ps.res], inc=(i == len(cks) - 1))
            n = len(cks)
            p.copy('dve', Gt[:, g * 4:g * 4 + n, :], ps[:, 0:n * 32].rearrange('p (a b) -> p a b', a=n), [ps.res], [Gt.res])
        p.act(NL[:], Gt[:], AF.Exp, [Gt.res], [NL.res], scale=-1.0)
        p.act(NL[:], NL[:], AF.Ln, [NL.res, self.ones_f.res], [NL.res], bias=self.ones_f[:, 0:1], scale=1.0)
        for d in range(2):
            p.ts('dve', LF[d][:], NL[:, :, 16 * d + 8:16 * d + 16], -1.0, None, ALU.mult, None, [NL.res], [LF[d].res])
        for d in range(2):
            lf2 = LF[d].ap.rearrange('p a b -> p (a b)')
            psB = self.next_psum()
            psT = self.next_psum()
            p.mm(psB[:, 0:272], tri[:, d, :], lf2, True, True, [tri.res, LF[d].res], [psB.res], inc=True)
            p.mm(psT[:, 0:272], self.ones_f[:], lf2, True, True, [self.ones_f.res, LF[d].res], [psT.res], inc=True)
            w2 = Wd[d].ap.rearrange('p a b -> p (a b)')
            p.tt('dve', Wd[d][:], Gt[:, :, 16 * d:16 * d + 8], psB[:, 0:272].rearrange('p (a b) -> p a b', a=34), ALU.subtract,
                 [Gt.res, psB.res], [Wd[d].res])
            p.act(w2, w2, AF.Exp, [Wd[d].res], [Wd[d].res])
            p.act(Ed[d].ap.rearrange('p a b -> p (a b)'), psB[:, 0:272], AF.Exp, [psB.res], [Ed[d].res])
            p.act(ELd[d].ap.rearrange('p a b -> p (a b)'), psT[:, 0:272], AF.Exp, [psT.res], [ELd[d].res])
        for v4 in Vta:
            p.memset('dve', v4[:], 1.0, [v4.res])
        p.barrier()
        order = [list(range(34)), [1, 0] + list(range(33, 1, -1))]
        vi = 0
        for h in range(8):
            p.dma('sp', qT[:], S1[h * 128:(h + 1) * 128, :], [S1r], [qT.res])
            p.dma('sp', kT[:], S2[h * 128:(h + 1) * 128, :], [S2r], [kT.res])
            p.dma('sp', vT[:], S3[2 * h * 128:(2 * h + 2) * 128, :].rearrange('(j p) t -> p j t', p=128), [S3r], [vT.res])
            p.dma('sp', so[:], S4[2 * h * 128:(2 * h + 2) * 128, :].rearrange('(j p) t -> p j t', p=128), [S4r], [so.res])
            p.memset('dve', Hs[:], 0.0, [Hs.res])
            for d in range(2):
                p.memset('dve', Cs[d][:], 0.0, [Cs[d].res])
                p.memset('dve', Cb[d][:], 0.0, [Cb[d].res])
            for i in range(34):
                for d in range(2):
                    ck = order[d][i]
                    col = tcol(ck)
                    wcol = Wd[d][:, ck, h:h + 1]
                    ecol = Ed[d][:, ck, h:h + 1]
                    elcol = ELd[d][:, ck, h:h + 1]
                    va = Vta[vi % 4]
                    vi += 1
                    tp = self.next_psum()
                    tpv = tp.ap.bitcast(BF16)
                    p.tr(tpv[:, 0:128], kT[:, col:col + 128], self.ident_b[:], [kT.res, self.ident_b.res], [tp.res], inc=False)
                    p.tr(tpv[:, 128:256], vT[:, 0, col:col + 128], self.ident_b[:], [vT.res], [tp.res], inc=False)
                    p.tr(tpv[:, 256:384], vT[:, 1, col:col + 128], self.ident_b[:], [vT.res], [tp.res], inc=True)
                    ktw, ktwv = self.bfstage()
                    p.act(ktwv[:, 0:128], tpv[:, 0:128], AF.Copy, [tp.res, Wd[d].res], [ktw.res], scale=wcol)
                    p.copy('dve', va[:, 0:256], tpv[:, 128:384], [tp.res], [va.res])
                    ps_s = self.next_psum()
                    p.mm(ps_s[:, 0:128], kT[:, col:col + 128], qT[:, col:col + 128], True, True, [kT.res, qT.res],
                         [ps_s.res], inc=True)
                    pt, ptv = self.bfstage()
                    p.stt('dve', ptv[:, 0:128], ps_s[:, 0:128], wcol, tri[:, d, :], ALU.mult, ALU.mult,
                          [ps_s.res, Wd[d].res, tri.res], [pt.res])
                    ps_n = self.next_psum()
                    p.mm(ps_n[:, 0:257], ptv[:, 0:128], va[:, 0:257], True, False, [pt.res, va.res], [ps_n.res], inc=False)
                    p.mm(ps_n[:, 0:257], qT[:, col:col + 128], Cb[d][:, 0:257], False, True, [qT.res, Cb[d].res], [ps_n.res],
                         inc=True)
                    sm = ssb[vi % 8]
                    p.act(sm[:, 0:1], ps_n[:, 256:257], AF.Abs, [ps_n.res], [sm.res])
                    p.recip(sm[:, 1:2], sm[:, 0:1], [sm.res], [sm.res])
                    p.tt('dve', sm[:, 2:3], sm[:, 1:2], ecol, ALU.min, [sm.res, Ed[d].res], [sm.res])
                    p.stt('dve', Hs[:, ck, :], ps_n[:, 0:256], sm[:, 2:3], Hs[:, ck, :], ALU.mult, ALU.add,
                          [ps_n.res, sm.res, Hs.res], [Hs.res])
                    ps_c = self.next_psum()
                    p.mm(ps_c[:, 0:257], ktwv[:, 0:128], va[:, 0:257], True, True, [ktw.res, va.res], [ps_c.res], inc=True)
                    p.tt('dve', Cs[d][:, 0:257], ps_c[:, 0:257], Cs[d][:, 0:257], ALU.add, [ps_c.res, Cs[d].res], [Cs[d].res])
                    p.act(Cb[d][:, 0:257], Cs[d][:, 0:257], AF.Copy, [Cs[d].res, ELd[d].res], [Cb[d].res], scale=elcol)
                    p.ts('dve', Cs[d][:, 0:257], Cs[d][:, 0:257], elcol, None, ALU.mult, None, [Cs[d].res, ELd[d].res], [Cs[d].res])
            for ck in range(34):
                col = tcol(ck)
                sm = ssb[ck % 8]
                junk = self.next_stage()
                p.op('act', lambda e, o_=junk[:, 0:256], i_=Hs[:, ck, :], a_=sm[:, 0:1]: e.activation(
                    out=o_, in_=i_, func=AF.Square, accum_out=a_), [Hs.res], [junk.res, sm.res])
                p.act(sm[:, 1:2], sm[:, 0:1], AF.Sqrt, [sm.res, self.eps_t.res], [sm.res], bias=self.eps_t[:, 0:1], scale=1.0 / 256)
                p.recip(sm[:, 2:3], sm[:, 1:2], [sm.res], [sm.res])
                hn = self.next_stage()
                p.stt('dve', hn[:, 0:256], Hs[:, ck, :], sm[:, 2:3], hgB[:, h * 256:(h + 1) * 256], ALU.mult, ALU.mult,
                      [Hs.res, sm.res, hgB.res], [hn.res])
                ps_t = self.next_psum()
                p.tr(ps_t[:, 0:128], hn[:, 0:128], self.ident_f[:], [hn.res, self.ident_f.res], [ps_t.res], inc=False)
                p.tr(ps_t[:, 128:256], hn[:, 128:256], self.ident_f[:], [hn.res], [ps_t.res], inc=True)
                p.tt('dve', Ob[:, :, col:col + 128], ps_t[:, 0:256].rearrange('p (j t) -> p j t', j=2), so[:, :, col:col + 128],
                     ALU.mult, [ps_t.res, so.res], [Ob.res])
            dst = S5[2 * h * 128:(2 * h + 2) * 128, :].rearrange('(j p) t -> p j t', p=128)
            p.dma('sp', dst[:, :, C0:C1], Ob[:, :, C0:C1], [Ob.res], [S5r])
            p.dma('sp', dst[:, :, L0:L1], Ob[:, :, L0:L1], [Ob.res], [S5r])
        p.barrier()
        self.out_proj(self.ml_w_out, 'S5', last)


    def mixer_na(self, l, last):
        p = self.p
        S1, S1r = self.S['S1'], self.Sres['S1']
        S2, S2r = self.S['S2'], self.Sres['S2']
        S3, S3r = self.S['S3'], self.Sres['S3']
        S4, S4r = self.S['S4'], self.Sres['S4']
        gq = self.aview('nag', self.EXTRA, [128, 2], F32)
        p.dma('sp', gq[:], self.na_g, [self.in_res], [gq.res])
        p.ts('dve', gq[:, 0:1], gq[:, 0:1], 128.0 ** -0.5, None, ALU.mult, None, [gq.res], [gq.res])
        blocks = []
        c = 0
        while c < 48:
            n = min(10, 48 - c)
            blocks.append(dict(loads=[(self.na_w_qkv[:, c * 128:(c + n) * 128], n * 128)],
                               groups=[[(i * 128, 128)] for i in range(n)], c0=c))
            c += n

        def epiA(bi, gi, ti, tile, pss):
            c0, w, s, hl, hr = tile
            cg = blocks[bi]['c0'] + gi
            ps = pss[0]
            if cg < 32:
                which, h = cg // 16, cg % 16
                sq, sqv = self.bfstage()
                p.act(sqv[:, 0:w], ps[:, 0:w], AF.Square, [ps.res], [sq.res])
                ps2 = self.next_psum()
                p.mm(ps2[:, 0:w], self.ones_b[:], sqv[:, 0:w], True, True, [sq.res, self.ones_b.res], [ps2.res], inc=True)
                rs = self.next_stage()
                p.act(rs[:, 0:w], ps2[:, 0:w], AF.Sqrt, [ps2.res, self.eps_t.res], [rs.res], bias=self.eps_t[:, 0:1],
                      scale=1.0 / 128)
                p.recip(rs[:, 0:w], rs[:, 0:w], [rs.res], [rs.res])
                ob, obv = self.bfstage()
                p.stt('dve', obv[:, 0:w], ps[:, 0:w], gq[:, which:which + 1], rs[:, 0:w], ALU.mult, ALU.mult,
                      [ps.res, gq.res, rs.res], [ob.res])
                dst, dr = (S1, S1r) if which == 0 else (S2, S2r)
                p.dma('sp', dst[h * 128:(h + 1) * 128, c0:c0 + w], obv[:, 0:w], [ob.res], [dr])
            else:
                cc = cg - 32
                ob, obv = self.bfstage()
                p.act(obv[:, 0:w], ps[:, 0:w], AF.Copy, [ps.res], [ob.res])
                p.dma('sp', S3[cc * 128:(cc + 1) * 128, c0:c0 + w], obv[:, 0:w], [ob.res], [S3r])

        self.linear(self.hT, self.hT_res, KC, blocks, self.plain(), epiA, defer=True)
        p.barrier()
        SEQ = TP * 2
        qT = [self.aview('naq%d' % i, i * SEQ, [128, TP], BF16) for i in range(2)]
        kT = [self.aview('nak%d' % i, (2 + i) * SEQ, [128, TP], BF16) for i in range(2)]
        vT = [self.aview('nav%d' % i, (4 + i) * SEQ, [128, TP], BF16) for i in range(2)]
        Ob = [self.aview('nao%d' % i, (6 + i) * SEQ, [128, TP], BF16) for i in range(2)]
        o2 = 8 * SEQ
        Vt = [self.aview('naVt%d' % i, o2 + i * 8704, [128, 34, 128], BF16) for i in range(2)]
        o3 = o2 + 2 * 8704
        tab = [self.aview('natab%d' % i, o3 + i * 8192, [128, 2, 16, 64], BF16) for i in range(2)]
        Sb = self.psum[0:4]
        Ob_ps = self.psum[4:6]
        Db_ps = self.psum[6:8]

        def tcol(tk):
            return C0 + 128 * tk if tk < 2 else L0 + 128 * (tk - 2)

        def load_head(h):
            p.dma('pool', qT[h % 2][:], S1[h * 128:(h + 1) * 128, :], [S1r], [qT[h % 2].res])
            p.dma('pool', kT[h % 2][:], S2[h * 128:(h + 1) * 128, :], [S2r], [kT[h % 2].res])
            p.dma('pool', vT[h % 2][:], S3[h * 128:(h + 1) * 128, :], [S3r], [vT[h % 2].res])
            p.dma('pool', tab[h % 2][:], self.na_tab[h].rearrange('a p u c -> p a u c'), [self.in_res], [tab[h % 2].res])

        load_head(0)
        for h in range(16):
            q, k, v, O, V, tb = qT[h % 2], kT[h % 2], vT[h % 2], Ob[h % 2], Vt[h % 2], tab[h % 2]
            if h + 1 < 16:
                load_head(h + 1)
            for g in range(9):
                tks = list(range(g * 4, min(34, g * 4 + 4)))
                ps = Sb[g % 4]
                psv = ps.ap.bitcast(BF16)
                for i, tk in enumerate(tks):
                    p.tr(psv[:, i * 128:(i + 1) * 128], v[:, tcol(tk):tcol(tk) + 128], self.ident_b[:],
                         [v.res, self.ident_b.res], [ps.res], inc=(i == len(tks) - 1))
                n = len(tks)
                p.copy('dve' if g % 2 else 'act_copy', V[:, g * 4:g * 4 + n, :],
                       psv[:, 0:n * 128].rearrange('p (a b) -> p a b', a=n), [ps.res], [V.res])
            units = []
            qts = ([] if last else [(C0, None)]) + [(L0 + 256 * j, j) for j in range(16)]
            for qi, (qc0, j) in enumerate(qts):
                keys = []
                if j is not None:
                    if j == 0:
                        kts, ti_ = range(0, 4), 1
                    elif j == 15:
                        kts, ti_ = range(28, 32), 1
                    else:
                        kts, ti_ = range(2 * j - 2, 2 * j + 4), 0
                    for kt in kts:
                        keys.append((2 + kt, (ti_, 7 - 2 * kt + 4 * j)))
                keys += [(0, None), (1, None)]
                for ki, (tk, bias) in enumerate(keys):
                    units.append(dict(qi=qi, qc0=qc0, tk=tk, bias=bias, first=(ki == 0), last=(ki == len(keys) - 1)))

            def emit_S(i, u):
                ps = Sb[i % 4]
                kc_ = tcol(u['tk'])
                hb_ = u['bias'] is not None
                p.mm(ps[:, 0:256], k[:, kc_:kc_ + 128], q[:, u['qc0']:u['qc0'] + 256], True, not hb_, [k.res, q.res], [ps.res],
                     inc=not hb_)
                if hb_:
                    ti_, u0 = u['bias']
                    p.mm(ps[:, 0:256], self.ident_b[:], tb.ap[:, ti_, u0:u0 + 4, :].rearrange('p a b -> p (a b)'), False, True,
                         [self.ident_b.res, tb.res], [ps.res], inc=True)
                pt, ptv = self.bfstage()
                p.act(ptv[:, 0:256], ps[:, 0:256], AF.Exp, [ps.res], [pt.res])
                u['pt'] = (pt, ptv)

            def emit_PV(u):
                pt, ptv = u['pt']
                ob_ = Ob_ps[u['qi'] % 2]
                db_ = Db_ps[u['qi'] % 2]
                p.mm(ob_[:, 0:256], V[:, u['tk'], :], ptv[:, 0:256], u['first'], u['last'], [V.res, pt.res], [ob_.res],
                     inc=u['last'])
                p.mm(db_[:, 0:256], self.ones_b[:], ptv[:, 0:256], u['first'], u['last'], [self.ones_b.res, pt.res],
                     [db_.res], inc=True)
                if u['last']:
                    rec = self.next_stage()
                    p.recip(rec[:, 0:256], db_[:, 0:256], [db_.res], [rec.res])
                    p.tt('dve', O[:, u['qc0']:u['qc0'] + 256], ob_[:, 0:256], rec[:, 0:256], ALU.mult,
                         [ob_.res, rec.res], [O.res])

            pend = []
            for i, u in enumerate(units):
                emit_S(i, u)
                pend.append(u)
                if len(pend) > 2:
                    emit_PV(pend.pop(0))
            while pend:
                emit_PV(pend.pop(0))
            if not last:
                p.dma('sp', S4[h * 128:(h + 1) * 128, C0:C1], O[:, C0:C1], [O.res], [S4r])
            p.dma('sp', S4[h * 128:(h + 1) * 128, L0:L1], O[:, L0:L1], [O.res], [S4r])
        p.barrier()
        self.out_proj(self.na_w_o, 'S4', last)


LRU_C = 8.0
NEG = -30000.0


def rope_tables():
    t = np.arange(NLAT)
    freqs = (10000.0 ** (-np.arange(0, 64, 2, dtype=np.float32) / np.float32(64))).astype(np.float32)
    d = np.arange(128)
    pos = np.where(d[:, None] < 64, (t // 64)[None, :], (t % 64)[None, :]).astype(np.float32)
    ang = (pos * freqs[d % 32][:, None]).astype(np.float32)
    cos = np.ones((128, TP), np.float32)
    sin = np.zeros((128, TP), np.float32)
    cos[:, L0:L1] = np.cos(ang)
    sgn = np.where((d % 64) < 32, -1.0, 1.0).astype(np.float32)
    sin[:, L0:L1] = np.sin(ang) * sgn[:, None]
    sc = np.float32(128.0 ** -0.5)
    return np.ascontiguousarray(np.stack([cos * sc, sin * sc, cos, sin]).astype(np.float32))


def na_table(rpb):
    H = rpb.shape[0]
    krl = np.arange(128) // 64
    kc = np.arange(128) % 64
    u = np.arange(16)
    qc = np.arange(64)
    dr = 14 + krl[:, None] - u[None, :]
    dc = np.clip(kc[:, None] - qc[None, :] + 15, 0, 30)
    cstart = np.clip(qc - 8, 0, 48)
    cmask = (kc[:, None] >= cstart[None, :]) & (kc[:, None] < cstart[None, :] + 16)
    drc = np.clip(dr, 0, 14)
    g = rpb[:, drc[:, :, None], dc[:, None, :]]
    out = np.full((H, 2, 128, 16, 64), NEG, np.float32)
    for t, (lo, hi) in enumerate(((3, 10), (0, 14))):
        valid = ((dr >= lo) & (dr <= hi))[:, :, None] & cmask[:, None, :]
        out[:, t] = np.where(valid[None], g, np.float32(NEG))
    return out


def prep_mixers(inp, b, layers=(0, 1, 2, 3)):
    m = {}
    if 0 in layers:
        w_in = inp['ml_w_in'][0]
        d = np.arange(128)
        perm = np.where((d % 64) < 32, d + 32, d - 32)
        cols = (np.arange(16)[:, None] * 128 + perm[None, :]).reshape(-1)
        m['ml_w_in'] = w_in
        m['ml_w_qkp'] = np.ascontiguousarray(w_in[:, cols])
        m['ml_w_gate'] = inp['ml_w_gate'][0]
        m['ml_bg'] = np.ascontiguousarray(inp['ml_b_gate'][0].reshape(32, 1))
        m['ml_hg'] = np.ascontiguousarray(inp['ml_head_g'][0].reshape(1, 2048))
        m['ml_w_out'] = inp['ml_w_out'][0]
        m['rope'] = rope_tables()
        m['tri'] = np.stack([np.triu(np.ones((128, 128), np.float32)), np.tril(np.ones((128, 128), np.float32))])
    if 1 in layers:
        m['na_w_qkv'] = inp['na_w_qkv'][0]
        m['na_g'] = np.ascontiguousarray(np.stack([inp['na_q_g'][0], inp['na_k_g'][0]], axis=1).astype(np.float32))
        m['na_tab'] = na_table(inp['na_rpb'][0])
        m['na_w_o'] = inp['na_w_o'][0]
    if 2 in layers:
        m['cv_w_pw1'] = inp['cv_w_pw1'][0]
        m['cv_dwT'] = np.ascontiguousarray(np.transpose(colT(inp['cv_dw'][0]), (0, 2, 1)))
        m['cv_vT'] = colT(np.stack([inp['cv_dw_b'][0], inp['cv_ln_g'][0], inp['cv_ln_b'][0]]))
        m['cv_w_pw2'] = inp['cv_w_pw2'][0]
    if 3 in layers:
        m['lr_w_in'] = inp['lr_w_in'][0]
        m['lr_cvT'] = colT(np.concatenate([inp['lr_conv'][0], inp['lr_conv_b']], axis=0))
        m['lr_w_gate'] = inp['lr_w_gate'][0]
        m['lr_bgT'] = colT(inp['lr_b_gate'][0])
        m['lr_lamT'] = colT(inp['lr_lambda'][0])
        m['lr_w_out'] = inp['lr_w_out'][0]
    return m
```
